# Optimizing a Trainium2 kernel written in Bass

```python
import math
import jax, jax.numpy as jnp
from jax import lax
import numpy as np

D_MODEL = 1024
BATCH = 8
SEQ = 8192
DEPTH = 2

ATTN_WIDTH = D_MODEL // 2
HGRN_WIDTH = D_MODEL - ATTN_WIDTH
HEAD_DIM = 64
N_ATTN_HEADS = ATTN_WIDTH // HEAD_DIM
HGRN_EXPAND = 128
N_HGRN_HEADS = HGRN_WIDTH // HGRN_EXPAND
HGRN_HEAD_DIM = HGRN_WIDTH // N_HGRN_HEADS
DILATED_CONFIGS = ((128, 1), (512, 4), (2048, 16))
ATTN_BLOCK = 128
ROPE_THETA = 500000.0
ROPE_DIM = HEAD_DIM // 4
HGRN_CHUNK = 64
D_FF = 2816
FFN_RES_WEIGHT = 0.5
DEEPNORM_ALPHA = (2 * DEPTH) ** 0.25
DEEPNORM_BETA = (8 * DEPTH) ** -0.25
LN_EPS = 1e-5
RMS_EPS = 1e-6
N_SUBLAYERS = 3
IN_SPLITS = [int(s) for s in np.cumsum([ATTN_WIDTH] * 3 + [HGRN_WIDTH] * 3)]
IN_COLS = 3 * ATTN_WIDTH + 4 * HGRN_WIDTH

kernel_name = 'hybrid_dilated_attn_hgrn2_macaron'


def layer_norm(x, g, b):
    xf = x.astype(jnp.float32)
    mu = jnp.mean(xf, axis=-1, keepdims=True)
    var = jnp.mean(jnp.square(xf - mu), axis=-1, keepdims=True)
    return ((xf - mu) * lax.rsqrt(var + LN_EPS) * g + b).astype(x.dtype)


def modulate(x, shift, scale):
    return x * (1 + scale[:, None, :]) + shift[:, None, :]


def swiglu_ffn(h, w_in, w_out):
    g, u = jnp.split(h @ w_in, 2, axis=-1)
    return (jax.nn.silu(g) * u) @ w_out


def partial_rope(t, pos):
    half = ROPE_DIM // 2
    inv_freq = ROPE_THETA ** (-jnp.arange(half, dtype=jnp.float32) * 2.0 / ROPE_DIM)
    ang = pos.astype(jnp.float32)[:, None, :, None] * inv_freq
    cos = jnp.cos(ang).astype(t.dtype)
    sin = jnp.sin(ang).astype(t.dtype)
    t1, t2, rest = t[..., :half], t[..., half:ROPE_DIM], t[..., ROPE_DIM:]
    return jnp.concatenate([t1 * cos - t2 * sin, t2 * cos + t1 * sin, rest], axis=-1)


def dilated_branch(q, k, v, window, dilation):
    B, H, S, hd = q.shape
    L = S // dilation
    wm = window // dilation
    nb = -(-L // ATTN_BLOCK)
    lp = nb * ATTN_BLOCK

    def res_view(t):
        return t.reshape(B, H, L, dilation, hd).transpose(0, 1, 3, 2, 4)

    qr = jnp.pad(res_view(q), ((0, 0), (0, 0), (0, 0), (0, lp - L), (0, 0)))
    kr = jnp.pad(res_view(k), ((0, 0), (0, 0), (0, 0), (ATTN_BLOCK, lp - L), (0, 0)))
    vr = jnp.pad(res_view(v), ((0, 0), (0, 0), (0, 0), (ATTN_BLOCK, lp - L), (0, 0)))

    def banded(t):
        prev = t[..., :lp, :].reshape(B, H, dilation, nb, ATTN_BLOCK, hd)
        cur = t[..., ATTN_BLOCK:, :].reshape(B, H, dilation, nb, ATTN_BLOCK, hd)
        return jnp.concatenate([prev, cur], axis=-2)

    kb, vb = banded(kr), banded(vr)
    qb = qr.reshape(B, H, dilation, nb, ATTN_BLOCK, hd)
    s = jnp.einsum('bhrnqe,bhrnke->bhrnqk', qb, kb, preferred_element_type=jnp.float32)
    qi = jnp.arange(ATTN_BLOCK)[:, None] + ATTN_BLOCK
    ki = jnp.arange(2 * ATTN_BLOCK)[None, :]
    dist = qi - ki
    key_m = jnp.arange(nb)[:, None, None] * ATTN_BLOCK - ATTN_BLOCK + ki[None]
    mask = (dist >= 0) & (dist <= wm) & (key_m >= 0)
    s = jnp.where(mask, s, -jnp.inf)
    mx = jnp.max(s, axis=-1, keepdims=True)
    p = jnp.exp(s - mx)
    den = jnp.sum(p, axis=-1, keepdims=True)
    o = jnp.einsum('bhrnqk,bhrnke->bhrnqe', p.astype(v.dtype), vb,
                   preferred_element_type=jnp.float32) / den
    lse = (mx + jnp.log(den))[..., 0]
    o = o.reshape(B, H, dilation, lp, hd)[:, :, :, :L].transpose(0, 1, 3, 2, 4).reshape(B, H, S, hd)
    lse = lse.reshape(B, H, dilation, lp)[:, :, :, :L].transpose(0, 1, 3, 2).reshape(B, H, S)
    return o, lse


def dilated_attention(q, k, v):
    outs, lses = zip(*[dilated_branch(q, k, v, w, d) for (w, d) in DILATED_CONFIGS])
    wts = jax.nn.softmax(jnp.stack(lses), axis=0)
    return jnp.sum(jnp.stack(outs) * wts[..., None], axis=0)


def hgrn2_scan(q, k, v, log_f):
    B, S, H, dk = q.shape
    dv = v.shape[-1]
    nc = S // HGRN_CHUNK

    def chunks(t):
        return t.reshape(B, nc, HGRN_CHUNK, H, t.shape[-1]).transpose(1, 0, 3, 2, 4)

    causal = jnp.tril(jnp.ones((HGRN_CHUNK, HGRN_CHUNK), dtype=bool))[:, :, None]

    def step(state, inp):
        qc, kc, vc, ac = inp
        b = jnp.cumsum(ac, axis=2)
        decay = jnp.exp(jnp.where(causal, b[:, :, :, None, :] - b[:, :, None, :, :], -jnp.inf))
        attn = jnp.einsum('bhtd,bhsd,bhtsd->bhts', qc, kc, decay)
        o = (jnp.einsum('bhts,bhse->bhte', attn, vc)
             + jnp.einsum('bhtd,bhde->bhte', qc * jnp.exp(b), state))
        b_last = b[:, :, -1:, :]
        state = (jnp.exp(b_last[:, :, 0, :, None]) * state
                 + jnp.einsum('bhsd,bhse->bhde', kc * jnp.exp(b_last - b), vc))
        return state, o

    state0 = jnp.zeros((B, H, dk, dv), jnp.float32)
    _, o = lax.scan(step, state0, (chunks(q), chunks(k), chunks(v), chunks(log_f)))
    return o.transpose(1, 0, 3, 2, 4).reshape(B, S, H, dv)


def hgrn2_mixer(hq, hf, hi, hg, lb, norm_w):
    B, S, _ = hq.shape
    shp = (B, S, N_HGRN_HEADS, HGRN_EXPAND)
    q = jax.nn.silu(hq.astype(jnp.float32)).reshape(shp)
    fp = hf.astype(jnp.float32).reshape(shp)
    log_f = jnp.logaddexp(jnp.log(lb), jnp.log1p(-lb) + jax.nn.log_sigmoid(fp))
    k = (1 - lb) * jax.nn.sigmoid(-fp)
    v = hi.astype(jnp.float32).reshape(B, S, N_HGRN_HEADS, HGRN_HEAD_DIM)
    o = hgrn2_scan(q, k, v, log_f)
    o = o * lax.rsqrt(jnp.mean(jnp.square(o), axis=-1, keepdims=True) + RMS_EPS)
    o = o * norm_w.reshape(N_HGRN_HEADS, HGRN_HEAD_DIM)
    o = o.reshape(B, S, HGRN_WIDTH) * jax.nn.silu(hg.astype(jnp.float32))
    return o.astype(hq.dtype)


def hybrid_mixer(h, pos, w_in, w_out, norm_w, lb):
    B, S, _ = h.shape
    aq, ak, av, hq, hf, hi, hg = jnp.split(h @ w_in, IN_SPLITS, axis=-1)

    def heads(t):
        return t.reshape(B, S, N_ATTN_HEADS, HEAD_DIM).transpose(0, 2, 1, 3)

    q = partial_rope(heads(aq), pos) * (HEAD_DIM ** -0.5)
    k = partial_rope(heads(ak), pos)
    v = heads(av)
    ao = dilated_attention(q, k, v).astype(h.dtype).transpose(0, 2, 1, 3).reshape(B, S, ATTN_WIDTH)
    go = hgrn2_mixer(hq, hf, hi, hg, lb, norm_w)
    return jnp.concatenate([ao, go], axis=-1) @ w_out


def post_norm_update(x, y, gate, g, b, res_weight):
    return layer_norm(DEEPNORM_ALPHA * x + res_weight * (1 + gate[:, None, :]) * y, g, b)


def setup_inputs(seed: int = 0) -> dict:
    key = jax.random.key(seed)
    ks = jax.random.split(key, 16)
    nrm = jax.random.normal
    x = nrm(ks[0], (BATCH, SEQ, D_MODEL), jnp.float32)
    c = nrm(ks[1], (BATCH, D_MODEL), jnp.float32)
    positions = jnp.broadcast_to(jnp.arange(SEQ, dtype=jnp.int32)[None, :], (BATCH, SEQ))
    ln_g = 1.0 + 0.02 * nrm(ks[2], (DEPTH, N_SUBLAYERS, D_MODEL), jnp.float32)
    ln_b = 0.02 * nrm(ks[3], (DEPTH, N_SUBLAYERS, D_MODEL), jnp.float32)
    ada_w = nrm(ks[4], (DEPTH, D_MODEL, N_SUBLAYERS * 3 * D_MODEL), jnp.float32) * (0.1 * D_MODEL ** -0.5)
    ada_b = 0.02 * nrm(ks[5], (DEPTH, N_SUBLAYERS * 3 * D_MODEL), jnp.float32)
    ffn1_w_in = nrm(ks[6], (DEPTH, D_MODEL, 2 * D_FF), jnp.float32) * D_MODEL ** -0.5
    ffn1_w_out = nrm(ks[7], (DEPTH, D_FF, D_MODEL), jnp.float32) * (D_FF ** -0.5 * DEEPNORM_BETA)
    ffn2_w_in = nrm(ks[8], (DEPTH, D_MODEL, 2 * D_FF), jnp.float32) * D_MODEL ** -0.5
    ffn2_w_out = nrm(ks[9], (DEPTH, D_FF, D_MODEL), jnp.float32) * (D_FF ** -0.5 * DEEPNORM_BETA)
    mix_w_in = nrm(ks[10], (DEPTH, D_MODEL, IN_COLS), jnp.float32) * D_MODEL ** -0.5
    mix_w_out = nrm(ks[11], (DEPTH, D_MODEL, D_MODEL), jnp.float32) * (D_MODEL ** -0.5 * DEEPNORM_BETA)
    hgrn_norm_w = 1.0 + 0.02 * nrm(ks[12], (DEPTH, HGRN_WIDTH), jnp.float32)
    hgrn_lb_logits = 0.5 * nrm(ks[13], (DEPTH, HGRN_WIDTH), jnp.float32)
    return {'x': x, 'c': c, 'positions': positions, 'ln_g': ln_g, 'ln_b': ln_b,
            'ada_w': ada_w, 'ada_b': ada_b,
            'ffn1_w_in': ffn1_w_in, 'ffn1_w_out': ffn1_w_out,
            'ffn2_w_in': ffn2_w_in, 'ffn2_w_out': ffn2_w_out,
            'mix_w_in': mix_w_in, 'mix_w_out': mix_w_out,
            'hgrn_norm_w': hgrn_norm_w, 'hgrn_lb_logits': hgrn_lb_logits}


def reference(x, c, positions, ln_g, ln_b, ada_w, ada_b, ffn1_w_in, ffn1_w_out,
              ffn2_w_in, ffn2_w_out, mix_w_in, mix_w_out, hgrn_norm_w, hgrn_lb_logits):
    B = x.shape[0]
    lb_all = jnp.cumsum(jax.nn.softmax(hgrn_lb_logits.astype(jnp.float32), axis=0), axis=0)
    lb_all = lb_all - lb_all[0:1]
    cond = jax.nn.silu(c)
    for l in range(DEPTH):
        ada = (cond @ ada_w[l] + ada_b[l]).reshape(B, N_SUBLAYERS, 3, D_MODEL)
        h = modulate(x, ada[:, 0, 0], ada[:, 0, 1])
        x = post_norm_update(x, swiglu_ffn(h, ffn1_w_in[l], ffn1_w_out[l]), ada[:, 0, 2],
                             ln_g[l, 0], ln_b[l, 0], FFN_RES_WEIGHT)
        h = modulate(x, ada[:, 1, 0], ada[:, 1, 1])
        lb = lb_all[l].reshape(N_HGRN_HEADS, HGRN_EXPAND)
        y = hybrid_mixer(h, positions, mix_w_in[l], mix_w_out[l], hgrn_norm_w[l], lb)
        x = post_norm_update(x, y, ada[:, 1, 2], ln_g[l, 1], ln_b[l, 1], 1.0)
        h = modulate(x, ada[:, 2, 0], ada[:, 2, 1])
        x = post_norm_update(x, swiglu_ffn(h, ffn2_w_in[l], ffn2_w_out[l]), ada[:, 2, 2],
                             ln_g[l, 2], ln_b[l, 2], FFN_RES_WEIGHT)
    return x
```

```python
import math
from contextlib import ExitStack
import numpy as np
import concourse.bass as bass
import concourse.mybir as mybir
from concourse.bass_utils import run_bass_kernel_spmd

F32 = mybir.dt.float32
BF16 = mybir.dt.bfloat16
I32 = mybir.dt.int32
AF = mybir.ActivationFunctionType
ALU = mybir.AluOpType

D = 1024
NCH = 8
DEPTH = 2
DFF = 2816
NHC = DFF // 128
IN_COLS = 3584
ALPHA = (2 * DEPTH) ** 0.25
LN_EPS = 1e-5
RMS_EPS = 1e-6
ROPE_THETA = 500000.0

V_ADAB = 0
V_LNG = 144
V_LNB = 192
V_C = 240
V_NW = 248
V_LB = 256
NVEC = 264


class Tracker:
    def __init__(self, nc, es):
        self.nc = nc
        self.es = es
        self.sems = {}
        self.cnt = {}
        self.seen = {}
        self.engs = {"pe": nc.tensor, "act": nc.scalar, "dve": nc.vector, "pool": nc.gpsimd, "sp": nc.sync}
        for n in self.engs:
            self.sems[n] = es.enter_context(nc.semaphore("s_" + n))
            self.cnt[n] = 0
            self.seen[n] = {}
        self.dsems = {}

    def dsem(self, name):
        if name not in self.dsems:
            s = self.es.enter_context(self.nc.semaphore("d_" + name))
            self.dsems[name] = [s, 0]
        return self.dsems[name]

    def wait(self, en, tok):
        if tok is None:
            return
        if isinstance(tok, list):
            for t in tok:
                self.wait(en, t)
            return
        key, sem, val = tok
        if self.seen[en].get(key, 0) >= val:
            return
        self.engs[en].wait_ge(sem, val)
        self.seen[en][key] = val

    def op(self, en, deps, inst_fn):
        self.wait(en, deps)
        inst = inst_fn(self.engs[en])
        self.cnt[en] += 1
        inst.then_inc(self.sems[en], 1)
        return ("e_" + en, self.sems[en], self.cnt[en])

    def last(self, en):
        if self.cnt[en] == 0:
            return None
        return ("e_" + en, self.sems[en], self.cnt[en])

    def dma(self, en, deps, dname, out, in_):
        self.wait(en, deps)
        d = self.dsem(dname)
        self.engs[en].dma_start(out=out, in_=in_).then_inc(d[0], 16)
        d[1] += 16
        return ("d_" + dname, d[0], d[1])

    def all_tokens(self):
        toks = [self.last(n) for n in self.engs if self.cnt[n]]
        toks += [("d_" + k, v[0], v[1]) for k, v in self.dsems.items() if v[1]]
        return toks

    def barrier(self):
        toks = self.all_tokens()
        for en in self.engs:
            self.wait(en, toks)


class Buf:
    def __init__(self, t):
        self.t = t
        self.w = {}
        self.r = {}


def _deps(reads, writes, deps):
    d = []
    for b in reads:
        d += list(b.w.values())
    for b in writes:
        d += list(b.w.values()) + list(b.r.values())
    if deps:
        d += deps if isinstance(deps, list) else [deps]
    return d


def _mark(tok, reads, writes):
    for b in reads:
        if b not in writes:
            b.r[tok[0]] = tok
    for b in writes:
        b.w[tok[0]] = tok
        b.r = {}


def opb(tr, en, reads, writes, fn, deps=None):
    tok = tr.op(en, _deps(reads, writes, deps), fn)
    _mark(tok, reads, writes)
    return tok


def dmab(tr, en, dname, reads, writes, out, in_, deps=None):
    tok = tr.dma(en, _deps(reads, writes, deps), dname, out, in_)
    _mark(tok, reads, writes)
    return tok


class Builder:
    def __init__(self, S, layers=DEPTH, debug_outs=(), stop_after=None):
        self.S = S
        self.layers = layers
        self.debug_outs = debug_outs
        self.stop_after = stop_after

    def sb(self, es, name, shape, dt):
        self.uid = getattr(self, "uid", 0) + 1
        return es.enter_context(self.nc.sbuf_tensor("sb%d_%s" % (self.uid, name), shape, dt))

    def ps(self, es, name, shape, dt=F32):
        self.uid = getattr(self, "uid", 0) + 1
        return es.enter_context(self.nc.psum_tensor("ps%d_%s" % (self.uid, name), shape, dt))

    def build(self):
        S = self.S
        nc = bass.Bass("TRN2", target_bir_lowering=False)
        self.nc = nc
        dt_ = nc.dram_tensor
        self.x_in = dt_("x", [S, D], F32, kind="ExternalInput").ap()
        self.pos_in = dt_("pos", [1, S], I32, kind="ExternalInput").ap()
        self.vecs_in = dt_("vecs", [NVEC, 128], F32, kind="ExternalInput").ap()
        self.consts_in = dt_("consts", [128, 1024], F32, kind="ExternalInput").ap()
        self.ada_w = dt_("ada_w", [DEPTH, D, 9 * D], F32, kind="ExternalInput").ap()
        self.w_ffn_in = [dt_("ffn1_w_in", [DEPTH, D, 2 * DFF], F32, kind="ExternalInput").ap(),
                         dt_("ffn2_w_in", [DEPTH, D, 2 * DFF], F32, kind="ExternalInput").ap()]
        self.w_ffn_out = [dt_("ffn1_w_out", [DEPTH, DFF, D], F32, kind="ExternalInput").ap(),
                          dt_("ffn2_w_out", [DEPTH, DFF, D], F32, kind="ExternalInput").ap()]
        self.w_mix_in = dt_("mix_w_in", [DEPTH, D, IN_COLS], F32, kind="ExternalInput").ap()
        self.w_mix_out = dt_("mix_w_out", [DEPTH, D, D], F32, kind="ExternalInput").ap()
        self.out = dt_("out", [S, D], F32, kind="ExternalOutput").ap()

        def scratch(name, shape, dt):
            kind = "ExternalOutput" if name in self.debug_outs else "Internal"
            return dt_(name, shape, dt, kind=kind).ap()
        self.xT = [scratch("xT0", [D, S], F32), scratch("xT1", [D, S], F32)]
        self.mixT = scratch("mixT", [D, S], BF16)
        self.hT = scratch("hT", [D, S], BF16)
        self.cosT = scratch("cosT", [128, S], F32)
        self.sinT = scratch("sinT", [128, S], F32)

        with ExitStack() as es:
            self.tr = Tracker(nc, es)
            self.setup(es)
            self.run_phases()
            self.tr.barrier()
        return nc

    def setup(self, es):
        nc, tr = self.nc, self.tr
        self.ident_f = self.sb(es, "ident_f", [128, 128], F32)
        self.ident_b = self.sb(es, "ident_b", [128, 128], BF16)
        self.ones_b = self.sb(es, "ones_b", [128, 128], BF16)
        self.cst = self.sb(es, "cst", [128, 1024], F32)
        self.vecs = self.sb(es, "vecs", [128, NVEC], F32)
        self.ada = self.sb(es, "ada", [128, DEPTH * 72], F32)
        self.msc = self.sb(es, "msc", [128, DEPTH * 72], F32)
        t0 = tr.dma("sp", None, "setup_c", self.cst[:], self.consts_in)
        tok = tr.op("dve", t0, lambda e: e.tensor_copy(out=self.ident_f[:], in_=self.cst[:, 0:128]))
        tokb = tr.op("dve", tok, lambda e: e.tensor_copy(out=self.ident_b[:], in_=self.cst[:, 0:128]))
        tok1 = tr.op("pool", None, lambda e: e.memset(self.ones_b[:], 1.0))
        self.lbv = self.sb(es, "lbv", [128, 8], F32)
        self.oml = self.sb(es, "oml", [128, 8], F32)
        self.epst = self.sb(es, "epst", [128, 4], F32)
        tok2 = tr.op("pool", None, lambda e: e.memset(self.epst[:, 0:1], LN_EPS / (ALPHA * ALPHA)))
        tok3 = tr.op("pool", None, lambda e: e.memset(self.epst[:, 1:2], RMS_EPS))
        self.const_tok = [tok, tokb, tok1, tok2, tok3, t0]
        with ExitStack() as es2:
            vraw = self.sb(es2, "vraw", [128, 3, 128], F32)
            pst = self.ps(es2, "pst", [128, 512], F32)
            psa = self.ps(es2, "psa", [128, 512], F32)
            cond = self.sb(es2, "cond", [128, 8], F32)
            wbuf = [self.sb(es2, "adaw%d" % i, [128, 8, 1152], F32) for i in range(2)]
            nrow = [128, 128, NVEC - 256]
            toks = []
            for i in range(3):
                toks.append(tr.dma("sp", None, "setup_v%d" % i, vraw[0:nrow[i], i, :],
                                   self.vecs_in[i * 128:i * 128 + nrow[i], :]))
            tp = None
            for i in range(3):
                tp = tr.op("pe", [toks[i], tok], lambda e: e.transpose(
                    pst[:, i * 128:i * 128 + nrow[i]], vraw[0:nrow[i], i, :], self.ident_f[0:nrow[i], 0:nrow[i]]))
            tv = tr.op("dve", tp, lambda e: e.tensor_copy(out=self.vecs[:], in_=pst[:, 0:NVEC]))
            tc = tr.op("act", tv, lambda e: e.activation(out=cond[:], in_=self.vecs[:, V_C:V_C + 8], func=AF.Silu))
            free = [None, None]
            blk = 0
            for l in range(self.layers):
                for cb in range(8):
                    buf = wbuf[blk % 2]
                    tl = tr.dma("sp", free[blk % 2], "adaw%d" % (blk % 2), buf[:],
                                self.ada_w[l, :, cb * 1152:(cb + 1) * 1152].rearrange("(c p) n -> p c n", p=128))
                    tm = None
                    for j in range(9):
                        col = l * 72 + cb * 9 + j
                        for kc in range(8):
                            tm = tr.op("pe", [tl, tc], lambda e: e.matmul(
                                psa[:, col:col + 1], lhsT=buf[:, kc, j * 128:(j + 1) * 128], rhs=cond[:, kc:kc + 1],
                                start=(kc == 0), stop=(kc == 7)))
                    free[blk % 2] = tm
                    blk += 1
            n = self.layers * 72
            ta = tr.op("dve", [tm, tv], lambda e: e.tensor_tensor(
                out=self.ada[:, 0:n], in0=psa[:, 0:n], in1=self.vecs[:, V_ADAB:V_ADAB + n], op=ALU.add))
            tlast = ta
            for l in range(self.layers):
                for sub in range(3):
                    base = l * 72 + sub * 24
                    rw = 1.0 if sub == 1 else 0.5
                    t1 = tr.op("dve", ta, lambda e: e.tensor_copy(out=self.msc[:, base:base + 8], in_=self.ada[:, base:base + 8]))
                    t2 = tr.op("dve", ta, lambda e: e.tensor_scalar(
                        out=self.msc[:, base + 8:base + 16], in0=self.ada[:, base + 8:base + 16],
                        scalar1=1.0, scalar2=None, op0=ALU.add))
                    tlast = tr.op("dve", ta, lambda e: e.tensor_scalar(
                        out=self.msc[:, base + 16:base + 24], in0=self.ada[:, base + 16:base + 24],
                        scalar1=1.0, scalar2=rw / ALPHA, op0=ALU.add, op1=ALU.mult))
            self.setup_tok = [tlast, t1, t2, tv]
            self.dump("ada", self.ada[:], [128, DEPTH * 72])
            self.dump("msc", self.msc[:], [128, DEPTH * 72])
            self.dump("vecs", self.vecs[:], [128, NVEC])
            tr.barrier()

    def dump(self, name, ap, shape, dt=F32):
        if not self.debug_outs:
            return
        d = self.nc.dram_tensor("dbg_" + name, list(shape), dt, kind="ExternalOutput").ap()
        self.tr.dma("sp", self.tr.all_tokens(), "dbg", d, ap)

    def epsc(self, eps):
        return self.epst[:, 0:1] if eps > 2e-6 else self.epst[:, 1:2]

    def run_phases(self):
        self.rope_setup()
        self.transpose_in()
        cur = 0
        for l in range(self.layers):
            self.ffn(l, 0, self.xT[cur], self.xT[1 - cur], hdst=self.hT)
            cur = 1 - cur
            if self.stop_after == ("ffn", l, 0):
                break
            self.attn(l)
            if self.stop_after == ("attn", l):
                break
            self.hgrn(l)
            if self.stop_after == ("hgrn", l):
                break
            self.ffn(l, 1, self.xT[cur], self.xT[1 - cur])
            cur = 1 - cur
            if self.stop_after == ("mix", l):
                break
            self.ffn(l, 2, self.xT[cur], self.xT[1 - cur])
            cur = 1 - cur
        self.transpose_out(self.xT[cur])

    def transpose_in(self):
        nc, tr, S = self.nc, self.tr, self.S
        with ExitStack() as es:
            xin = [self.sb(es, "ti_x%d" % i, [128, 4, D], F32) for i in range(2)]
            xo = [self.sb(es, "ti_o%d" % i, [128, NCH, 512], F32) for i in range(2)]
            pss = [self.ps(es, "ti_p%d" % i, [128, 512], F32) for i in range(4)]
            ps_free = [None] * 4
            xin_free = [None] * 2
            xo_free = [None] * 2
            nb = 0
            for it in range(S // 512):
                b = it % 2
                tl = tr.dma("sp", xin_free[b], "ti_l%d" % b, xin[b][:],
                            self.x_in[it * 512:(it + 1) * 512, :].rearrange("(j p) d -> p j d", p=128))
                evs = []
                for c in range(NCH):
                    pb = nb % 4
                    nb += 1
                    tp = None
                    for j in range(4):
                        tp = tr.op("pe", [tl, ps_free[pb], self.const_tok], lambda e: e.transpose(
                            pss[pb][:, j * 128:(j + 1) * 128], xin[b][:, j, c * 128:(c + 1) * 128], self.ident_f[:]))
                    eng = "act" if c % 2 == 0 else "dve"
                    if eng == "act":
                        te = tr.op("act", [tp, xo_free[b]], lambda e: e.copy(out=xo[b][:, c, :], in_=pss[pb][:]))
                    else:
                        te = tr.op("dve", [tp, xo_free[b]], lambda e: e.tensor_copy(out=xo[b][:, c, :], in_=pss[pb][:]))
                    ps_free[pb] = te
                    evs.append(te)
                xin_free[b] = tp
                ts = tr.dma("pool", evs, "ti_s%d" % b,
                            self.xT[0][:, it * 512:(it + 1) * 512].rearrange("(c p) t -> p c t", p=128), xo[b][:])
                xo_free[b] = ts
            tr.barrier()

    def transpose_out(self, src):
        nc, tr, S = self.nc, self.tr, self.S
        with ExitStack() as es:
            xin = [self.sb(es, "to_x%d" % i, [128, NCH, 512], F32) for i in range(2)]
            xo = [self.sb(es, "to_o%d" % i, [128, 4, D], F32) for i in range(2)]
            pss = [self.ps(es, "to_p%d" % i, [128, 512], F32) for i in range(4)]
            ps_free = [None] * 4
            xin_free = [None] * 2
            xo_free = [None] * 2
            nb = 0
            for it in range(S // 512):
                b = it % 2
                tl = tr.dma("sp", xin_free[b], "to_l%d" % b, xin[b][:],
                            src[:, it * 512:(it + 1) * 512].rearrange("(c p) t -> p c t", p=128))
                evs = []
                for j in range(4):
                    for h in range(2):
                        pb = nb % 4
                        nb += 1
                        tp = None
                        for cc in range(4):
                            c = h * 4 + cc
                            tp = tr.op("pe", [tl, ps_free[pb], self.const_tok], lambda e: e.transpose(
                                pss[pb][:, cc * 128:(cc + 1) * 128], xin[b][:, c, j * 128:(j + 1) * 128], self.ident_f[:]))
                        if (j + h) % 2 == 0:
                            te = tr.op("act", [tp, xo_free[b]], lambda e: e.copy(
                                out=xo[b][:, j, h * 512:(h + 1) * 512], in_=pss[pb][:]))
                        else:
                            te = tr.op("dve", [tp, xo_free[b]], lambda e: e.tensor_copy(
                                out=xo[b][:, j, h * 512:(h + 1) * 512], in_=pss[pb][:]))
                        ps_free[pb] = te
                        evs.append(te)
                xin_free[b] = tp
                ts = tr.dma("pool", evs, "to_s%d" % b,
                            self.out[it * 512:(it + 1) * 512, :].rearrange("(j p) d -> p j d", p=128), xo[b][:])
                xo_free[b] = ts
            tr.barrier()

    def rope_setup(self):
        nc, tr, S = self.nc, self.tr, self.S
        CW = min(2048, S)
        PI = math.pi
        with ExitStack() as es:
            posi = Buf(self.sb(es, "r_pi", [128, CW], I32))
            ang = Buf(self.sb(es, "r_ang", [128, CW], F32))
            m1 = Buf(self.sb(es, "r_m1", [128, CW], F32))
            m2 = Buf(self.sb(es, "r_m2", [128, CW], F32))
            cK = Buf(None)
            cK.w = {t[0]: t for t in self.const_tok}
            invf = self.cst[:, 128:129]
            halfpi = self.cst[:, 129:130]
            for i in range(S // CW):
                a, b = i * CW, (i + 1) * CW
                dmab(tr, "sp", "r_l", [], [posi], posi.t[:], self.pos_in[0:1, a:b].partition_broadcast(128))
                opb(tr, "dve", [posi], [ang], lambda e: e.tensor_copy(out=ang.t[:], in_=posi.t[:]))
                opb(tr, "dve", [cK], [ang], lambda e: e.tensor_scalar(
                    out=ang.t[:], in0=ang.t[:], scalar1=invf, scalar2=None, op0=ALU.mult))
                C1 = 6.28125
                C2 = 2 * PI - C1
                opb(tr, "dve", [ang], [m1], lambda e: e.tensor_scalar(
                    out=m1.t[:], in0=ang.t[:], scalar1=1.0 / (2 * PI), scalar2=None, op0=ALU.mult))
                opb(tr, "dve", [m1], [posi], lambda e: e.tensor_copy(out=posi.t[:], in_=m1.t[:]))
                opb(tr, "dve", [posi], [m1], lambda e: e.tensor_copy(out=m1.t[:], in_=posi.t[:]))
                opb(tr, "dve", [m1], [ang], lambda e: e.scalar_tensor_tensor(
                    out=ang.t[:], in0=m1.t[:], scalar=-C1, in1=ang.t[:], op0=ALU.mult, op1=ALU.add))
                opb(tr, "dve", [m1], [ang], lambda e: e.scalar_tensor_tensor(
                    out=ang.t[:], in0=m1.t[:], scalar=-C2, in1=ang.t[:], op0=ALU.mult, op1=ALU.add))
                opb(tr, "dve", [], [ang], lambda e: e.tensor_scalar(
                    out=ang.t[:], in0=ang.t[:], scalar1=-PI, scalar2=PI, op0=ALU.max, op1=ALU.min))
                opb(tr, "dve", [ang], [m2], lambda e: e.scalar_tensor_tensor(
                    out=m2.t[:], in0=ang.t[:], scalar=-1.0, in1=ang.t[:], op0=ALU.mult, op1=ALU.max))
                opb(tr, "act", [ang], [m1], lambda e: e.activation(out=m1.t[:], in_=ang.t[:], func=AF.Sin))
                opb(tr, "act", [cK], [m2], lambda e: e.activation(out=m2.t[:], in_=m2.t[:], func=AF.Sin, bias=halfpi, scale=-1.0))
                dmab(tr, "sp", "r_s1", [m1], [], self.sinT[:, a:b], m1.t[:])
                dmab(tr, "sp", "r_s2", [m2], [], self.cosT[:, a:b], m2.t[:])
            t1 = tr.op("pool", self.setup_tok, lambda e: e.memset(self.lbv[:], 0.0))
            t2 = tr.op("dve", [t1] + self.setup_tok, lambda e: e.tensor_tensor(
                out=self.lbv[:, 4:8], in0=self.vecs[:, V_LB + 4:V_LB + 8], in1=self.vecs[:, V_LB:V_LB + 4], op=ALU.subtract))
            t3 = tr.op("act", t2, lambda e: e.activation(out=self.lbv[:, 4:8], in_=self.lbv[:, 4:8], func=AF.Sigmoid))
            t4 = tr.op("dve", t3, lambda e: e.tensor_scalar(
                out=self.oml[:], in0=self.lbv[:], scalar1=-1.0, scalar2=1.0, op0=ALU.mult, op1=ALU.add))
            self.lb_tok = [t3, t4]
            tr.barrier()

    def attn(self, l):
        nc, tr, S = self.nc, self.tr, self.S
        NSEG = S // 2048
        with ExitStack() as es:
            B = lambda name, shape, dt: Buf(self.sb(es, name, shape, dt))
            P = lambda name, shape, dt=F32: Buf(self.ps(es, name, shape, dt))
            cK = Buf(None)
            cK.w = {t[0]: t for t in self.const_tok}
            wq = B("a_wq", [128, NCH, 640], BF16)
            ht = [B("a_ht%d" % i, [128, NCH, 512], BF16) for i in range(2)]
            cs = [B("a_cs%d" % i, [128, 512], F32) for i in range(2)]
            sn = [B("a_sn%d" % i, [128, 512], F32) for i in range(2)]
            t1b = [B("a_t1%d" % i, [128, 512], F32) for i in range(2)]
            t2b = [B("a_t2%d" % i, [128, 512], F32) for i in range(2)]
            qs = [[B("a_q%d%d" % (p, o), [128, 2048], BF16) for o in range(3)] for p in range(2)]
            ks = [[B("a_k%d%d" % (p, o), [128, 2048], BF16) for o in range(3)] for p in range(2)]
            vt = [B("a_vt%d" % o, [128, 2048], BF16) for o in range(3)]
            V = [[B("a_V%d%d" % (p, o), [128, 16, 128], BF16) for o in range(3)] for p in range(2)]
            pT = [B("a_pT%d" % i, [128, 512], BF16) for i in range(4)]
            accN = B("a_accN", [128, 2048], F32)
            accD = B("a_accD", [128, 2048], F32)
            ao = [B("a_ao%d" % i, [128, 2048], BF16) for i in range(2)]
            masks = [B("a_mk%d" % i, [128, 512], BF16) for i in range(3)]
            psp = [P("a_pp%d" % i, [128, 512]) for i in range(2)]
            pss = [P("a_ps%d" % i, [128, 512]) for i in range(3)]
            psn = [P("a_pn%d" % i, [128, 512]) for i in range(2)]
            pst = P("a_pt", [128, 1024], BF16)
            mP, mC, mN = self.cst[:, 256:384], self.cst[:, 384:512], self.cst[:, 512:640]
            for mi, pat in enumerate([(mP, mC, mP, mC), (mN, mC, mP, mC), (mN, mC, mN, mC)]):
                for q_, src_ in enumerate(pat):
                    opb(tr, "dve", [cK], [masks[mi]], lambda e: e.tensor_copy(
                        out=masks[mi].t[:, q_ * 128:(q_ + 1) * 128], in_=src_))
            cnt = {"pp": 0, "ps": 0, "pn": 0, "pT": 0, "tile": 0, "ao": 0}

            def nxt(k, n):
                v = cnt[k] % n
                cnt[k] += 1
                return v

            for c in range(4):
                for g, col0 in ((0, c * 128), (2, 512 + c * 128), (4, 1024 + c * 128)):
                    dmab(tr, "pool", "a_w%d" % g, [], [wq], wq.t[:, :, g * 128:(g + 1) * 128],
                         self.w_mix_in[l, :, col0:col0 + 128].rearrange("(k p) n -> p k n", p=128))
                for g in (0, 2):
                    opb(tr, "pool", [], [wq], lambda e: e.memset(wq.t[:, :, (g + 1) * 128:(g + 2) * 128], 0.0))
                    for hh in range(2):
                        so = g * 128 + hh * 64
                        do = (g + 1) * 128 + hh * 64
                        opb(tr, "dve", [], [wq], lambda e: e.tensor_scalar(
                            out=wq.t[:, :, do:do + 8], in0=wq.t[:, :, so + 8:so + 16], scalar1=-1.0, scalar2=None, op0=ALU.mult))
                        opb(tr, "dve", [], [wq], lambda e: e.tensor_copy(
                            out=wq.t[:, :, do + 8:do + 16], in_=wq.t[:, :, so:so + 8]))
                for s_ in range(NSEG):
                    par = s_ % 2
                    for jl in range(4):
                        t0 = s_ * 2048 + jl * 512
                        hb = nxt("tile", 2)
                        dmab(tr, "sp", "a_lh%d" % hb, [], [ht[hb]], ht[hb].t[:],
                             self.hT[:, t0:t0 + 512].rearrange("(k p) t -> p k t", p=128))
                        dmab(tr, "sp", "a_lc%d" % hb, [], [cs[hb]], cs[hb].t[:], self.cosT[:, t0:t0 + 512])
                        dmab(tr, "sp", "a_ls%d" % hb, [], [sn[hb]], sn[hb].t[:], self.sinT[:, t0:t0 + 512])
                        nat = slice(jl * 512, (jl + 1) * 512)
                        for gi, (g, dest) in enumerate(((0, qs[par]), (2, ks[par]))):
                            pa = psp[nxt("pp", 2)]
                            for kc in range(NCH):
                                opb(tr, "pe", [wq, ht[hb]], [pa], lambda e: e.matmul(
                                    pa.t[:], lhsT=wq.t[:, kc, g * 128:(g + 1) * 128], rhs=ht[hb].t[:, kc, :],
                                    start=(kc == 0), stop=(kc == NCH - 1)))
                            opb(tr, "dve", [pa, cs[hb]], [t1b[gi]], lambda e: e.tensor_tensor(
                                out=t1b[gi].t[:], in0=pa.t[:], in1=cs[hb].t[:], op=ALU.mult))
                            pb = psp[nxt("pp", 2)]
                            for kc in range(NCH):
                                opb(tr, "pe", [wq, ht[hb]], [pb], lambda e: e.matmul(
                                    pb.t[:], lhsT=wq.t[:, kc, (g + 1) * 128:(g + 2) * 128], rhs=ht[hb].t[:, kc, :],
                                    start=(kc == 0), stop=(kc == NCH - 1)))
                            opb(tr, "dve", [pb, sn[hb]], [t2b[gi]], lambda e: e.tensor_tensor(
                                out=t2b[gi].t[:], in0=pb.t[:], in1=sn[hb].t[:], op=ALU.mult))
                            opb(tr, "pool", [t1b[gi], t2b[gi]], [dest[0]], lambda e: e.tensor_tensor(
                                out=dest[0].t[:, nat], in0=t1b[gi].t[:], in1=t2b[gi].t[:], op=ALU.add))
                            opb(tr, "pool", [dest[0]], [dest[1]], lambda e: e.tensor_copy(
                                out=dest[1].t[:].rearrange("p (r n i) -> p r n i", r=4, n=4)[:, :, jl, :],
                                in_=dest[0].t[:, nat].rearrange("p (i r) -> p r i", r=4)))
                            opb(tr, "act", [dest[0]], [dest[2]], lambda e: e.copy(
                                out=dest[2].t[:].rearrange("p (r j i) -> p r j i", r=16, j=4)[:, :, jl, :],
                                in_=dest[0].t[:, nat].rearrange("p (i r) -> p r i", r=16)))
                        pv = psp[nxt("pp", 2)]
                        for kc in range(NCH):
                            opb(tr, "pe", [wq, ht[hb]], [pv], lambda e: e.matmul(
                                pv.t[:], lhsT=wq.t[:, kc, 512:640], rhs=ht[hb].t[:, kc, :],
                                start=(kc == 0), stop=(kc == NCH - 1)))
                        opb(tr, "act", [pv], [vt[0]], lambda e: e.copy(out=vt[0].t[:, nat], in_=pv.t[:]))
                        opb(tr, "pool", [vt[0]], [vt[1]], lambda e: e.tensor_copy(
                            out=vt[1].t[:].rearrange("p (r n i) -> p r n i", r=4, n=4)[:, :, jl, :],
                            in_=vt[0].t[:, nat].rearrange("p (i r) -> p r i", r=4)))
                        opb(tr, "dve", [vt[0]], [vt[2]], lambda e: e.tensor_copy(
                            out=vt[2].t[:].rearrange("p (r j i) -> p r j i", r=16, j=4)[:, :, jl, :],
                            in_=vt[0].t[:, nat].rearrange("p (i r) -> p r i", r=16)))
                    for o in range(3):
                        for half in range(2):
                            for k_ in range(8):
                                blk = half * 8 + k_
                                opb(tr, "pe", [vt[o], cK], [pst], lambda e: e.transpose(
                                    pst.t[:, k_ * 128:(k_ + 1) * 128], vt[o].t[:, blk * 128:(blk + 1) * 128], self.ident_b[:]))
                            opb(tr, "act" if half == 0 else "dve", [pst], [V[par][o]],
                                (lambda e: e.copy(out=V[par][o].t[:, half * 8:(half + 1) * 8, :],
                                                  in_=pst.t[:].rearrange("p (k n) -> p k n", k=8))) if half == 0 else
                                (lambda e: e.tensor_copy(out=V[par][o].t[:, half * 8:(half + 1) * 8, :],
                                                         in_=pst.t[:].rearrange("p (k n) -> p k n", k=8))))
                    for o in range(3):
                        for lbp in range(8):
                            nd = psn[nxt("pn", 2)]
                            prevs = []
                            for bb in range(2):
                                lb = lbp * 2 + bb
                                if o == 0:
                                    pv_ = (par, lb - 1) if lb > 0 else ((1 - par, 15) if s_ > 0 else None)
                                elif o == 1:
                                    pv_ = (par, lb - 1) if lb % 4 > 0 else ((1 - par, lb + 3) if s_ > 0 else None)
                                else:
                                    pv_ = (1 - par, lb) if s_ > 0 else None
                                prevs.append(pv_)
                            mk = masks[0] if prevs[0] is not None else (masks[1] if prevs[1] is not None else masks[2])
                            pts = []
                            for hh in range(2):
                                rows = slice(hh * 64, (hh + 1) * 64)
                                sp_ = pss[nxt("ps", 3)]
                                opb(tr, "pe", [mk, cK], [sp_], lambda e: e.matmul(
                                    sp_.t[:], lhsT=self.ident_b[:], rhs=mk.t[:], start=True, stop=False))
                                for bb in range(2):
                                    lb = lbp * 2 + bb
                                    qv = qs[par][o].t[rows, lb * 128:(lb + 1) * 128]
                                    last = (bb == 1)
                                    if prevs[bb] is not None:
                                        pp, pl = prevs[bb]
                                        opb(tr, "pe", [ks[pp][o], qs[par][o]], [sp_], lambda e: e.matmul(
                                            sp_.t[:, bb * 256:bb * 256 + 128], lhsT=ks[pp][o].t[rows, pl * 128:(pl + 1) * 128],
                                            rhs=qv, start=False, stop=False))
                                    opb(tr, "pe", [ks[par][o], qs[par][o]], [sp_], lambda e: e.matmul(
                                        sp_.t[:, bb * 256 + 128:bb * 256 + 256], lhsT=ks[par][o].t[rows, lb * 128:(lb + 1) * 128],
                                        rhs=qv, start=False, stop=last))
                                pt = pT[nxt("pT", 4)]
                                opb(tr, "act", [sp_], [pt], lambda e: e.activation(
                                    out=pt.t[:], in_=sp_.t[:], func=AF.Exp, scale=0.125))
                                pts.append(pt)
                            for hh in range(2):
                                rows = slice(hh * 64, (hh + 1) * 64)
                                pt = pts[hh]
                                for bb in range(2):
                                    lb = lbp * 2 + bb
                                    srcs = []
                                    if prevs[bb] is not None:
                                        pp, pl = prevs[bb]
                                        srcs.append((V[pp][o], pl, bb * 256))
                                    srcs.append((V[par][o], lb, bb * 256 + 128))
                                    for kind in range(2):
                                        for si, (vb, blk, pc) in enumerate(srcs):
                                            lhs = vb.t[:, blk, rows] if kind == 0 else self.ones_b[:, 0:64]
                                            opb(tr, "pe", [vb, pt, cK], [nd], lambda e: e.matmul(
                                                nd.t[rows, kind * 256 + bb * 128:kind * 256 + (bb + 1) * 128],
                                                lhsT=lhs, rhs=pt.t[:, pc:pc + 128],
                                                start=(si == 0), stop=(si == len(srcs) - 1)))
                            for kind, acc in ((0, accN), (1, accD)):
                                if o == 0:
                                    av = acc.t[:, lbp * 256:(lbp + 1) * 256].rearrange("p (b i) -> p b i", b=2)
                                elif o == 1:
                                    r_, n0 = lbp // 2, (lbp % 2) * 2
                                    av = acc.t[:].rearrange("p (n i r) -> p r n i", n=4, r=4)[:, r_, n0:n0 + 2, :]
                                else:
                                    av = acc.t[:].rearrange("p (i r) -> p r i", r=16)[:, lbp * 2:lbp * 2 + 2, :]
                                sv = nd.t[:, kind * 256:(kind + 1) * 256].rearrange("p (b i) -> p b i", b=2)
                                if o == 0:
                                    opb(tr, "dve", [nd], [acc], lambda e: e.tensor_copy(out=av, in_=sv))
                                else:
                                    opb(tr, "dve", [nd], [acc], lambda e: e.tensor_tensor(out=av, in0=av, in1=sv, op=ALU.add))
                    ab = ao[nxt("ao", 2)]
                    opb(tr, "dve", [], [accD], lambda e: e.reciprocal(out=accD.t[:], in_=accD.t[:]))
                    opb(tr, "pool", [accN, accD], [ab], lambda e: e.tensor_tensor(
                        out=ab.t[:], in0=accN.t[:], in1=accD.t[:], op=ALU.mult))
                    dmab(tr, "sp", "a_so%d" % (cnt["ao"] % 2), [ab], [],
                         self.mixT[c * 128:(c + 1) * 128, s_ * 2048:(s_ + 1) * 2048], ab.t[:])
            tr.barrier()

    def hgrn(self, l):
        nc, tr, S = self.nc, self.tr, self.S
        NT = S // 512
        with ExitStack() as es:
            B = lambda name, shape, dt: Buf(self.sb(es, name, shape, dt))
            P = lambda name, shape, dt=F32: Buf(self.ps(es, name, shape, dt))
            cK = Buf(None)
            cK.w = {t[0]: t for t in self.const_tok + self.lb_tok + self.setup_tok}
            wh = B("h_w", [128, NCH, 2048], BF16)
            ht = [B("h_ht%d" % i, [128, NCH, 512], BF16) for i in range(2)]
            qsl = [B("h_qs%d" % i, [128, 512], F32) for i in range(2)]
            fb = [B("h_f%d" % i, [128, 512], F32) for i in range(2)]
            ab_ = [B("h_a%d" % i, [128, 512], F32) for i in range(2)]
            kk = [B("h_kk%d" % i, [128, 512], F32) for i in range(2)]
            bc = [B("h_b%d" % i, [128, 512], F32) for i in range(2)]
            e1 = [B("h_e1%d" % i, [128, 512], F32) for i in range(2)]
            e2 = [B("h_e2%d" % i, [128, 512], F32) for i in range(2)]
            kh = [B("h_kh%d" % i, [128, 512], BF16) for i in range(2)]
            QT = [[B("h_QT%d%d" % (h, i), [128, 512], BF16) for i in range(2)] for h in range(4)]
            KT = [[B("h_KT%d%d" % (h, i), [128, 512], BF16) for i in range(2)] for h in range(4)]
            KK = [[B("h_KK%d%d" % (h, i), [64, 8, 128], BF16) for i in range(2)] for h in range(4)]
            SG = [[B("h_SG%d%d" % (h, i), [128, 512], BF16) for i in range(2)] for h in range(4)]
            DEC = [[B("h_DC%d%d" % (h, i), [128, 8], F32) for i in range(2)] for h in range(4)]
            VV = [B("h_VV%d" % i, [64, 8, 512], BF16) for i in range(2)]
            ST = [B("h_ST%d" % h, [128, 128], F32) for h in range(4)]
            STb = [B("h_STb%d" % h, [128, 128], BF16) for h in range(4)]
            AM = [B("h_AM%d" % i, [64, 256], BF16) for i in range(2)]
            O32 = [B("h_O%d" % i, [128, 4, 512], F32) for i in range(2)]
            osq = B("h_osq", [128, 512], BF16)
            rt = B("h_rt", [128, 512], F32)
            g1 = B("h_g1", [128, 512], F32)
            GO = [B("h_GO%d" % i, [128, 512], BF16) for i in range(2)]
            tri4 = B("h_tri", [64, 256], BF16)
            rmask = B("h_rm", [128, 512], F32)
            psp = [P("h_pp%d" % i, [128, 512]) for i in range(2)]
            psA = P("h_pA", [128, 512])
            pso = [P("h_po%d" % i, [128, 512]) for i in range(2)]
            psS = P("h_pS", [128, 512])
            pst = P("h_pt", [128, 1024], BF16)
            psm = P("h_pm", [128, 512])
            cnt = {"pp": 0, "po": 0, "am": 0, "t": 0, "go": 0}

            def nxt(k, n):
                v = cnt[k] % n
                cnt[k] += 1
                return v

            for h in range(4):
                opb(tr, "dve", [cK], [tri4], lambda e: e.tensor_copy(
                    out=tri4.t[:, h * 64:(h + 1) * 64], in_=self.cst[0:64, 640:704]))
            opb(tr, "pool", [], [rmask], lambda e: e.memset(rmask.t[:], 1.0))
            opb(tr, "pool", [], [rmask], lambda e: e.memset(
                rmask.t[:].rearrange("p (n t) -> p n t", t=64)[:, :, 0:1], 0.0))
            for h in range(4):
                opb(tr, "pool", [], [ST[h]], lambda e: e.memset(ST[h].t[:], 0.0))
                opb(tr, "pool", [], [STb[h]], lambda e: e.memset(STb[h].t[:], 0.0))
            for kc in range(NCH):
                dmab(tr, "pool", "h_w%d" % kc, [], [wh], wh.t[:, kc, :],
                     self.w_mix_in[l, kc * 128:(kc + 1) * 128, 1536:3584])
            lbc = lambda h: self.lbv[:, l * 4 + h:l * 4 + h + 1]
            omc = lambda h: self.oml[:, l * 4 + h:l * 4 + h + 1]
            nwc = lambda h: self.vecs[:, V_NW + l * 4 + h:V_NW + l * 4 + h + 1]

            def proj(hb, col0, dst):
                for kc in range(NCH):
                    opb(tr, "pe", [wh, ht[hb]], [dst], lambda e: e.matmul(
                        dst.t[:], lhsT=wh.t[:, kc, col0:col0 + 128], rhs=ht[hb].t[:, kc, :],
                        start=(kc == 0), stop=(kc == NCH - 1)))

            for j in range(NT):
                t0 = j * 512
                tb = nxt("t", 2)
                dmab(tr, "sp", "h_lh%d" % tb, [], [ht[tb]], ht[tb].t[:],
                     self.hT[:, t0:t0 + 512].rearrange("(k p) t -> p k t", p=128))
                for h in range(4):
                    i2 = h % 2
                    pq = psp[nxt("pp", 2)]
                    proj(tb, h * 128, pq)
                    opb(tr, "act", [pq], [qsl[i2]], lambda e: e.activation(out=qsl[i2].t[:], in_=pq.t[:], func=AF.Silu))
                    pg = psp[nxt("pp", 2)]
                    proj(tb, 1536 + h * 128, pg)
                    opb(tr, "act", [pg], [SG[h][tb]], lambda e: e.activation(out=SG[h][tb].t[:], in_=pg.t[:], func=AF.Silu))
                    pf = psp[nxt("pp", 2)]
                    proj(tb, 512 + h * 128, pf)
                    opb(tr, "act", [pf], [fb[i2]], lambda e: e.activation(out=fb[i2].t[:], in_=pf.t[:], func=AF.Sigmoid))
                    opb(tr, "dve", [cK], [fb[i2]], lambda e: e.tensor_scalar(
                        out=fb[i2].t[:], in0=fb[i2].t[:], scalar1=omc(h), scalar2=lbc(h), op0=ALU.mult, op1=ALU.add))
                    opb(tr, "act", [fb[i2]], [ab_[i2]], lambda e: e.activation(out=ab_[i2].t[:], in_=fb[i2].t[:], func=AF.Ln))
                    opb(tr, "pool", [fb[i2]], [kk[i2]], lambda e: e.tensor_scalar(
                        out=kk[i2].t[:], in0=fb[i2].t[:], scalar1=-1.0, scalar2=1.0, op0=ALU.mult, op1=ALU.add))
                    opb(tr, "dve", [rmask, ab_[i2]], [bc[i2]], lambda e: e.tensor_tensor_scan(
                        out=bc[i2].t[:], data0=rmask.t[:], data1=ab_[i2].t[:], initial=0.0, op0=ALU.mult, op1=ALU.add))
                    opb(tr, "act", [bc[i2]], [e1[i2]], lambda e: e.activation(out=e1[i2].t[:], in_=bc[i2].t[:], func=AF.Exp))
                    opb(tr, "act", [bc[i2]], [e2[i2]], lambda e: e.activation(out=e2[i2].t[:], in_=bc[i2].t[:], func=AF.Exp, scale=-1.0))
                    opb(tr, "pool", [qsl[i2], e1[i2]], [QT[h][tb]], lambda e: e.tensor_tensor(
                        out=QT[h][tb].t[:], in0=qsl[i2].t[:], in1=e1[i2].t[:], op=ALU.mult))
                    opb(tr, "dve", [kk[i2], e2[i2]], [KT[h][tb]], lambda e: e.tensor_tensor(
                        out=KT[h][tb].t[:], in0=kk[i2].t[:], in1=e2[i2].t[:], op=ALU.mult))
                    e1v = e1[i2].t[:].rearrange("p (n t) -> p n t", t=64)
                    opb(tr, "dve", [e1[i2]], [DEC[h][tb]], lambda e: e.tensor_copy(
                        out=DEC[h][tb].t[:].rearrange("p (n o) -> p n o", o=1), in_=e1v[:, :, 63:64]))
                    opb(tr, "pool", [KT[h][tb], e1[i2]], [kh[i2]], lambda e: e.tensor_tensor(
                        out=kh[i2].t[:].rearrange("p (n t) -> p n t", t=64),
                        in0=KT[h][tb].t[:].rearrange("p (n t) -> p n t", t=64),
                        in1=e1v[:, :, 63:64].to_broadcast([128, 8, 64]), op=ALU.mult))
                    for ch in range(8):
                        opb(tr, "pe", [kh[i2], cK], [pst], lambda e: e.transpose(
                            pst.t[0:64, ch * 128:(ch + 1) * 128], kh[i2].t[:, ch * 64:(ch + 1) * 64], self.ident_b[:]))
                    opb(tr, "act", [pst], [KK[h][tb]], lambda e: e.copy(
                        out=KK[h][tb].t[:], in_=pst.t[0:64, :].rearrange("p (k n) -> p k n", k=8)))
                for ch in range(8):
                    pv = psp[nxt("pp", 2)]
                    for kc in range(NCH):
                        opb(tr, "pe", [wh, ht[tb]], [pv], lambda e: e.matmul(
                            pv.t[0:64, :], lhsT=ht[tb].t[:, kc, ch * 64:(ch + 1) * 64], rhs=wh.t[:, kc, 1024:1536],
                            start=(kc == 0), stop=(kc == NCH - 1)))
                    if ch % 2 == 0:
                        opb(tr, "act", [pv], [VV[tb]], lambda e: e.copy(out=VV[tb].t[:, ch, :], in_=pv.t[0:64, :]))
                    else:
                        opb(tr, "dve", [pv], [VV[tb]], lambda e: e.tensor_copy(out=VV[tb].t[:, ch, :], in_=pv.t[0:64, :]))
                ob = O32[tb]
                for ch in range(8):
                    cols = slice(ch * 64, (ch + 1) * 64)
                    for h in range(4):
                        opb(tr, "pe", [KT[h][tb], QT[h][tb]], [psA], lambda e: e.matmul(
                            psA.t[0:64, h * 64:(h + 1) * 64], lhsT=KT[h][tb].t[:, cols], rhs=QT[h][tb].t[:, cols],
                            start=True, stop=True))
                    am = AM[nxt("am", 2)]
                    opb(tr, "dve", [psA, tri4], [am], lambda e: e.tensor_tensor(
                        out=am.t[:], in0=psA.t[0:64, 0:256], in1=tri4.t[:], op=ALU.mult))
                    po = pso[nxt("po", 2)]
                    for h in range(4):
                        opb(tr, "pe", [VV[tb], am], [po], lambda e: e.matmul(
                            po.t[:, h * 64:(h + 1) * 64], lhsT=VV[tb].t[:, ch, h * 128:(h + 1) * 128], rhs=am.t[:, h * 64:(h + 1) * 64],
                            start=True, stop=False))
                        opb(tr, "pe", [STb[h], QT[h][tb]], [po], lambda e: e.matmul(
                            po.t[:, h * 64:(h + 1) * 64], lhsT=STb[h].t[:], rhs=QT[h][tb].t[:, cols],
                            start=False, stop=True))
                    opb(tr, "act", [po], [ob], lambda e: e.copy(
                        out=ob.t[:, :, cols], in_=po.t[:, 0:256].rearrange("p (h t) -> p h t", h=4)))
                    for h in range(4):
                        opb(tr, "pe", [KK[h][tb], VV[tb]], [psS], lambda e: e.matmul(
                            psS.t[:, h * 128:(h + 1) * 128], lhsT=KK[h][tb].t[:, ch, :], rhs=VV[tb].t[:, ch, h * 128:(h + 1) * 128],
                            start=True, stop=True))
                    for h in range(4):
                        opb(tr, "dve", [psS, DEC[h][tb]], [ST[h]], lambda e: e.scalar_tensor_tensor(
                            out=ST[h].t[:], in0=ST[h].t[:], scalar=DEC[h][tb].t[:, ch:ch + 1], in1=psS.t[:, h * 128:(h + 1) * 128],
                            op0=ALU.mult, op1=ALU.add))
                        opb(tr, "act", [ST[h]], [STb[h]], lambda e: e.copy(out=STb[h].t[:], in_=ST[h].t[:]))
                for h in range(4):
                    opb(tr, "act", [ob], [osq], lambda e: e.activation(out=osq.t[:], in_=ob.t[:, h, :], func=AF.Square))
                    opb(tr, "pe", [osq, cK], [psm], lambda e: e.matmul(
                        psm.t[:], lhsT=self.ones_b[:], rhs=osq.t[:], start=True, stop=True))
                    opb(tr, "act", [psm, cK], [rt], lambda e: e.activation(
                        out=rt.t[:], in_=psm.t[:], func=AF.Sqrt, bias=self.epsc(RMS_EPS), scale=1.0 / 128))
                    opb(tr, "dve", [], [rt], lambda e: e.reciprocal(out=rt.t[:], in_=rt.t[:]))
                    opb(tr, "dve", [ob, rt], [g1], lambda e: e.tensor_tensor(
                        out=g1.t[:], in0=ob.t[:, h, :], in1=rt.t[:], op=ALU.mult))
                    go = GO[nxt("go", 2)]
                    opb(tr, "dve", [g1, SG[h][tb], cK], [go], lambda e: e.scalar_tensor_tensor(
                        out=go.t[:], in0=g1.t[:], scalar=nwc(h), in1=SG[h][tb].t[:], op0=ALU.mult, op1=ALU.mult))
                    dmab(tr, "sp", "h_so%d" % (cnt["go"] % 2), [go], [],
                         self.mixT[512 + h * 128:512 + (h + 1) * 128, t0:t0 + 512], go.t[:])
            tr.barrier()

    def ffn(self, l, sub, src, dst, hdst=None):
        nc, tr, S = self.nc, self.tr, self.S
        T = 256
        is_ffn = sub != 1
        which = 0 if sub == 0 else 1
        base = l * 72 + sub * 24
        sh = lambda c: self.msc[:, base + c:base + c + 1]
        sc = lambda c: self.msc[:, base + 8 + c:base + 9 + c]
        gf = lambda c: self.msc[:, base + 16 + c:base + 17 + c]
        nb_ = l * 72 + (sub + 1) * 24
        sh2 = lambda c: self.msc[:, nb_ + c:nb_ + c + 1]
        sc2 = lambda c: self.msc[:, nb_ + 8 + c:nb_ + 9 + c]
        gcol = lambda c: self.vecs[:, V_LNG + (l * 3 + sub) * 8 + c:V_LNG + (l * 3 + sub) * 8 + c + 1]
        bcol = lambda c: self.vecs[:, V_LNB + (l * 3 + sub) * 8 + c:V_LNB + (l * 3 + sub) * 8 + c + 1]
        eps = LN_EPS / (ALPHA * ALPHA)
        nh = NHC if is_ffn else NCH
        with ExitStack() as es:
            if is_ffn:
                w_in = self.sb(es, "w_in", [128, NCH, 2 * DFF], BF16)
                ht = [self.sb(es, "f_h%d" % i, [128, NCH, T], BF16) for i in range(1)]
                sg = [self.sb(es, "f_sg%d" % i, [128, T], F32) for i in range(2)]
                psg = [self.ps(es, "f_pg%d" % i, [128, 512], F32) for i in range(4)]
            w_out = self.sb(es, "w_out", [128, nh, D], BF16)
            xt = [self.sb(es, "f_x%d" % i, [128, NCH, T], F32) for i in range(2)]
            hid = [self.sb(es, "f_hid%d" % i, [128, nh, T], BF16) for i in range(1 if is_ffn else 2)]
            zb = self.sb(es, "f_zb", [128, NCH, T], BF16)
            zq = self.sb(es, "f_zq", [128, NCH, T], BF16)
            st = self.sb(es, "f_st", [128, 4, T], F32)
            if hdst is not None:
                hb = [self.sb(es, "f_hb%d" % i, [128, NCH, T], BF16) for i in range(1)]
            psy = [self.ps(es, "f_py%d" % i, [128, 512], F32) for i in range(3)]
            pss = self.ps(es, "f_pss", [128, 512], F32)
            wtok = []
            wtok2 = []
            if is_ffn:
                for c in range(NCH):
                    wtok.append(tr.dma("pool", None, "w_a%d" % c, w_in[:, c, :],
                                       self.w_ffn_in[which][l, c * 128:(c + 1) * 128, :]))
                for m0 in range(0, NHC, 2):
                    wtok2.append(tr.dma("pool", None, "w_b%d" % (m0 // 2), w_out[:, m0:m0 + 2, :],
                                        self.w_ffn_out[which][l, m0 * 128:(m0 + 2) * 128, :].rearrange("(m p) d -> p m d", p=128)))
            else:
                for m0 in range(0, NCH, 2):
                    wtok2.append(tr.dma("pool", None, "w_b%d" % (m0 // 2), w_out[:, m0:m0 + 2, :],
                                        self.w_mix_out[l, m0 * 128:(m0 + 2) * 128, :].rearrange("(m p) d -> p m d", p=128)))
            x_free = [None, None]
            h_free = [None]
            hid_free = [None, None]
            hb_free = [None, None]
            sg_free = [None, None]
            psg_free = [None] * 4
            psy_free = [None] * 3
            pss_free = None
            zb_free = None
            st_free = None
            ng = 0
            ny = 0
            nsg = 0
            nhb = 1 if is_ffn else 2

            def load_h(it):
                b = it % 2
                t0 = it * T
                tl = tr.dma("sp", x_free[b], "f_l%d" % b, xt[b][:],
                            src[:, t0:t0 + T].rearrange("(c p) t -> p c t", p=128))
                th = []
                if is_ffn:
                    for c in range(NCH):
                        th.append(tr.op("act", [tl, h_free[0], self.setup_tok], lambda e: e.activation(
                            out=ht[0][:, c, :], in_=xt[b][:, c, :], func=AF.Identity, bias=sh(c), scale=sc(c))))
                else:
                    hb_ = it % nhb
                    th = tr.dma("sp", hid_free[hb_], "f_lm%d" % hb_, hid[hb_][:],
                                self.mixT[:, t0:t0 + T].rearrange("(c p) t -> p c t", p=128))
                return tl, th

            pre = load_h(0)
            for it in range(S // T):
                b = it % 2
                hbi = it % nhb
                t0 = it * T
                tl, th = pre
                if is_ffn:
                    thid = []
                    for m in range(NHC):
                        pb = ng % 4
                        ng += 1
                        tm = None
                        for half in range(2):
                            col0 = half * DFF + m * 128
                            for c in range(NCH):
                                tm = tr.op("pe", [th[c], wtok[c], psg_free[pb]], lambda e: e.matmul(
                                    psg[pb][:, half * T:(half + 1) * T], lhsT=w_in[:, c, col0:col0 + 128], rhs=ht[0][:, c, :],
                                    start=(c == 0), stop=(c == NCH - 1)))
                        sb_ = nsg % 2
                        nsg += 1
                        ts = tr.op("act", [tm, sg_free[sb_]], lambda e: e.activation(
                            out=sg[sb_][:], in_=psg[pb][:, 0:T], func=AF.Silu))
                        tu = tr.op("dve", [ts, tm, hid_free[hbi]], lambda e: e.tensor_tensor(
                            out=hid[hbi][:, m, :], in0=sg[sb_][:], in1=psg[pb][:, T:2 * T], op=ALU.mult))
                        psg_free[pb] = tu
                        sg_free[sb_] = tu
                        thid.append(tu)
                    h_free[0] = tm
                else:
                    thid = [th] * nh
                if it + 1 < S // T:
                    pre = load_h(it + 1)
                tz = []
                tzb = []
                for dc in range(NCH):
                    pb = ny % 3
                    ny += 1
                    tm = None
                    for m in range(nh):
                        tm = tr.op("pe", [thid[m], wtok2[m // 2], psy_free[pb]], lambda e: e.matmul(
                            psy[pb][:, 0:T], lhsT=w_out[:, m, dc * 128:(dc + 1) * 128], rhs=hid[hbi][:, m, :],
                            start=(m == 0), stop=(m == nh - 1)))
                    t1 = tr.op("dve", [tm, tl, self.setup_tok], lambda e: e.scalar_tensor_tensor(
                        out=xt[b][:, dc, :], in0=psy[pb][:, 0:T], scalar=gf(dc), in1=xt[b][:, dc, :],
                        op0=ALU.mult, op1=ALU.add))
                    psy_free[pb] = t1
                    tz.append(t1)
                    t2 = tr.op("act", [t1, zb_free], lambda e: e.copy(out=zb[:, dc, :], in_=xt[b][:, dc, :]))
                    t3 = tr.op("act", [t1, zb_free], lambda e: e.activation(
                        out=zq[:, dc, :], in_=xt[b][:, dc, :], func=AF.Square))
                    tzb.append([t2, t3])
                hid_free[hbi] = tm
                tm = None
                for half in range(2):
                    srcb = zb if half == 0 else zq
                    for dc in range(NCH):
                        tm = tr.op("pe", [tzb[dc], pss_free, self.const_tok], lambda e: e.matmul(
                            pss[:, half * T:(half + 1) * T], lhsT=self.ones_b[:], rhs=srcb[:, dc, :],
                            start=(dc == 0), stop=(dc == NCH - 1)))
                zb_free = tm
                ta = tr.op("dve", [tm, st_free], lambda e: e.tensor_scalar(
                    out=st[:, 0:2, :], in0=pss[:].rearrange("p (a t) -> p a t", a=2), scalar1=1.0 / D, scalar2=None, op0=ALU.mult))
                pss_free = ta
                tb = tr.op("dve", ta, lambda e: e.tensor_tensor(out=st[:, 2, :], in0=st[:, 0, :], in1=st[:, 0, :], op=ALU.mult))
                tc = tr.op("dve", tb, lambda e: e.tensor_tensor(out=st[:, 3, :], in0=st[:, 1, :], in1=st[:, 2, :], op=ALU.subtract))
                tc2 = tr.op("act", tc, lambda e: e.activation(
                    out=st[:, 3, :], in_=st[:, 3, :], func=AF.Sqrt, bias=self.epsc(eps), scale=1.0))
                td = tr.op("dve", tc2, lambda e: e.reciprocal(out=st[:, 2, :], in_=st[:, 3, :]))
                touts = []
                thb = []
                for dc in range(NCH):
                    eng = "dve" if dc % 2 == 0 else "pool"
                    t1 = tr.op(eng, [td, tz[dc], tzb[dc]], lambda e: e.tensor_tensor(
                        out=xt[b][:, dc, :], in0=xt[b][:, dc, :], in1=st[:, 0, :], op=ALU.subtract))
                    t2 = tr.op(eng, t1, lambda e: e.tensor_tensor(
                        out=xt[b][:, dc, :], in0=xt[b][:, dc, :], in1=st[:, 2, :], op=ALU.mult))
                    t3 = tr.op("act", t2, lambda e: e.activation(
                        out=xt[b][:, dc, :], in_=xt[b][:, dc, :], func=AF.Identity, bias=bcol(dc), scale=gcol(dc)))
                    touts.append(t3)
                    if hdst is not None:
                        thb.append(tr.op("act", [t3, hb_free[0]], lambda e: e.activation(
                            out=hb[0][:, dc, :], in_=xt[b][:, dc, :], func=AF.Identity, bias=sh2(dc), scale=sc2(dc))))
                st_free = touts
                tst = tr.dma("sp", touts, "f_s%d" % b,
                             dst[:, t0:t0 + T].rearrange("(c p) t -> p c t", p=128), xt[b][:])
                x_free[b] = [tst] + thb
                if hdst is not None:
                    hb_free[0] = tr.dma("sp", thb, "f_sh",
                                        hdst[:, t0:t0 + T].rearrange("(c p) t -> p c t", p=128), hb[0][:])
            tr.barrier()


NEG = -30000.0


def host_consts():
    c = np.zeros((128, 1024), np.float32)
    c[:, 0:128] = np.eye(128, dtype=np.float32)
    p = np.arange(128)
    ch = p % 64
    invf = np.where(ch < 16, np.float32(ROPE_THETA) ** (-(ch % 8).astype(np.float32) * np.float32(2.0 / 16)), 0.0)
    c[:, 128] = invf.astype(np.float32)
    c[:, 129] = math.pi / 2
    j = p[:, None]
    i = p[None, :]
    c[:, 256:384] = np.where(j >= i, 0.0, NEG)
    c[:, 384:512] = np.where(j <= i, 0.0, NEG)
    c[:, 512:640] = NEG
    s_ = (p % 64)[:, None]
    t_ = np.arange(64)[None, :]
    c[:, 640:704] = (s_ <= t_).astype(np.float32)
    return c


def pack_vecs(b, c, ln_g, ln_b, ada_b, hgrn_norm_w, hgrn_lb_logits):
    v = np.zeros((NVEC, 128), np.float32)
    v[V_ADAB:V_ADAB + 144] = ada_b.reshape(144, 128)
    v[V_LNG:V_LNG + 48] = ln_g.reshape(48, 128)
    v[V_LNB:V_LNB + 48] = ln_b.reshape(48, 128)
    v[V_C:V_C + 8] = c[b].reshape(8, 128)
    v[V_NW:V_NW + 8] = hgrn_norm_w.reshape(8, 128)
    v[V_LB:V_LB + 8] = hgrn_lb_logits.reshape(8, 128)
    return v


def make_in_maps(inputs, ncores, S):
    g = lambda k: np.ascontiguousarray(np.asarray(inputs[k]))
    consts = host_consts()
    maps = []
    for b in range(ncores):
        m = {
            "x": np.ascontiguousarray(g("x")[b, :S]),
            "pos": np.ascontiguousarray(g("positions")[b, :S].reshape(1, S).astype(np.int32)),
            "vecs": pack_vecs(b, g("c"), g("ln_g"), g("ln_b"), g("ada_b"), g("hgrn_norm_w"), g("hgrn_lb_logits")),
            "consts": consts,
            "ada_w": g("ada_w"),
            "ffn1_w_in": g("ffn1_w_in"), "ffn2_w_in": g("ffn2_w_in"),
            "ffn1_w_out": g("ffn1_w_out"), "ffn2_w_out": g("ffn2_w_out"),
            "mix_w_in": g("mix_w_in"), "mix_w_out": g("mix_w_out"),
        }
        maps.append(m)
    return maps


def kernel(**inputs):
    S = 8192
    nc = Builder(S).build()
    maps = make_in_maps(inputs, 8, S)
    res = run_bass_kernel_spmd(nc, maps, core_ids=list(range(8)))
    return np.stack([r["out"] for r in res.results], axis=0)
```

```python
import math
from contextlib import ExitStack
import numpy as np
import concourse.bass as bass
import concourse.mybir as mybir
from concourse.bass_utils import run_bass_kernel_spmd

F32 = mybir.dt.float32
BF16 = mybir.dt.bfloat16
I32 = mybir.dt.int32
AF = mybir.ActivationFunctionType
ALU = mybir.AluOpType

D = 1024
NCH = 8
DEPTH = 2
DFF = 2816
NHC = DFF // 128
IN_COLS = 3584
ALPHA = (2 * DEPTH) ** 0.25
LN_EPS = 1e-5
RMS_EPS = 1e-6
ROPE_THETA = 500000.0

V_ADAB = 0
V_LNG = 144
V_LNB = 192
V_C = 240
V_NW = 248
V_LB = 256
NVEC = 264


class Tracker:
    def __init__(self, nc, es):
        self.nc = nc
        self.es = es
        self.sems = {}
        self.cnt = {}
        self.seen = {}
        self.engs = {"pe": nc.tensor, "act": nc.scalar, "dve": nc.vector, "pool": nc.gpsimd, "sp": nc.sync}
        for n in self.engs:
            self.sems[n] = es.enter_context(nc.semaphore("s_" + n))
            self.cnt[n] = 0
            self.seen[n] = {}
        self.dsems = {}

    def dsem(self, name):
        if name not in self.dsems:
            s = self.es.enter_context(self.nc.semaphore("d_" + name))
            self.dsems[name] = [s, 0]
        return self.dsems[name]

    def wait(self, en, tok):
        if tok is None:
            return
        if isinstance(tok, list):
            for t in tok:
                self.wait(en, t)
            return
        key, sem, val = tok
        if self.seen[en].get(key, 0) >= val:
            return
        self.engs[en].wait_ge(sem, val)
        self.seen[en][key] = val

    def op(self, en, deps, inst_fn):
        self.wait(en, deps)
        inst = inst_fn(self.engs[en])
        self.cnt[en] += 1
        inst.then_inc(self.sems[en], 1)
        return ("e_" + en, self.sems[en], self.cnt[en])

    def last(self, en):
        if self.cnt[en] == 0:
            return None
        return ("e_" + en, self.sems[en], self.cnt[en])

    def dma(self, en, deps, dname, out, in_):
        self.wait(en, deps)
        d = self.dsem(dname)
        self.engs[en].dma_start(out=out, in_=in_).then_inc(d[0], 16)
        d[1] += 16
        return ("d_" + dname, d[0], d[1])

    def all_tokens(self):
        toks = [self.last(n) for n in self.engs if self.cnt[n]]
        toks += [("d_" + k, v[0], v[1]) for k, v in self.dsems.items() if v[1]]
        return toks

    def barrier(self):
        toks = self.all_tokens()
        for en in self.engs:
            self.wait(en, toks)


class Buf:
    def __init__(self, t):
        self.t = t
        self.w = {}
        self.r = {}


def _deps(reads, writes, deps, en=None):
    d = []
    for b in reads:
        d += list(b.w.values())
    for b in writes:
        d += list(b.w.values()) + list(b.r.values())
    if deps:
        d += deps if isinstance(deps, list) else [deps]
    if en == "pe":
        d = [t for t in d if t[0] != "e_pe"]
    return d


def _mark(tok, reads, writes):
    for b in reads:
        if b not in writes:
            b.r[tok[0]] = tok
    for b in writes:
        b.w[tok[0]] = tok
        b.r = {}


def opb(tr, en, reads, writes, fn, deps=None):
    tok = tr.op(en, _deps(reads, writes, deps, en), fn)
    _mark(tok, reads, writes)
    return tok


def dmab(tr, en, dname, reads, writes, out, in_, deps=None):
    tok = tr.dma(en, _deps(reads, writes, deps), dname, out, in_)
    _mark(tok, reads, writes)
    return tok


def interleave(gens):
    gens = list(gens)
    while gens:
        for g in list(gens):
            try:
                next(g)
            except StopIteration:
                gens.remove(g)


class Builder:
    def __init__(self, S, layers=DEPTH, debug_outs=(), stop_after=None):
        self.S = S
        self.layers = layers
        self.debug_outs = debug_outs
        self.stop_after = stop_after

    def sb(self, es, name, shape, dt):
        self.uid = getattr(self, "uid", 0) + 1
        return es.enter_context(self.nc.sbuf_tensor("sb%d_%s" % (self.uid, name), shape, dt))

    def ps(self, es, name, shape, dt=F32):
        self.uid = getattr(self, "uid", 0) + 1
        return es.enter_context(self.nc.psum_tensor("ps%d_%s" % (self.uid, name), shape, dt))

    def build(self):
        S = self.S
        nc = bass.Bass("TRN2", target_bir_lowering=False)
        self.nc = nc
        dt_ = nc.dram_tensor
        self.x_in = dt_("x", [S, D], F32, kind="ExternalInput").ap()
        self.pos_in = dt_("pos", [1, S], I32, kind="ExternalInput").ap()
        self.vecs_in = dt_("vecs", [NVEC, 128], F32, kind="ExternalInput").ap()
        self.consts_in = dt_("consts", [128, 1024], F32, kind="ExternalInput").ap()
        self.ada_w = dt_("ada_w", [DEPTH, D, 9 * D], F32, kind="ExternalInput").ap()
        self.w_ffn_in = [dt_("ffn1_w_in", [DEPTH, D, 2 * DFF], F32, kind="ExternalInput").ap(),
                         dt_("ffn2_w_in", [DEPTH, D, 2 * DFF], F32, kind="ExternalInput").ap()]
        self.w_ffn_out = [dt_("ffn1_w_out", [DEPTH, DFF, D], F32, kind="ExternalInput").ap(),
                          dt_("ffn2_w_out", [DEPTH, DFF, D], F32, kind="ExternalInput").ap()]
        self.w_mix_in = dt_("mix_w_in", [DEPTH, D, IN_COLS], F32, kind="ExternalInput").ap()
        self.w_mix_out = dt_("mix_w_out", [DEPTH, D, D], F32, kind="ExternalInput").ap()
        self.out = dt_("out", [S, D], F32, kind="ExternalOutput").ap()

        def scratch(name, shape, dt):
            kind = "ExternalOutput" if name in self.debug_outs else "Internal"
            return dt_(name, shape, dt, kind=kind).ap()
        self.xT = [scratch("xT0", [D, S], F32), scratch("xT1", [D, S], F32)]
        self.mixT = scratch("mixT", [D, S], BF16)
        self.hT = scratch("hT", [D, S], BF16)
        self.cosT = scratch("cosT", [128, S], F32)
        self.sinT = scratch("sinT", [128, S], F32)

        with ExitStack() as es:
            self.tr = Tracker(nc, es)
            self.setup(es)
            self.run_phases()
            self.tr.barrier()
        return nc

    def setup(self, es):
        nc, tr = self.nc, self.tr
        self.ident_f = self.sb(es, "ident_f", [128, 128], F32)
        self.ident_b = self.sb(es, "ident_b", [128, 128], BF16)
        self.ones_b = self.sb(es, "ones_b", [128, 128], BF16)
        self.cst = self.sb(es, "cst", [128, 1024], F32)
        self.vecs = self.sb(es, "vecs", [128, NVEC], F32)
        self.ada = self.sb(es, "ada", [128, DEPTH * 72], F32)
        self.msc = self.sb(es, "msc", [128, DEPTH * 72], F32)
        t0 = tr.dma("sp", None, "setup_c", self.cst[:], self.consts_in)
        tok = tr.op("dve", t0, lambda e: e.tensor_copy(out=self.ident_f[:], in_=self.cst[:, 0:128]))
        tokb = tr.op("dve", tok, lambda e: e.tensor_copy(out=self.ident_b[:], in_=self.cst[:, 0:128]))
        tok1 = tr.op("pool", None, lambda e: e.memset(self.ones_b[:], 1.0))
        self.lbv = self.sb(es, "lbv", [128, 8], F32)
        self.oml = self.sb(es, "oml", [128, 8], F32)
        self.epst = self.sb(es, "epst", [128, 4], F32)
        tok2 = tr.op("pool", None, lambda e: e.memset(self.epst[:, 0:1], LN_EPS / (ALPHA * ALPHA)))
        tok3 = tr.op("pool", None, lambda e: e.memset(self.epst[:, 1:2], RMS_EPS))
        self.const_tok = [tok, tokb, tok1, tok2, tok3, t0]
        with ExitStack() as es2:
            vraw = self.sb(es2, "vraw", [128, 3, 128], F32)
            pst = self.ps(es2, "pst", [128, 512], F32)
            psa = self.ps(es2, "psa", [128, 512], F32)
            cond = self.sb(es2, "cond", [128, 8], F32)
            wbuf = [self.sb(es2, "adaw%d" % i, [128, 8, 1152], F32) for i in range(2)]
            nrow = [128, 128, NVEC - 256]
            toks = []
            for i in range(3):
                toks.append(tr.dma("sp", None, "setup_v%d" % i, vraw[0:nrow[i], i, :],
                                   self.vecs_in[i * 128:i * 128 + nrow[i], :]))
            tp = None
            for i in range(3):
                tp = tr.op("pe", [toks[i], tok], lambda e: e.transpose(
                    pst[:, i * 128:i * 128 + nrow[i]], vraw[0:nrow[i], i, :], self.ident_f[0:nrow[i], 0:nrow[i]]))
            tv = tr.op("dve", tp, lambda e: e.tensor_copy(out=self.vecs[:], in_=pst[:, 0:NVEC]))
            tc = tr.op("act", tv, lambda e: e.activation(out=cond[:], in_=self.vecs[:, V_C:V_C + 8], func=AF.Silu))
            free = [None, None]
            blk = 0
            for l in range(self.layers):
                for cb in range(8):
                    buf = wbuf[blk % 2]
                    tl = tr.dma("sp", free[blk % 2], "adaw%d" % (blk % 2), buf[:],
                                self.ada_w[l, :, cb * 1152:(cb + 1) * 1152].rearrange("(c p) n -> p c n", p=128))
                    tm = None
                    for j in range(9):
                        col = l * 72 + cb * 9 + j
                        for kc in range(8):
                            tm = tr.op("pe", [tl, tc], lambda e: e.matmul(
                                psa[:, col:col + 1], lhsT=buf[:, kc, j * 128:(j + 1) * 128], rhs=cond[:, kc:kc + 1],
                                start=(kc == 0), stop=(kc == 7)))
                    free[blk % 2] = tm
                    blk += 1
            n = self.layers * 72
            ta = tr.op("dve", [tm, tv], lambda e: e.tensor_tensor(
                out=self.ada[:, 0:n], in0=psa[:, 0:n], in1=self.vecs[:, V_ADAB:V_ADAB + n], op=ALU.add))
            tlast = ta
            for l in range(self.layers):
                for sub in range(3):
                    base = l * 72 + sub * 24
                    rw = 1.0 if sub == 1 else 0.5
                    t1 = tr.op("dve", ta, lambda e: e.tensor_copy(out=self.msc[:, base:base + 8], in_=self.ada[:, base:base + 8]))
                    t2 = tr.op("dve", ta, lambda e: e.tensor_scalar(
                        out=self.msc[:, base + 8:base + 16], in0=self.ada[:, base + 8:base + 16],
                        scalar1=1.0, scalar2=None, op0=ALU.add))
                    tlast = tr.op("dve", ta, lambda e: e.tensor_scalar(
                        out=self.msc[:, base + 16:base + 24], in0=self.ada[:, base + 16:base + 24],
                        scalar1=1.0, scalar2=rw / ALPHA, op0=ALU.add, op1=ALU.mult))
            self.setup_tok = [tlast, t1, t2, tv]
            self.dump("ada", self.ada[:], [128, DEPTH * 72])
            self.dump("msc", self.msc[:], [128, DEPTH * 72])
            self.dump("vecs", self.vecs[:], [128, NVEC])
            tr.barrier()

    def dump(self, name, ap, shape, dt=F32):
        if not self.debug_outs:
            return
        d = self.nc.dram_tensor("dbg_" + name, list(shape), dt, kind="ExternalOutput").ap()
        self.tr.dma("sp", self.tr.all_tokens(), "dbg", d, ap)

    def epsc(self, eps):
        return self.epst[:, 0:1] if eps > 2e-6 else self.epst[:, 1:2]

    def run_phases(self):
        self.rope_setup()
        self.transpose_in()
        cur = 0
        for l in range(self.layers):
            self.ffn(l, 0, self.xT[cur], self.xT[1 - cur], hdst=self.hT)
            cur = 1 - cur
            if self.stop_after == ("ffn", l, 0):
                break
            self.attn(l)
            if self.stop_after == ("attn", l):
                break
            self.hgrn(l)
            if self.stop_after == ("hgrn", l):
                break
            self.ffn(l, 1, self.xT[cur], self.xT[1 - cur])
            cur = 1 - cur
            if self.stop_after == ("mix", l):
                break
            self.ffn(l, 2, self.xT[cur], self.xT[1 - cur])
            cur = 1 - cur
        self.transpose_out(self.xT[cur])

    def transpose_in(self):
        nc, tr, S = self.nc, self.tr, self.S
        with ExitStack() as es:
            xin = [self.sb(es, "ti_x%d" % i, [128, 4, D], F32) for i in range(2)]
            xo = [self.sb(es, "ti_o%d" % i, [128, NCH, 512], F32) for i in range(2)]
            pss = [self.ps(es, "ti_p%d" % i, [128, 512], F32) for i in range(4)]
            ps_free = [None] * 4
            xin_free = [None] * 2
            xo_free = [None] * 2
            nb = 0
            for it in range(S // 512):
                b = it % 2
                tl = tr.dma("sp", xin_free[b], "ti_l%d" % b, xin[b][:],
                            self.x_in[it * 512:(it + 1) * 512, :].rearrange("(j p) d -> p j d", p=128))
                evs = []
                for c in range(NCH):
                    pb = nb % 4
                    nb += 1
                    tp = None
                    for j in range(4):
                        tp = tr.op("pe", [tl, ps_free[pb], self.const_tok], lambda e: e.transpose(
                            pss[pb][:, j * 128:(j + 1) * 128], xin[b][:, j, c * 128:(c + 1) * 128], self.ident_f[:]))
                    eng = "act" if c % 2 == 0 else "dve"
                    if eng == "act":
                        te = tr.op("act", [tp, xo_free[b]], lambda e: e.copy(out=xo[b][:, c, :], in_=pss[pb][:]))
                    else:
                        te = tr.op("dve", [tp, xo_free[b]], lambda e: e.tensor_copy(out=xo[b][:, c, :], in_=pss[pb][:]))
                    ps_free[pb] = te
                    evs.append(te)
                xin_free[b] = tp
                ts = tr.dma("pool", evs, "ti_s%d" % b,
                            self.xT[0][:, it * 512:(it + 1) * 512].rearrange("(c p) t -> p c t", p=128), xo[b][:])
                xo_free[b] = ts
            tr.barrier()

    def transpose_out(self, src):
        nc, tr, S = self.nc, self.tr, self.S
        with ExitStack() as es:
            xin = [self.sb(es, "to_x%d" % i, [128, NCH, 512], F32) for i in range(2)]
            xo = [self.sb(es, "to_o%d" % i, [128, 4, D], F32) for i in range(2)]
            pss = [self.ps(es, "to_p%d" % i, [128, 512], F32) for i in range(4)]
            ps_free = [None] * 4
            xin_free = [None] * 2
            xo_free = [None] * 2
            nb = 0
            for it in range(S // 512):
                b = it % 2
                tl = tr.dma("sp", xin_free[b], "to_l%d" % b, xin[b][:],
                            src[:, it * 512:(it + 1) * 512].rearrange("(c p) t -> p c t", p=128))
                evs = []
                for j in range(4):
                    for h in range(2):
                        pb = nb % 4
                        nb += 1
                        tp = None
                        for cc in range(4):
                            c = h * 4 + cc
                            tp = tr.op("pe", [tl, ps_free[pb], self.const_tok], lambda e: e.transpose(
                                pss[pb][:, cc * 128:(cc + 1) * 128], xin[b][:, c, j * 128:(j + 1) * 128], self.ident_f[:]))
                        if (j + h) % 2 == 0:
                            te = tr.op("act", [tp, xo_free[b]], lambda e: e.copy(
                                out=xo[b][:, j, h * 512:(h + 1) * 512], in_=pss[pb][:]))
                        else:
                            te = tr.op("dve", [tp, xo_free[b]], lambda e: e.tensor_copy(
                                out=xo[b][:, j, h * 512:(h + 1) * 512], in_=pss[pb][:]))
                        ps_free[pb] = te
                        evs.append(te)
                xin_free[b] = tp
                ts = tr.dma("pool", evs, "to_s%d" % b,
                            self.out[it * 512:(it + 1) * 512, :].rearrange("(j p) d -> p j d", p=128), xo[b][:])
                xo_free[b] = ts
            tr.barrier()

    def rope_setup(self):
        nc, tr, S = self.nc, self.tr, self.S
        CW = min(2048, S)
        PI = math.pi
        with ExitStack() as es:
            posi = Buf(self.sb(es, "r_pi", [128, CW], I32))
            ang = Buf(self.sb(es, "r_ang", [128, CW], F32))
            m1 = Buf(self.sb(es, "r_m1", [128, CW], F32))
            m2 = Buf(self.sb(es, "r_m2", [128, CW], F32))
            cK = Buf(None)
            cK.w = {t[0]: t for t in self.const_tok}
            invf = self.cst[:, 128:129]
            halfpi = self.cst[:, 129:130]
            for i in range(S // CW):
                a, b = i * CW, (i + 1) * CW
                dmab(tr, "sp", "r_l", [], [posi], posi.t[:], self.pos_in[0:1, a:b].partition_broadcast(128))
                opb(tr, "dve", [posi], [ang], lambda e: e.tensor_copy(out=ang.t[:], in_=posi.t[:]))
                opb(tr, "dve", [cK], [ang], lambda e: e.tensor_scalar(
                    out=ang.t[:], in0=ang.t[:], scalar1=invf, scalar2=None, op0=ALU.mult))
                C1 = 6.28125
                C2 = 2 * PI - C1
                opb(tr, "dve", [ang], [m1], lambda e: e.tensor_scalar(
                    out=m1.t[:], in0=ang.t[:], scalar1=1.0 / (2 * PI), scalar2=None, op0=ALU.mult))
                opb(tr, "dve", [m1], [posi], lambda e: e.tensor_copy(out=posi.t[:], in_=m1.t[:]))
                opb(tr, "dve", [posi], [m1], lambda e: e.tensor_copy(out=m1.t[:], in_=posi.t[:]))
                opb(tr, "dve", [m1], [ang], lambda e: e.scalar_tensor_tensor(
                    out=ang.t[:], in0=m1.t[:], scalar=-C1, in1=ang.t[:], op0=ALU.mult, op1=ALU.add))
                opb(tr, "dve", [m1], [ang], lambda e: e.scalar_tensor_tensor(
                    out=ang.t[:], in0=m1.t[:], scalar=-C2, in1=ang.t[:], op0=ALU.mult, op1=ALU.add))
                opb(tr, "dve", [], [ang], lambda e: e.tensor_scalar(
                    out=ang.t[:], in0=ang.t[:], scalar1=-PI, scalar2=PI, op0=ALU.max, op1=ALU.min))
                opb(tr, "dve", [ang], [m2], lambda e: e.scalar_tensor_tensor(
                    out=m2.t[:], in0=ang.t[:], scalar=-1.0, in1=ang.t[:], op0=ALU.mult, op1=ALU.max))
                opb(tr, "act", [ang], [m1], lambda e: e.activation(out=m1.t[:], in_=ang.t[:], func=AF.Sin))
                opb(tr, "act", [cK], [m2], lambda e: e.activation(out=m2.t[:], in_=m2.t[:], func=AF.Sin, bias=halfpi, scale=-1.0))
                dmab(tr, "sp", "r_s1", [m1], [], self.sinT[:, a:b], m1.t[:])
                dmab(tr, "sp", "r_s2", [m2], [], self.cosT[:, a:b], m2.t[:])
            t1 = tr.op("pool", self.setup_tok, lambda e: e.memset(self.lbv[:], 0.0))
            t2 = tr.op("dve", [t1] + self.setup_tok, lambda e: e.tensor_tensor(
                out=self.lbv[:, 4:8], in0=self.vecs[:, V_LB + 4:V_LB + 8], in1=self.vecs[:, V_LB:V_LB + 4], op=ALU.subtract))
            t3 = tr.op("act", t2, lambda e: e.activation(out=self.lbv[:, 4:8], in_=self.lbv[:, 4:8], func=AF.Sigmoid))
            t4 = tr.op("dve", t3, lambda e: e.tensor_scalar(
                out=self.oml[:], in0=self.lbv[:], scalar1=-1.0, scalar2=1.0, op0=ALU.mult, op1=ALU.add))
            self.lb_tok = [t3, t4]
            tr.barrier()

    def attn(self, l):
        nc, tr, S = self.nc, self.tr, self.S
        NSEG = S // 2048
        with ExitStack() as es:
            B = lambda name, shape, dt: Buf(self.sb(es, name, shape, dt))
            P = lambda name, shape, dt=F32: Buf(self.ps(es, name, shape, dt))
            cK = Buf(None)
            cK.w = {t[0]: t for t in self.const_tok}
            wq = B("a_wq", [128, NCH, 640], BF16)
            ht = [B("a_ht%d" % i, [128, NCH, 512], BF16) for i in range(2)]
            cs = [B("a_cs%d" % i, [128, 512], F32) for i in range(2)]
            sn = [B("a_sn%d" % i, [128, 512], F32) for i in range(2)]
            t1b = [B("a_t1%d" % i, [128, 512], F32) for i in range(2)]
            t2b = [B("a_t2%d" % i, [128, 512], F32) for i in range(2)]
            qs = [[B("a_q%d%d" % (p, o), [128, 2048], BF16) for o in range(3)] for p in range(2)]
            ks = [[B("a_k%d%d" % (p, o), [128, 2048], BF16) for o in range(3)] for p in range(2)]
            vt = [B("a_vt%d" % o, [128, 2048], BF16) for o in range(3)]
            V = [[B("a_V%d%d" % (p, o), [128, 16, 128], BF16) for o in range(3)] for p in range(2)]
            pT = [B("a_pT%d" % i, [128, 512], BF16) for i in range(4)]
            accN = B("a_accN", [128, 2048], F32)
            accD = B("a_accD", [128, 2048], F32)
            ao = [B("a_ao%d" % i, [128, 2048], BF16) for i in range(2)]
            masks = [B("a_mk%d" % i, [128, 512], BF16) for i in range(3)]
            psp = [P("a_pp%d" % i, [128, 512]) for i in range(2)]
            pss = [P("a_ps%d" % i, [128, 512]) for i in range(4)]
            psn = [P("a_pn%d" % i, [128, 512]) for i in range(2)]
            pst = psp[0]
            pstT = psp[0].t.bitcast(BF16)
            mP, mC, mN = self.cst[:, 256:384], self.cst[:, 384:512], self.cst[:, 512:640]
            for mi, pat in enumerate([(mP, mC, mP, mC), (mN, mC, mP, mC), (mN, mC, mN, mC)]):
                for q_, src_ in enumerate(pat):
                    opb(tr, "dve", [cK], [masks[mi]], lambda e: e.tensor_copy(
                        out=masks[mi].t[:, q_ * 128:(q_ + 1) * 128], in_=src_))
            cnt = {"pp": 0, "ps": 0, "pn": 0, "pT": 0, "tile": 0, "ao": 0}

            def nxt(k, n):
                v = cnt[k] % n
                cnt[k] += 1
                return v

            prev_done = {}

            def phaseA(c, s_, gate=None):
                par = s_ % 2
                if s_ == 0:
                    for g, col0 in ((0, c * 128), (2, 512 + c * 128), (4, 1024 + c * 128)):
                        dmab(tr, "pool", "a_w%d" % g, [], [wq], wq.t[:, :, g * 128:(g + 1) * 128],
                             self.w_mix_in[l, :, col0:col0 + 128].rearrange("(k p) n -> p k n", p=128))
                    for g in (0, 2):
                        opb(tr, "pool", [], [wq], lambda e: e.memset(wq.t[:, :, (g + 1) * 128:(g + 2) * 128], 0.0))
                        for hh in range(2):
                            so = g * 128 + hh * 64
                            do = (g + 1) * 128 + hh * 64
                            opb(tr, "dve", [], [wq], lambda e: e.tensor_scalar(
                                out=wq.t[:, :, do:do + 8], in0=wq.t[:, :, so + 8:so + 16], scalar1=-1.0, scalar2=None, op0=ALU.mult))
                            opb(tr, "dve", [], [wq], lambda e: e.tensor_copy(
                                out=wq.t[:, :, do + 8:do + 16], in_=wq.t[:, :, so:so + 8]))
                    yield
                for jl in range(4):
                    t0 = s_ * 2048 + jl * 512
                    hb = nxt("tile", 2)
                    dmab(tr, "sp", "a_lh%d" % hb, [], [ht[hb]], ht[hb].t[:],
                         self.hT[:, t0:t0 + 512].rearrange("(k p) t -> p k t", p=128))
                    dmab(tr, "sp", "a_lc%d" % hb, [], [cs[hb]], cs[hb].t[:], self.cosT[:, t0:t0 + 512])
                    dmab(tr, "sp", "a_ls%d" % hb, [], [sn[hb]], sn[hb].t[:], self.sinT[:, t0:t0 + 512])
                    nat = slice(jl * 512, (jl + 1) * 512)
                    for gi, (g, dest) in enumerate(((0, qs[par]), (2, ks[par]))):
                        while gi == 1 and gate is not None and not prev_done.get(gate, False):
                            yield
                        pa = psp[nxt("pp", 2)]
                        for kc in range(NCH):
                            opb(tr, "pe", [wq, ht[hb]], [pa], lambda e: e.matmul(
                                pa.t[:], lhsT=wq.t[:, kc, g * 128:(g + 1) * 128], rhs=ht[hb].t[:, kc, :],
                                start=(kc == 0), stop=(kc == NCH - 1)))
                        opb(tr, "dve", [pa, cs[hb]], [t1b[gi]], lambda e: e.tensor_tensor(
                            out=t1b[gi].t[:], in0=pa.t[:], in1=cs[hb].t[:], op=ALU.mult))
                        pb = psp[nxt("pp", 2)]
                        for kc in range(NCH):
                            opb(tr, "pe", [wq, ht[hb]], [pb], lambda e: e.matmul(
                                pb.t[:], lhsT=wq.t[:, kc, (g + 1) * 128:(g + 2) * 128], rhs=ht[hb].t[:, kc, :],
                                start=(kc == 0), stop=(kc == NCH - 1)))
                        opb(tr, "dve", [pb, sn[hb]], [t2b[gi]], lambda e: e.tensor_tensor(
                            out=t2b[gi].t[:], in0=pb.t[:], in1=sn[hb].t[:], op=ALU.mult))
                        opb(tr, "pool", [t1b[gi], t2b[gi]], [dest[0]], lambda e: e.tensor_tensor(
                            out=dest[0].t[:, nat], in0=t1b[gi].t[:], in1=t2b[gi].t[:], op=ALU.add))
                        opb(tr, "pool", [dest[0]], [dest[1]], lambda e: e.tensor_copy(
                            out=dest[1].t[:].rearrange("p (r n i) -> p r n i", r=4, n=4)[:, :, jl, :],
                            in_=dest[0].t[:, nat].rearrange("p (i r) -> p r i", r=4)))
                        opb(tr, "act", [dest[0]], [dest[2]], lambda e: e.copy(
                            out=dest[2].t[:].rearrange("p (r j i) -> p r j i", r=16, j=4)[:, :, jl, :],
                            in_=dest[0].t[:, nat].rearrange("p (i r) -> p r i", r=16)))
                        yield
                    pv = psp[nxt("pp", 2)]
                    for kc in range(NCH):
                        opb(tr, "pe", [wq, ht[hb]], [pv], lambda e: e.matmul(
                            pv.t[:], lhsT=wq.t[:, kc, 512:640], rhs=ht[hb].t[:, kc, :],
                            start=(kc == 0), stop=(kc == NCH - 1)))
                    opb(tr, "act", [pv], [vt[0]], lambda e: e.copy(out=vt[0].t[:, nat], in_=pv.t[:]))
                    opb(tr, "pool", [vt[0]], [vt[1]], lambda e: e.tensor_copy(
                        out=vt[1].t[:].rearrange("p (r n i) -> p r n i", r=4, n=4)[:, :, jl, :],
                        in_=vt[0].t[:, nat].rearrange("p (i r) -> p r i", r=4)))
                    opb(tr, "dve", [vt[0]], [vt[2]], lambda e: e.tensor_copy(
                        out=vt[2].t[:].rearrange("p (r j i) -> p r j i", r=16, j=4)[:, :, jl, :],
                        in_=vt[0].t[:, nat].rearrange("p (i r) -> p r i", r=16)))
                    yield
                for o in range(3):
                    for half in range(2):
                        for k_ in range(8):
                            blk = half * 8 + k_
                            opb(tr, "pe", [vt[o], cK], [pst], lambda e: e.transpose(
                                pstT[:, k_ * 128:(k_ + 1) * 128], vt[o].t[:, blk * 128:(blk + 1) * 128], self.ident_b[:]))
                        opb(tr, "act" if half == 0 else "dve", [pst], [V[par][o]],
                            (lambda e: e.copy(out=V[par][o].t[:, half * 8:(half + 1) * 8, :],
                                              in_=pstT[:].rearrange("p (k n) -> p k n", k=8))) if half == 0 else
                            (lambda e: e.tensor_copy(out=V[par][o].t[:, half * 8:(half + 1) * 8, :],
                                                     in_=pstT[:].rearrange("p (k n) -> p k n", k=8))))
                        yield

            def phaseB(c, s_):
                par = s_ % 2
                order = ([(2, p_) for p_ in range(8)] + [(1, p_) for p_ in (0, 2, 4, 6)] + [(0, 0)]
                         + [(1, p_) for p_ in (1, 3, 5, 7)] + [(0, p_) for p_ in range(1, 8)])

                def scores(o, lbp):
                    prevs = []
                    for bb in range(2):
                        lb = lbp * 2 + bb
                        if o == 0:
                            pv_ = (par, lb - 1) if lb > 0 else ((1 - par, 15) if s_ > 0 else None)
                        elif o == 1:
                            pv_ = (par, lb - 1) if lb % 4 > 0 else ((1 - par, lb + 3) if s_ > 0 else None)
                        else:
                            pv_ = (1 - par, lb) if s_ > 0 else None
                        prevs.append(pv_)
                    mk = masks[0] if prevs[0] is not None else (masks[1] if prevs[1] is not None else masks[2])
                    sps = [pss[nxt("ps", 4)] for _ in range(2)]
                    for hh in range(2):
                        opb(tr, "pe", [mk, cK], [sps[hh]], lambda e: e.matmul(
                            sps[hh].t[:], lhsT=self.ident_b[:], rhs=mk.t[:], start=True, stop=False))
                    for bb in range(2):
                        lb = lbp * 2 + bb
                        last = (bb == 1)
                        if prevs[bb] is not None:
                            pp, pl = prevs[bb]
                            for hh in range(2):
                                rows = slice(hh * 64, (hh + 1) * 64)
                                opb(tr, "pe", [ks[pp][o], qs[par][o]], [sps[hh]], lambda e: e.matmul(
                                    sps[hh].t[:, bb * 256:bb * 256 + 128], lhsT=ks[pp][o].t[rows, pl * 128:(pl + 1) * 128],
                                    rhs=qs[par][o].t[rows, lb * 128:(lb + 1) * 128], start=False, stop=False))
                        for hh in range(2):
                            rows = slice(hh * 64, (hh + 1) * 64)
                            opb(tr, "pe", [ks[par][o], qs[par][o]], [sps[hh]], lambda e: e.matmul(
                                sps[hh].t[:, bb * 256 + 128:bb * 256 + 256], lhsT=ks[par][o].t[rows, lb * 128:(lb + 1) * 128],
                                rhs=qs[par][o].t[rows, lb * 128:(lb + 1) * 128], start=False, stop=last))
                    pts = []
                    for hh in range(2):
                        pt = pT[nxt("pT", 4)]
                        opb(tr, "act", [sps[hh]], [pt], lambda e: e.activation(
                            out=pt.t[:], in_=sps[hh].t[:], func=AF.Exp, scale=0.125))
                        pts.append(pt)
                    return prevs, pts

                def pv_acc(o, lbp, prevs, pts):
                    nd = psn[nxt("pn", 2)]
                    for bb in range(2):
                        lb = lbp * 2 + bb
                        srcs = []
                        if prevs[bb] is not None:
                            pp, pl = prevs[bb]
                            srcs.append((V[pp][o], pl, bb * 256))
                        srcs.append((V[par][o], lb, bb * 256 + 128))
                        for kind in range(2):
                            for si, (vb, blk, pc) in enumerate(srcs):
                                for hh in range(2):
                                    rows = slice(hh * 64, (hh + 1) * 64)
                                    pt = pts[hh]
                                    lhs = vb.t[:, blk, rows] if kind == 0 else self.ones_b[:, 0:64]
                                    opb(tr, "pe", [vb, pt, cK], [nd], lambda e: e.matmul(
                                        nd.t[rows, kind * 256 + bb * 128:kind * 256 + (bb + 1) * 128],
                                        lhsT=lhs, rhs=pt.t[:, pc:pc + 128],
                                        start=(si == 0), stop=(si == len(srcs) - 1)))
                    for kind, acc in ((0, accN), (1, accD)):
                        if o == 0:
                            av = acc.t[:, lbp * 256:(lbp + 1) * 256].rearrange("p (b i) -> p b i", b=2)
                        elif o == 1:
                            r_, n0 = lbp // 2, (lbp % 2) * 2
                            av = acc.t[:].rearrange("p (n i r) -> p r n i", n=4, r=4)[:, r_, n0:n0 + 2, :]
                        else:
                            av = acc.t[:].rearrange("p (i r) -> p r i", r=16)[:, lbp * 2:lbp * 2 + 2, :]
                        sv = nd.t[:, kind * 256:(kind + 1) * 256].rearrange("p (b i) -> p b i", b=2)
                        if o == 2:
                            opb(tr, "dve", [nd], [acc], lambda e: e.tensor_copy(out=av, in_=sv))
                        else:
                            opb(tr, "dve", [nd], [acc], lambda e: e.tensor_tensor(out=av, in0=av, in1=sv, op=ALU.add))

                ctx = scores(*order[0])
                for oi_, (o, lbp) in enumerate(order):
                    nctx = scores(*order[oi_ + 1]) if oi_ + 1 < len(order) else None
                    pv_acc(o, lbp, *ctx)
                    ctx = nctx
                    if oi_ == 12:
                        prev_done[(c, s_)] = True
                    yield
                ab = ao[nxt("ao", 2)]
                opb(tr, "act", [], [accD], lambda e: e.activation(out=accD.t[:], in_=accD.t[:], func=AF.Ln))
                opb(tr, "act", [], [accD], lambda e: e.activation(out=accD.t[:], in_=accD.t[:], func=AF.Exp, scale=-1.0))
                opb(tr, "pool", [accN, accD], [ab], lambda e: e.tensor_tensor(
                    out=ab.t[:], in0=accN.t[:], in1=accD.t[:], op=ALU.mult))
                dmab(tr, "sp", "a_so%d" % (cnt["ao"] % 2), [ab], [],
                     self.mixT[c * 128:(c + 1) * 128, s_ * 2048:(s_ + 1) * 2048], ab.t[:])
                yield

            items = [(c, s_) for c in range(4) for s_ in range(NSEG)]
            interleave([phaseA(*items[0])])
            for i_, it_ in enumerate(items):
                gens = [phaseB(*it_)]
                if i_ + 1 < len(items):
                    gens.append(phaseA(*items[i_ + 1], gate=it_))
                interleave(gens)
            tr.barrier()

    def hgrn(self, l):
        nc, tr, S = self.nc, self.tr, self.S
        NT = S // 512
        with ExitStack() as es:
            B = lambda name, shape, dt: Buf(self.sb(es, name, shape, dt))
            P = lambda name, shape, dt=F32: Buf(self.ps(es, name, shape, dt))
            cK = Buf(None)
            cK.w = {t[0]: t for t in self.const_tok + self.lb_tok + self.setup_tok}
            wh = B("h_w", [128, NCH, 2048], BF16)
            ht = [B("h_ht%d" % i, [128, NCH, 512], BF16) for i in range(2)]
            qsl = [B("h_qs%d" % i, [128, 512], F32) for i in range(2)]
            fb = [B("h_f%d" % i, [128, 512], F32) for i in range(2)]
            ab_ = [B("h_a%d" % i, [128, 512], F32) for i in range(2)]
            kk = [B("h_kk%d" % i, [128, 512], F32) for i in range(2)]
            bc = [B("h_b%d" % i, [128, 512], F32) for i in range(2)]
            e1 = [B("h_e1%d" % i, [128, 512], F32) for i in range(2)]
            e2 = [B("h_e2%d" % i, [128, 512], F32) for i in range(2)]
            kh = [B("h_kh%d" % i, [128, 512], BF16) for i in range(2)]
            QT = [[B("h_QT%d%d" % (h, i), [128, 512], BF16) for i in range(2)] for h in range(4)]
            KT = [[B("h_KT%d%d" % (h, i), [128, 512], BF16) for i in range(2)] for h in range(4)]
            KK = [[B("h_KK%d%d" % (h, i), [64, 8, 128], BF16) for i in range(2)] for h in range(4)]
            SG = [[B("h_SG%d%d" % (h, i), [128, 512], BF16) for i in range(2)] for h in range(4)]
            DEC = [[B("h_DC%d%d" % (h, i), [128, 8], F32) for i in range(2)] for h in range(4)]
            VV = [B("h_VV%d" % i, [64, 8, 512], BF16) for i in range(2)]
            ST = [B("h_ST%d" % h, [128, 128], F32) for h in range(4)]
            STb = [B("h_STb%d" % h, [128, 128], BF16) for h in range(4)]
            AM = [B("h_AM%d" % i, [64, 256], BF16) for i in range(2)]
            O32 = [B("h_O%d" % i, [128, 4, 512], F32) for i in range(2)]
            osq = B("h_osq", [128, 512], BF16)
            rt = B("h_rt", [128, 512], F32)
            g1 = B("h_g1", [128, 512], F32)
            GO = [B("h_GO%d" % i, [128, 512], BF16) for i in range(2)]
            tri4 = B("h_tri", [64, 256], BF16)
            rmask = B("h_rm", [128, 512], F32)
            psp = [P("h_pp%d" % i, [128, 512]) for i in range(2)]
            psA = P("h_pA", [128, 512])
            pso = [P("h_po%d" % i, [128, 512]) for i in range(2)]
            psS = [P("h_pS%d" % i, [128, 512]) for i in range(2)]
            pst = psp[0]
            pstT = psp[0].t.bitcast(BF16)
            psm = P("h_pm", [128, 512])
            cnt = {"pp": 0, "po": 0, "am": 0, "t": 0, "go": 0}

            def nxt(k, n):
                v = cnt[k] % n
                cnt[k] += 1
                return v

            for h in range(4):
                opb(tr, "dve", [cK], [tri4], lambda e: e.tensor_copy(
                    out=tri4.t[:, h * 64:(h + 1) * 64], in_=self.cst[0:64, 640:704]))
            opb(tr, "pool", [], [rmask], lambda e: e.memset(rmask.t[:], 1.0))
            opb(tr, "pool", [], [rmask], lambda e: e.memset(
                rmask.t[:].rearrange("p (n t) -> p n t", t=64)[:, :, 0:1], 0.0))
            for h in range(4):
                opb(tr, "pool", [], [ST[h]], lambda e: e.memset(ST[h].t[:], 0.0))
                opb(tr, "pool", [], [STb[h]], lambda e: e.memset(STb[h].t[:], 0.0))
            for kc in range(NCH):
                dmab(tr, "pool", "h_w%d" % kc, [], [wh], wh.t[:, kc, :],
                     self.w_mix_in[l, kc * 128:(kc + 1) * 128, 1536:3584])
            lbc = lambda h: self.lbv[:, l * 4 + h:l * 4 + h + 1]
            omc = lambda h: self.oml[:, l * 4 + h:l * 4 + h + 1]
            nwc = lambda h: self.vecs[:, V_NW + l * 4 + h:V_NW + l * 4 + h + 1]

            def proj(hb, col0, dst):
                for kc in range(NCH):
                    opb(tr, "pe", [wh, ht[hb]], [dst], lambda e: e.matmul(
                        dst.t[:], lhsT=wh.t[:, kc, col0:col0 + 128], rhs=ht[hb].t[:, kc, :],
                        start=(kc == 0), stop=(kc == NCH - 1)))

            def hA(j):
                t0 = j * 512
                tb = j % 2
                dmab(tr, "sp", "h_lh%d" % tb, [], [ht[tb]], ht[tb].t[:],
                     self.hT[:, t0:t0 + 512].rearrange("(k p) t -> p k t", p=128))
                for h in range(4):
                    i2 = h % 2
                    pq = psp[nxt("pp", 2)]
                    proj(tb, h * 128, pq)
                    opb(tr, "act", [pq], [qsl[i2]], lambda e: e.activation(out=qsl[i2].t[:], in_=pq.t[:], func=AF.Silu))
                    pg = psp[nxt("pp", 2)]
                    proj(tb, 1536 + h * 128, pg)
                    opb(tr, "act", [pg], [SG[h][tb]], lambda e: e.activation(out=SG[h][tb].t[:], in_=pg.t[:], func=AF.Silu))
                    pf = psp[nxt("pp", 2)]
                    proj(tb, 512 + h * 128, pf)
                    opb(tr, "act", [pf], [fb[i2]], lambda e: e.activation(out=fb[i2].t[:], in_=pf.t[:], func=AF.Sigmoid))
                    opb(tr, "dve", [cK], [fb[i2]], lambda e: e.tensor_scalar(
                        out=fb[i2].t[:], in0=fb[i2].t[:], scalar1=omc(h), scalar2=lbc(h), op0=ALU.mult, op1=ALU.add))
                    opb(tr, "act", [fb[i2]], [ab_[i2]], lambda e: e.activation(out=ab_[i2].t[:], in_=fb[i2].t[:], func=AF.Ln))
                    opb(tr, "pool", [fb[i2]], [kk[i2]], lambda e: e.tensor_scalar(
                        out=kk[i2].t[:], in0=fb[i2].t[:], scalar1=-1.0, scalar2=1.0, op0=ALU.mult, op1=ALU.add))
                    opb(tr, "dve", [rmask, ab_[i2]], [bc[i2]], lambda e: e.tensor_tensor_scan(
                        out=bc[i2].t[:], data0=rmask.t[:], data1=ab_[i2].t[:], initial=0.0, op0=ALU.mult, op1=ALU.add))
                    opb(tr, "act", [bc[i2]], [e1[i2]], lambda e: e.activation(out=e1[i2].t[:], in_=bc[i2].t[:], func=AF.Exp))
                    opb(tr, "act", [bc[i2]], [e2[i2]], lambda e: e.activation(out=e2[i2].t[:], in_=bc[i2].t[:], func=AF.Exp, scale=-1.0))
                    opb(tr, "pool", [qsl[i2], e1[i2]], [QT[h][tb]], lambda e: e.tensor_tensor(
                        out=QT[h][tb].t[:], in0=qsl[i2].t[:], in1=e1[i2].t[:], op=ALU.mult))
                    opb(tr, "dve", [kk[i2], e2[i2]], [KT[h][tb]], lambda e: e.tensor_tensor(
                        out=KT[h][tb].t[:], in0=kk[i2].t[:], in1=e2[i2].t[:], op=ALU.mult))
                    e1v = e1[i2].t[:].rearrange("p (n t) -> p n t", t=64)
                    opb(tr, "dve", [e1[i2]], [DEC[h][tb]], lambda e: e.tensor_copy(
                        out=DEC[h][tb].t[:].rearrange("p (n o) -> p n o", o=1), in_=e1v[:, :, 63:64]))
                    opb(tr, "pool", [KT[h][tb], e1[i2]], [kh[i2]], lambda e: e.tensor_tensor(
                        out=kh[i2].t[:].rearrange("p (n t) -> p n t", t=64),
                        in0=KT[h][tb].t[:].rearrange("p (n t) -> p n t", t=64),
                        in1=e1v[:, :, 63:64].to_broadcast([128, 8, 64]), op=ALU.mult))
                    for ch in range(8):
                        opb(tr, "pe", [kh[i2], cK], [pst], lambda e: e.transpose(
                            pstT[0:64, ch * 128:(ch + 1) * 128], kh[i2].t[:, ch * 64:(ch + 1) * 64], self.ident_b[:]))
                    opb(tr, "act", [pst], [KK[h][tb]], lambda e: e.copy(
                        out=KK[h][tb].t[:], in_=pstT[0:64, :].rearrange("p (k n) -> p k n", k=8)))
                    yield
                for ch in range(8):
                    pv = psp[nxt("pp", 2)]
                    for kc in range(NCH):
                        opb(tr, "pe", [wh, ht[tb]], [pv], lambda e: e.matmul(
                            pv.t[0:64, :], lhsT=ht[tb].t[:, kc, ch * 64:(ch + 1) * 64], rhs=wh.t[:, kc, 1024:1536],
                            start=(kc == 0), stop=(kc == NCH - 1)))
                    if ch % 2 == 0:
                        opb(tr, "act", [pv], [VV[tb]], lambda e: e.copy(out=VV[tb].t[:, ch, :], in_=pv.t[0:64, :]))
                    else:
                        opb(tr, "dve", [pv], [VV[tb]], lambda e: e.tensor_copy(out=VV[tb].t[:, ch, :], in_=pv.t[0:64, :]))
                        yield

            def hB(j):
                t0 = j * 512
                tb = j % 2
                ob = O32[tb]

                def mmA(ch):
                    cols = slice(ch * 64, (ch + 1) * 64)
                    for h in range(4):
                        opb(tr, "pe", [KT[h][tb], QT[h][tb]], [psA], lambda e: e.matmul(
                            psA.t[0:64, h * 64:(h + 1) * 64], lhsT=KT[h][tb].t[:, cols], rhs=QT[h][tb].t[:, cols],
                            start=True, stop=True))
                    am = AM[nxt("am", 2)]
                    opb(tr, "dve", [psA, tri4], [am], lambda e: e.tensor_tensor(
                        out=am.t[:], in0=psA.t[0:64, 0:256], in1=tri4.t[:], op=ALU.mult))
                    return am

                def mmS(ch):
                    pS = psS[ch % 2]
                    for h in range(4):
                        opb(tr, "pe", [KK[h][tb], VV[tb]], [pS], lambda e: e.matmul(
                            pS.t[:, h * 128:(h + 1) * 128], lhsT=KK[h][tb].t[:, ch, :], rhs=VV[tb].t[:, ch, h * 128:(h + 1) * 128],
                            start=True, stop=True))

                def mmO(ch, am):
                    cols = slice(ch * 64, (ch + 1) * 64)
                    po = pso[nxt("po", 2)]
                    for h in range(4):
                        opb(tr, "pe", [VV[tb], am], [po], lambda e: e.matmul(
                            po.t[:, h * 64:(h + 1) * 64], lhsT=VV[tb].t[:, ch, h * 128:(h + 1) * 128], rhs=am.t[:, h * 64:(h + 1) * 64],
                            start=True, stop=False))
                        opb(tr, "pe", [STb[h], QT[h][tb]], [po], lambda e: e.matmul(
                            po.t[:, h * 64:(h + 1) * 64], lhsT=STb[h].t[:], rhs=QT[h][tb].t[:, cols],
                            start=False, stop=True))
                    opb(tr, "act", [po], [ob], lambda e: e.copy(
                        out=ob.t[:, :, cols], in_=po.t[:, 0:256].rearrange("p (h t) -> p h t", h=4)))

                def upd(ch):
                    pS = psS[ch % 2]
                    for h in range(4):
                        opb(tr, "dve", [pS, DEC[h][tb]], [ST[h]], lambda e: e.scalar_tensor_tensor(
                            out=ST[h].t[:], in0=ST[h].t[:], scalar=DEC[h][tb].t[:, ch:ch + 1], in1=pS.t[:, h * 128:(h + 1) * 128],
                            op0=ALU.mult, op1=ALU.add))
                        opb(tr, "act", [ST[h]], [STb[h]], lambda e: e.copy(out=STb[h].t[:], in_=ST[h].t[:]))

                am = mmA(0)
                mmS(0)
                for ch in range(8):
                    am_n = mmA(ch + 1) if ch + 1 < 8 else None
                    mmO(ch, am)
                    upd(ch)
                    if ch + 1 < 8:
                        mmS(ch + 1)
                    am = am_n
                    yield
                for h in range(4):
                    opb(tr, "act", [ob], [osq], lambda e: e.activation(out=osq.t[:], in_=ob.t[:, h, :], func=AF.Square))
                    opb(tr, "pe", [osq, cK], [psm], lambda e: e.matmul(
                        psm.t[:], lhsT=self.ones_b[:], rhs=osq.t[:], start=True, stop=True))
                    opb(tr, "act", [psm, cK], [rt], lambda e: e.activation(
                        out=rt.t[:], in_=psm.t[:], func=AF.Ln, bias=self.epsc(RMS_EPS), scale=1.0 / 128))
                    opb(tr, "act", [], [rt], lambda e: e.activation(out=rt.t[:], in_=rt.t[:], func=AF.Exp, scale=-0.5))
                    opb(tr, "pool", [ob, rt], [g1], lambda e: e.tensor_tensor(
                        out=g1.t[:], in0=ob.t[:, h, :], in1=rt.t[:], op=ALU.mult))
                    go = GO[nxt("go", 2)]
                    opb(tr, "dve", [g1, SG[h][tb], cK], [go], lambda e: e.scalar_tensor_tensor(
                        out=go.t[:], in0=g1.t[:], scalar=nwc(h), in1=SG[h][tb].t[:], op0=ALU.mult, op1=ALU.mult))
                    dmab(tr, "sp", "h_so%d" % (cnt["go"] % 2), [go], [],
                         self.mixT[512 + h * 128:512 + (h + 1) * 128, t0:t0 + 512], go.t[:])
                    yield

            interleave([hA(0)])
            for j in range(NT):
                gens = [hB(j)]
                if j + 1 < NT:
                    gens.append(hA(j + 1))
                interleave(gens)
            tr.barrier()

    def ffn(self, l, sub, src, dst, hdst=None):
        nc, tr, S = self.nc, self.tr, self.S
        T = 256
        is_ffn = sub != 1
        which = 0 if sub == 0 else 1
        base = l * 72 + sub * 24
        sh = lambda c: self.msc[:, base + c:base + c + 1]
        sc = lambda c: self.msc[:, base + 8 + c:base + 9 + c]
        gf = lambda c: self.msc[:, base + 16 + c:base + 17 + c]
        nb_ = l * 72 + (sub + 1) * 24
        sh2 = lambda c: self.msc[:, nb_ + c:nb_ + c + 1]
        sc2 = lambda c: self.msc[:, nb_ + 8 + c:nb_ + 9 + c]
        gcol = lambda c: self.vecs[:, V_LNG + (l * 3 + sub) * 8 + c:V_LNG + (l * 3 + sub) * 8 + c + 1]
        bcol = lambda c: self.vecs[:, V_LNB + (l * 3 + sub) * 8 + c:V_LNB + (l * 3 + sub) * 8 + c + 1]
        eps = LN_EPS / (ALPHA * ALPHA)
        nh = NHC if is_ffn else NCH
        with ExitStack() as es:
            if is_ffn:
                w_in = self.sb(es, "w_in", [128, NCH, 2 * DFF], BF16)
                ht = [self.sb(es, "f_h%d" % i, [128, NCH, T], BF16) for i in range(1)]
                sg = [self.sb(es, "f_sg%d" % i, [128, T], F32) for i in range(2)]
                psg = [self.ps(es, "f_pg%d" % i, [128, 512], F32) for i in range(4)]
            w_out = self.sb(es, "w_out", [128, nh, D], BF16)
            xt = [self.sb(es, "f_x%d" % i, [128, NCH, T], F32) for i in range(2)]
            hid = [self.sb(es, "f_hid%d" % i, [128, nh, T], BF16) for i in range(1 if is_ffn else 2)]
            zb = self.sb(es, "f_zb", [128, NCH, T], BF16)
            zq = self.sb(es, "f_zq", [128, NCH, T], BF16)
            st = self.sb(es, "f_st", [128, 4, T], F32)
            if hdst is not None:
                hb = [self.sb(es, "f_hb%d" % i, [128, NCH, T], BF16) for i in range(1)]
            psy = [self.ps(es, "f_py%d" % i, [128, 512], F32) for i in range(3)]
            pss = self.ps(es, "f_pss", [128, 512], F32)
            wtok = []
            wtok2 = []
            if is_ffn:
                for c in range(NCH):
                    wtok.append(tr.dma("pool", None, "w_a%d" % c, w_in[:, c, :],
                                       self.w_ffn_in[which][l, c * 128:(c + 1) * 128, :]))
                for m0 in range(0, NHC, 2):
                    wtok2.append(tr.dma("pool", None, "w_b%d" % (m0 // 2), w_out[:, m0:m0 + 2, :],
                                        self.w_ffn_out[which][l, m0 * 128:(m0 + 2) * 128, :].rearrange("(m p) d -> p m d", p=128)))
            else:
                for m0 in range(0, NCH, 2):
                    wtok2.append(tr.dma("pool", None, "w_b%d" % (m0 // 2), w_out[:, m0:m0 + 2, :],
                                        self.w_mix_out[l, m0 * 128:(m0 + 2) * 128, :].rearrange("(m p) d -> p m d", p=128)))
            x_free = [None, None]
            h_free = [None]
            hid_free = [None, None]
            hb_free = [None, None]
            sg_free = [None, None]
            psg_free = [None] * 4
            psy_free = [None] * 3
            pss_free = None
            zb_free = None
            st_free = None
            ng = 0
            ny = 0
            nsg = 0
            nhb = 1 if is_ffn else 2

            def load_h(it):
                b = it % 2
                t0 = it * T
                tl = tr.dma("sp", x_free[b], "f_l%d" % b, xt[b][:],
                            src[:, t0:t0 + T].rearrange("(c p) t -> p c t", p=128))
                th = []
                if is_ffn:
                    for c in range(NCH):
                        th.append(tr.op("act", [tl, h_free[0], self.setup_tok], lambda e: e.activation(
                            out=ht[0][:, c, :], in_=xt[b][:, c, :], func=AF.Identity, bias=sh(c), scale=sc(c))))
                else:
                    hb_ = it % nhb
                    th = tr.dma("sp", hid_free[hb_], "f_lm%d" % hb_, hid[hb_][:],
                                self.mixT[:, t0:t0 + T].rearrange("(c p) t -> p c t", p=128))
                return tl, th

            pre = load_h(0)
            for it in range(S // T):
                b = it % 2
                hbi = it % nhb
                t0 = it * T
                tl, th = pre
                if is_ffn:
                    thid = []
                    for m in range(NHC):
                        pb = ng % 4
                        ng += 1
                        tm = None
                        for half in range(2):
                            col0 = half * DFF + m * 128
                            for c in range(NCH):
                                tm = tr.op("pe", [th[c], wtok[c], psg_free[pb]], lambda e: e.matmul(
                                    psg[pb][:, half * T:(half + 1) * T], lhsT=w_in[:, c, col0:col0 + 128], rhs=ht[0][:, c, :],
                                    start=(c == 0), stop=(c == NCH - 1)))
                        sb_ = nsg % 2
                        nsg += 1
                        ts = tr.op("act", [tm, sg_free[sb_]], lambda e: e.activation(
                            out=sg[sb_][:], in_=psg[pb][:, 0:T], func=AF.Silu))
                        tu = tr.op("dve", [ts, tm, hid_free[hbi]], lambda e: e.tensor_tensor(
                            out=hid[hbi][:, m, :], in0=sg[sb_][:], in1=psg[pb][:, T:2 * T], op=ALU.mult))
                        psg_free[pb] = tu
                        sg_free[sb_] = tu
                        thid.append(tu)
                    h_free[0] = tm
                else:
                    thid = [th] * nh
                if it + 1 < S // T:
                    pre = load_h(it + 1)
                tz = []
                tzb = []
                for dc in range(NCH):
                    pb = ny % 3
                    ny += 1
                    tm = None
                    for m in range(nh):
                        tm = tr.op("pe", [thid[m], wtok2[m // 2], psy_free[pb]], lambda e: e.matmul(
                            psy[pb][:, 0:T], lhsT=w_out[:, m, dc * 128:(dc + 1) * 128], rhs=hid[hbi][:, m, :],
                            start=(m == 0), stop=(m == nh - 1)))
                    t1 = tr.op("dve", [tm, tl, self.setup_tok], lambda e: e.scalar_tensor_tensor(
                        out=xt[b][:, dc, :], in0=psy[pb][:, 0:T], scalar=gf(dc), in1=xt[b][:, dc, :],
                        op0=ALU.mult, op1=ALU.add))
                    psy_free[pb] = t1
                    tz.append(t1)
                    t2 = tr.op("act", [t1, zb_free], lambda e: e.copy(out=zb[:, dc, :], in_=xt[b][:, dc, :]))
                    t3 = tr.op("act", [t1, zb_free], lambda e: e.activation(
                        out=zq[:, dc, :], in_=xt[b][:, dc, :], func=AF.Square))
                    tzb.append([t2, t3])
                hid_free[hbi] = tm
                tm = None
                for half in range(2):
                    srcb = zb if half == 0 else zq
                    for dc in range(NCH):
                        tm = tr.op("pe", [tzb[dc], pss_free, self.const_tok], lambda e: e.matmul(
                            pss[:, half * T:(half + 1) * T], lhsT=self.ones_b[:], rhs=srcb[:, dc, :],
                            start=(dc == 0), stop=(dc == NCH - 1)))
                zb_free = tm
                ta = tr.op("dve", [tm, st_free], lambda e: e.tensor_scalar(
                    out=st[:, 0:2, :], in0=pss[:].rearrange("p (a t) -> p a t", a=2), scalar1=1.0 / D, scalar2=None, op0=ALU.mult))
                pss_free = ta
                tb = tr.op("dve", ta, lambda e: e.tensor_tensor(out=st[:, 2, :], in0=st[:, 0, :], in1=st[:, 0, :], op=ALU.mult))
                tc = tr.op("dve", tb, lambda e: e.tensor_tensor(out=st[:, 3, :], in0=st[:, 1, :], in1=st[:, 2, :], op=ALU.subtract))
                tc2 = tr.op("act", tc, lambda e: e.activation(
                    out=st[:, 3, :], in_=st[:, 3, :], func=AF.Sqrt, bias=self.epsc(eps), scale=1.0))
                td = tr.op("dve", tc2, lambda e: e.reciprocal(out=st[:, 2, :], in_=st[:, 3, :]))
                touts = []
                thb = []
                for dc in range(NCH):
                    eng = "dve" if dc % 2 == 0 else "pool"
                    t1 = tr.op(eng, [td, tz[dc], tzb[dc]], lambda e: e.tensor_tensor(
                        out=xt[b][:, dc, :], in0=xt[b][:, dc, :], in1=st[:, 0, :], op=ALU.subtract))
                    t2 = tr.op(eng, t1, lambda e: e.tensor_tensor(
                        out=xt[b][:, dc, :], in0=xt[b][:, dc, :], in1=st[:, 2, :], op=ALU.mult))
                    t3 = tr.op("act", t2, lambda e: e.activation(
                        out=xt[b][:, dc, :], in_=xt[b][:, dc, :], func=AF.Identity, bias=bcol(dc), scale=gcol(dc)))
                    touts.append(t3)
                    if hdst is not None:
                        thb.append(tr.op("act", [t3, hb_free[0]], lambda e: e.activation(
                            out=hb[0][:, dc, :], in_=xt[b][:, dc, :], func=AF.Identity, bias=sh2(dc), scale=sc2(dc))))
                st_free = touts
                tst = tr.dma("sp", touts, "f_s%d" % b,
                             dst[:, t0:t0 + T].rearrange("(c p) t -> p c t", p=128), xt[b][:])
                x_free[b] = [tst] + thb
                if hdst is not None:
                    hb_free[0] = tr.dma("sp", thb, "f_sh",
                                        hdst[:, t0:t0 + T].rearrange("(c p) t -> p c t", p=128), hb[0][:])
            tr.barrier()


NEG = -30000.0


def host_consts():
    c = np.zeros((128, 1024), np.float32)
    c[:, 0:128] = np.eye(128, dtype=np.float32)
    p = np.arange(128)
    ch = p % 64
    invf = np.where(ch < 16, np.float32(ROPE_THETA) ** (-(ch % 8).astype(np.float32) * np.float32(2.0 / 16)), 0.0)
    c[:, 128] = invf.astype(np.float32)
    c[:, 129] = math.pi / 2
    j = p[:, None]
    i = p[None, :]
    c[:, 256:384] = np.where(j >= i, 0.0, NEG)
    c[:, 384:512] = np.where(j <= i, 0.0, NEG)
    c[:, 512:640] = NEG
    s_ = (p % 64)[:, None]
    t_ = np.arange(64)[None, :]
    c[:, 640:704] = (s_ <= t_).astype(np.float32)
    return c


def pack_vecs(b, c, ln_g, ln_b, ada_b, hgrn_norm_w, hgrn_lb_logits):
    v = np.zeros((NVEC, 128), np.float32)
    v[V_ADAB:V_ADAB + 144] = ada_b.reshape(144, 128)
    v[V_LNG:V_LNG + 48] = ln_g.reshape(48, 128)
    v[V_LNB:V_LNB + 48] = ln_b.reshape(48, 128)
    v[V_C:V_C + 8] = c[b].reshape(8, 128)
    v[V_NW:V_NW + 8] = hgrn_norm_w.reshape(8, 128)
    v[V_LB:V_LB + 8] = hgrn_lb_logits.reshape(8, 128)
    return v


def make_in_maps(inputs, ncores, S):
    g = lambda k: np.ascontiguousarray(np.asarray(inputs[k]))
    consts = host_consts()
    maps = []
    for b in range(ncores):
        m = {
            "x": np.ascontiguousarray(g("x")[b, :S]),
            "pos": np.ascontiguousarray(g("positions")[b, :S].reshape(1, S).astype(np.int32)),
            "vecs": pack_vecs(b, g("c"), g("ln_g"), g("ln_b"), g("ada_b"), g("hgrn_norm_w"), g("hgrn_lb_logits")),
            "consts": consts,
            "ada_w": g("ada_w"),
            "ffn1_w_in": g("ffn1_w_in"), "ffn2_w_in": g("ffn2_w_in"),
            "ffn1_w_out": g("ffn1_w_out"), "ffn2_w_out": g("ffn2_w_out"),
            "mix_w_in": g("mix_w_in"), "mix_w_out": g("mix_w_out"),
        }
        maps.append(m)
    return maps


def kernel(**inputs):
    S = 8192
    nc = Builder(S).build()
    maps = make_in_maps(inputs, 8, S)
    res = run_bass_kernel_spmd(nc, maps, core_ids=list(range(8)))
    return np.stack([r["out"] for r in res.results], axis=0)
```

```python
import math
from contextlib import ExitStack
import numpy as np
import concourse.bass as bass
import concourse.mybir as mybir
from concourse.bass_utils import run_bass_kernel_spmd

F32 = mybir.dt.float32
BF16 = mybir.dt.bfloat16
I32 = mybir.dt.int32
AF = mybir.ActivationFunctionType
ALU = mybir.AluOpType

D = 1024
NCH = 8
DEPTH = 2
DFF = 2816
NHC = DFF // 128
IN_COLS = 3584
ALPHA = (2 * DEPTH) ** 0.25
LN_EPS = 1e-5
RMS_EPS = 1e-6
ROPE_THETA = 500000.0

V_ADAB = 0
V_LNG = 144
V_LNB = 192
V_C = 240
V_NW = 248
V_LB = 256
NVEC = 264


class Tracker:
    def __init__(self, nc, es):
        self.nc = nc
        self.es = es
        self.sems = {}
        self.cnt = {}
        self.seen = {}
        self.engs = {"pe": nc.tensor, "act": nc.scalar, "dve": nc.vector, "pool": nc.gpsimd, "sp": nc.sync}
        for n in self.engs:
            self.sems[n] = es.enter_context(nc.semaphore("s_" + n))
            self.cnt[n] = 0
            self.seen[n] = {}
        self.dsems = {}

    def dsem(self, name):
        if name not in self.dsems:
            s = self.es.enter_context(self.nc.semaphore("d_" + name))
            self.dsems[name] = [s, 0]
        return self.dsems[name]

    def wait(self, en, tok):
        if tok is None:
            return
        if isinstance(tok, list):
            for t in tok:
                self.wait(en, t)
            return
        key, sem, val = tok
        if self.seen[en].get(key, 0) >= val:
            return
        self.engs[en].wait_ge(sem, val)
        self.seen[en][key] = val

    def op(self, en, deps, inst_fn):
        self.wait(en, deps)
        inst = inst_fn(self.engs[en])
        self.cnt[en] += 1
        inst.then_inc(self.sems[en], 1)
        return ("e_" + en, self.sems[en], self.cnt[en])

    def last(self, en):
        if self.cnt[en] == 0:
            return None
        return ("e_" + en, self.sems[en], self.cnt[en])

    def dma(self, en, deps, dname, out, in_):
        self.wait(en, deps)
        d = self.dsem(dname)
        self.engs[en].dma_start(out=out, in_=in_).then_inc(d[0], 16)
        d[1] += 16
        return ("d_" + dname, d[0], d[1])

    def all_tokens(self):
        toks = [self.last(n) for n in self.engs if self.cnt[n]]
        toks += [("d_" + k, v[0], v[1]) for k, v in self.dsems.items() if v[1]]
        return toks

    def barrier(self):
        toks = self.all_tokens()
        for en in self.engs:
            self.wait(en, toks)


class Buf:
    def __init__(self, t):
        self.t = t
        self.w = {}
        self.r = {}


def _deps(reads, writes, deps, en=None):
    d = []
    for b in reads:
        d += list(b.w.values())
    for b in writes:
        d += list(b.w.values()) + list(b.r.values())
    if deps:
        d += deps if isinstance(deps, list) else [deps]
    if en == "pe":
        d = [t for t in d if t[0] != "e_pe"]
    return d


def _mark(tok, reads, writes):
    for b in reads:
        if b not in writes:
            b.r[tok[0]] = tok
    for b in writes:
        b.w[tok[0]] = tok
        b.r = {}


def opb(tr, en, reads, writes, fn, deps=None):
    tok = tr.op(en, _deps(reads, writes, deps, en), fn)
    _mark(tok, reads, writes)
    return tok


def dmab(tr, en, dname, reads, writes, out, in_, deps=None):
    tok = tr.dma(en, _deps(reads, writes, deps), dname, out, in_)
    _mark(tok, reads, writes)
    return tok


def interleave(gens):
    gens = list(gens)
    while gens:
        for g in list(gens):
            try:
                next(g)
            except StopIteration:
                gens.remove(g)


class Builder:
    def __init__(self, S, layers=DEPTH, debug_outs=(), stop_after=None):
        self.S = S
        self.layers = layers
        self.debug_outs = debug_outs
        self.stop_after = stop_after

    def sb(self, es, name, shape, dt):
        self.uid = getattr(self, "uid", 0) + 1
        return es.enter_context(self.nc.sbuf_tensor("sb%d_%s" % (self.uid, name), shape, dt))

    def ps(self, es, name, shape, dt=F32):
        self.uid = getattr(self, "uid", 0) + 1
        return es.enter_context(self.nc.psum_tensor("ps%d_%s" % (self.uid, name), shape, dt))

    def build(self):
        S = self.S
        nc = bass.Bass("TRN2", target_bir_lowering=False)
        self.nc = nc
        dt_ = nc.dram_tensor
        self.x_in = dt_("x", [S, D], F32, kind="ExternalInput").ap()
        self.pos_in = dt_("pos", [1, S], I32, kind="ExternalInput").ap()
        self.vecs_in = dt_("vecs", [NVEC, 128], F32, kind="ExternalInput").ap()
        self.consts_in = dt_("consts", [128, 1024], F32, kind="ExternalInput").ap()
        self.ada_w = dt_("ada_w", [DEPTH, D, 9 * D], F32, kind="ExternalInput").ap()
        self.w_ffn_in = [dt_("ffn1_w_in", [DEPTH, D, 2 * DFF], F32, kind="ExternalInput").ap(),
                         dt_("ffn2_w_in", [DEPTH, D, 2 * DFF], F32, kind="ExternalInput").ap()]
        self.w_ffn_out = [dt_("ffn1_w_out", [DEPTH, DFF, D], F32, kind="ExternalInput").ap(),
                          dt_("ffn2_w_out", [DEPTH, DFF, D], F32, kind="ExternalInput").ap()]
        self.w_mix_in = dt_("mix_w_in", [DEPTH, D, IN_COLS], F32, kind="ExternalInput").ap()
        self.w_mix_out = dt_("mix_w_out", [DEPTH, D, D], F32, kind="ExternalInput").ap()
        self.out = dt_("out", [S, D], F32, kind="ExternalOutput").ap()

        def scratch(name, shape, dt):
            kind = "ExternalOutput" if name in self.debug_outs else "Internal"
            return dt_(name, shape, dt, kind=kind).ap()
        self.xT = [scratch("xT0", [D, S], F32), scratch("xT1", [D, S], F32)]
        self.mixT = scratch("mixT", [D, S], BF16)
        self.hT = scratch("hT", [D, S], BF16)
        self.cosT = scratch("cosT", [128, S], F32)
        self.sinT = scratch("sinT", [128, S], F32)

        with ExitStack() as es:
            self.tr = Tracker(nc, es)
            self.setup(es)
            self.run_phases()
            self.tr.barrier()
        return nc

    def setup(self, es):
        nc, tr = self.nc, self.tr
        self.ident_f = self.sb(es, "ident_f", [128, 128], F32)
        self.ident_b = self.sb(es, "ident_b", [128, 128], BF16)
        self.ones_b = self.sb(es, "ones_b", [128, 128], BF16)
        self.cst = self.sb(es, "cst", [128, 1024], F32)
        self.vecs = self.sb(es, "vecs", [128, NVEC], F32)
        self.ada = self.sb(es, "ada", [128, DEPTH * 72], F32)
        self.msc = self.sb(es, "msc", [128, DEPTH * 72], F32)
        t0 = tr.dma("sp", None, "setup_c", self.cst[:], self.consts_in)
        tok = tr.op("dve", t0, lambda e: e.tensor_copy(out=self.ident_f[:], in_=self.cst[:, 0:128]))
        tokb = tr.op("dve", tok, lambda e: e.tensor_copy(out=self.ident_b[:], in_=self.cst[:, 0:128]))
        tok1 = tr.op("pool", None, lambda e: e.memset(self.ones_b[:], 1.0))
        self.lbv = self.sb(es, "lbv", [128, 8], F32)
        self.oml = self.sb(es, "oml", [128, 8], F32)
        self.epst = self.sb(es, "epst", [128, 4], F32)
        tok2 = tr.op("pool", None, lambda e: e.memset(self.epst[:, 0:1], LN_EPS / (ALPHA * ALPHA)))
        tok3 = tr.op("pool", None, lambda e: e.memset(self.epst[:, 1:2], RMS_EPS))
        self.const_tok = [tok, tokb, tok1, tok2, tok3, t0]
        with ExitStack() as es2:
            vraw = self.sb(es2, "vraw", [128, 3, 128], F32)
            pst = self.ps(es2, "pst", [128, 512], F32)
            psa = self.ps(es2, "psa", [128, 512], F32)
            cond = self.sb(es2, "cond", [128, 8], F32)
            wbuf = [self.sb(es2, "adaw%d" % i, [128, 8, 1152], F32) for i in range(2)]
            nrow = [128, 128, NVEC - 256]
            toks = []
            for i in range(3):
                toks.append(tr.dma("sp", None, "setup_v%d" % i, vraw[0:nrow[i], i, :],
                                   self.vecs_in[i * 128:i * 128 + nrow[i], :]))
            tp = None
            for i in range(3):
                tp = tr.op("pe", [toks[i], tok], lambda e: e.transpose(
                    pst[:, i * 128:i * 128 + nrow[i]], vraw[0:nrow[i], i, :], self.ident_f[0:nrow[i], 0:nrow[i]]))
            tv = tr.op("dve", tp, lambda e: e.tensor_copy(out=self.vecs[:], in_=pst[:, 0:NVEC]))
            tc = tr.op("act", tv, lambda e: e.activation(out=cond[:], in_=self.vecs[:, V_C:V_C + 8], func=AF.Silu))
            free = [None, None]
            blk = 0
            for l in range(self.layers):
                for cb in range(8):
                    buf = wbuf[blk % 2]
                    tl = tr.dma("sp", free[blk % 2], "adaw%d" % (blk % 2), buf[:],
                                self.ada_w[l, :, cb * 1152:(cb + 1) * 1152].rearrange("(c p) n -> p c n", p=128))
                    tm = None
                    for j in range(9):
                        col = l * 72 + cb * 9 + j
                        for kc in range(8):
                            tm = tr.op("pe", [tl, tc], lambda e: e.matmul(
                                psa[:, col:col + 1], lhsT=buf[:, kc, j * 128:(j + 1) * 128], rhs=cond[:, kc:kc + 1],
                                start=(kc == 0), stop=(kc == 7)))
                    free[blk % 2] = tm
                    blk += 1
            n = self.layers * 72
            ta = tr.op("dve", [tm, tv], lambda e: e.tensor_tensor(
                out=self.ada[:, 0:n], in0=psa[:, 0:n], in1=self.vecs[:, V_ADAB:V_ADAB + n], op=ALU.add))
            tlast = ta
            for l in range(self.layers):
                for sub in range(3):
                    base = l * 72 + sub * 24
                    rw = 1.0 if sub == 1 else 0.5
                    t1 = tr.op("dve", ta, lambda e: e.tensor_copy(out=self.msc[:, base:base + 8], in_=self.ada[:, base:base + 8]))
                    t2 = tr.op("dve", ta, lambda e: e.tensor_scalar(
                        out=self.msc[:, base + 8:base + 16], in0=self.ada[:, base + 8:base + 16],
                        scalar1=1.0, scalar2=None, op0=ALU.add))
                    tlast = tr.op("dve", ta, lambda e: e.tensor_scalar(
                        out=self.msc[:, base + 16:base + 24], in0=self.ada[:, base + 16:base + 24],
                        scalar1=1.0, scalar2=rw / ALPHA, op0=ALU.add, op1=ALU.mult))
            self.setup_tok = [tlast, t1, t2, tv]
            self.dump("ada", self.ada[:], [128, DEPTH * 72])
            self.dump("msc", self.msc[:], [128, DEPTH * 72])
            self.dump("vecs", self.vecs[:], [128, NVEC])
            tr.barrier()

    def dump(self, name, ap, shape, dt=F32):
        if not self.debug_outs:
            return
        d = self.nc.dram_tensor("dbg_" + name, list(shape), dt, kind="ExternalOutput").ap()
        self.tr.dma("sp", self.tr.all_tokens(), "dbg", d, ap)

    def epsc(self, eps):
        return self.epst[:, 0:1] if eps > 2e-6 else self.epst[:, 1:2]

    def run_phases(self):
        self.rope_setup()
        self.transpose_in()
        cur = 0
        for l in range(self.layers):
            self.ffn(l, 0, self.xT[cur], self.xT[1 - cur], hdst=self.hT)
            cur = 1 - cur
            if self.stop_after == ("ffn", l, 0):
                break
            self.attn(l)
            if self.stop_after == ("attn", l):
                break
            self.hgrn(l)
            if self.stop_after == ("hgrn", l):
                break
            self.ffn(l, 1, self.xT[cur], self.xT[1 - cur])
            cur = 1 - cur
            if self.stop_after == ("mix", l):
                break
            self.ffn(l, 2, self.xT[cur], self.xT[1 - cur])
            cur = 1 - cur
        self.transpose_out(self.xT[cur])

    def transpose_in(self):
        nc, tr, S = self.nc, self.tr, self.S
        with ExitStack() as es:
            xin = [self.sb(es, "ti_x%d" % i, [128, 4, D], F32) for i in range(2)]
            xo = [self.sb(es, "ti_o%d" % i, [128, NCH, 512], F32) for i in range(2)]
            pss = [self.ps(es, "ti_p%d" % i, [128, 512], F32) for i in range(4)]
            ps_free = [None] * 4
            xin_free = [None] * 2
            xo_free = [None] * 2
            nb = 0
            for it in range(S // 512):
                b = it % 2
                tl = tr.dma("sp", xin_free[b], "ti_l%d" % b, xin[b][:],
                            self.x_in[it * 512:(it + 1) * 512, :].rearrange("(j p) d -> p j d", p=128))
                evs = []
                for c in range(NCH):
                    pb = nb % 4
                    nb += 1
                    tp = None
                    for j in range(4):
                        tp = tr.op("pe", [tl, ps_free[pb], self.const_tok], lambda e: e.transpose(
                            pss[pb][:, j * 128:(j + 1) * 128], xin[b][:, j, c * 128:(c + 1) * 128], self.ident_f[:]))
                    eng = "act" if c % 2 == 0 else "dve"
                    if eng == "act":
                        te = tr.op("act", [tp, xo_free[b]], lambda e: e.copy(out=xo[b][:, c, :], in_=pss[pb][:]))
                    else:
                        te = tr.op("dve", [tp, xo_free[b]], lambda e: e.tensor_copy(out=xo[b][:, c, :], in_=pss[pb][:]))
                    ps_free[pb] = te
                    evs.append(te)
                xin_free[b] = tp
                ts = tr.dma("pool", evs, "ti_s%d" % b,
                            self.xT[0][:, it * 512:(it + 1) * 512].rearrange("(c p) t -> p c t", p=128), xo[b][:])
                xo_free[b] = ts
            tr.barrier()

    def transpose_out(self, src):
        nc, tr, S = self.nc, self.tr, self.S
        with ExitStack() as es:
            xin = [self.sb(es, "to_x%d" % i, [128, NCH, 512], F32) for i in range(2)]
            xo = [self.sb(es, "to_o%d" % i, [128, 4, D], F32) for i in range(2)]
            pss = [self.ps(es, "to_p%d" % i, [128, 512], F32) for i in range(4)]
            ps_free = [None] * 4
            xin_free = [None] * 2
            xo_free = [None] * 2
            nb = 0
            for it in range(S // 512):
                b = it % 2
                tl = tr.dma("sp", xin_free[b], "to_l%d" % b, xin[b][:],
                            src[:, it * 512:(it + 1) * 512].rearrange("(c p) t -> p c t", p=128))
                evs = []
                for j in range(4):
                    for h in range(2):
                        pb = nb % 4
                        nb += 1
                        tp = None
                        for cc in range(4):
                            c = h * 4 + cc
                            tp = tr.op("pe", [tl, ps_free[pb], self.const_tok], lambda e: e.transpose(
                                pss[pb][:, cc * 128:(cc + 1) * 128], xin[b][:, c, j * 128:(j + 1) * 128], self.ident_f[:]))
                        if (j + h) % 2 == 0:
                            te = tr.op("act", [tp, xo_free[b]], lambda e: e.copy(
                                out=xo[b][:, j, h * 512:(h + 1) * 512], in_=pss[pb][:]))
                        else:
                            te = tr.op("dve", [tp, xo_free[b]], lambda e: e.tensor_copy(
                                out=xo[b][:, j, h * 512:(h + 1) * 512], in_=pss[pb][:]))
                        ps_free[pb] = te
                        evs.append(te)
                xin_free[b] = tp
                ts = tr.dma("pool", evs, "to_s%d" % b,
                            self.out[it * 512:(it + 1) * 512, :].rearrange("(j p) d -> p j d", p=128), xo[b][:])
                xo_free[b] = ts
            tr.barrier()

    def rope_setup(self):
        nc, tr, S = self.nc, self.tr, self.S
        CW = min(2048, S)
        PI = math.pi
        with ExitStack() as es:
            posi = Buf(self.sb(es, "r_pi", [128, CW], I32))
            ang = Buf(self.sb(es, "r_ang", [128, CW], F32))
            m1 = Buf(self.sb(es, "r_m1", [128, CW], F32))
            m2 = Buf(self.sb(es, "r_m2", [128, CW], F32))
            cK = Buf(None)
            cK.w = {t[0]: t for t in self.const_tok}
            invf = self.cst[:, 128:129]
            halfpi = self.cst[:, 129:130]
            for i in range(S // CW):
                a, b = i * CW, (i + 1) * CW
                dmab(tr, "sp", "r_l", [], [posi], posi.t[:], self.pos_in[0:1, a:b].partition_broadcast(128))
                opb(tr, "dve", [posi], [ang], lambda e: e.tensor_copy(out=ang.t[:], in_=posi.t[:]))
                opb(tr, "dve", [cK], [ang], lambda e: e.tensor_scalar(
                    out=ang.t[:], in0=ang.t[:], scalar1=invf, scalar2=None, op0=ALU.mult))
                C1 = 6.28125
                C2 = 2 * PI - C1
                opb(tr, "dve", [ang], [m1], lambda e: e.tensor_scalar(
                    out=m1.t[:], in0=ang.t[:], scalar1=1.0 / (2 * PI), scalar2=None, op0=ALU.mult))
                opb(tr, "dve", [m1], [posi], lambda e: e.tensor_copy(out=posi.t[:], in_=m1.t[:]))
                opb(tr, "dve", [posi], [m1], lambda e: e.tensor_copy(out=m1.t[:], in_=posi.t[:]))
                opb(tr, "dve", [m1], [ang], lambda e: e.scalar_tensor_tensor(
                    out=ang.t[:], in0=m1.t[:], scalar=-C1, in1=ang.t[:], op0=ALU.mult, op1=ALU.add))
                opb(tr, "dve", [m1], [ang], lambda e: e.scalar_tensor_tensor(
                    out=ang.t[:], in0=m1.t[:], scalar=-C2, in1=ang.t[:], op0=ALU.mult, op1=ALU.add))
                opb(tr, "dve", [], [ang], lambda e: e.tensor_scalar(
                    out=ang.t[:], in0=ang.t[:], scalar1=-PI, scalar2=PI, op0=ALU.max, op1=ALU.min))
                opb(tr, "dve", [ang], [m2], lambda e: e.scalar_tensor_tensor(
                    out=m2.t[:], in0=ang.t[:], scalar=-1.0, in1=ang.t[:], op0=ALU.mult, op1=ALU.max))
                opb(tr, "act", [ang], [m1], lambda e: e.activation(out=m1.t[:], in_=ang.t[:], func=AF.Sin))
                opb(tr, "act", [cK], [m2], lambda e: e.activation(out=m2.t[:], in_=m2.t[:], func=AF.Sin, bias=halfpi, scale=-1.0))
                dmab(tr, "sp", "r_s1", [m1], [], self.sinT[:, a:b], m1.t[:])
                dmab(tr, "sp", "r_s2", [m2], [], self.cosT[:, a:b], m2.t[:])
            t1 = tr.op("pool", self.setup_tok, lambda e: e.memset(self.lbv[:], 0.0))
            t2 = tr.op("dve", [t1] + self.setup_tok, lambda e: e.tensor_tensor(
                out=self.lbv[:, 4:8], in0=self.vecs[:, V_LB + 4:V_LB + 8], in1=self.vecs[:, V_LB:V_LB + 4], op=ALU.subtract))
            t3 = tr.op("act", t2, lambda e: e.activation(out=self.lbv[:, 4:8], in_=self.lbv[:, 4:8], func=AF.Sigmoid))
            t4 = tr.op("dve", t3, lambda e: e.tensor_scalar(
                out=self.oml[:], in0=self.lbv[:], scalar1=-1.0, scalar2=1.0, op0=ALU.mult, op1=ALU.add))
            self.lb_tok = [t3, t4]
            tr.barrier()

    def attn(self, l):
        nc, tr, S = self.nc, self.tr, self.S
        NSEG = S // 2048
        with ExitStack() as es:
            B = lambda name, shape, dt: Buf(self.sb(es, name, shape, dt))
            P = lambda name, shape, dt=F32: Buf(self.ps(es, name, shape, dt))
            cK = Buf(None)
            cK.w = {t[0]: t for t in self.const_tok}
            wq = B("a_wq", [128, NCH, 640], BF16)
            ht = [B("a_ht%d" % i, [128, NCH, 512], BF16) for i in range(2)]
            cs = [B("a_cs%d" % i, [128, 512], F32) for i in range(2)]
            sn = [B("a_sn%d" % i, [128, 512], F32) for i in range(2)]
            t1b = [B("a_t1%d" % i, [128, 512], F32) for i in range(2)]
            t2b = [B("a_t2%d" % i, [128, 512], F32) for i in range(2)]
            qs = [[B("a_q%d%d" % (p, o), [128, 2048], BF16) for o in range(3)] for p in range(2)]
            ks = [[B("a_k%d%d" % (p, o), [128, 2048], BF16) for o in range(3)] for p in range(2)]
            vt = [B("a_vt%d" % o, [128, 2048], BF16) for o in range(3)]
            V = [[B("a_V%d%d" % (p, o), [128, 16, 128], BF16) for o in range(3)] for p in range(2)]
            pT = [B("a_pT%d" % i, [128, 512], BF16) for i in range(4)]
            accN = B("a_accN", [128, 2048], F32)
            accD = B("a_accD", [128, 2048], F32)
            ao = [B("a_ao%d" % i, [128, 2048], BF16) for i in range(2)]
            masks = [B("a_mk%d" % i, [128, 512], BF16) for i in range(3)]
            psp = [P("a_pp%d" % i, [128, 512]) for i in range(2)]
            pss = [P("a_ps%d" % i, [128, 512]) for i in range(4)]
            psn = [P("a_pn%d" % i, [128, 512]) for i in range(2)]
            pst = psp[0]
            pstT = psp[0].t.bitcast(BF16)
            mP, mC, mN = self.cst[:, 256:384], self.cst[:, 384:512], self.cst[:, 512:640]
            for mi, pat in enumerate([(mP, mC, mP, mC), (mN, mC, mP, mC), (mN, mC, mN, mC)]):
                for q_, src_ in enumerate(pat):
                    opb(tr, "dve", [cK], [masks[mi]], lambda e: e.tensor_copy(
                        out=masks[mi].t[:, q_ * 128:(q_ + 1) * 128], in_=src_))
            cnt = {"pp": 0, "ps": 0, "pn": 0, "pT": 0, "tile": 0, "ao": 0}

            def nxt(k, n):
                v = cnt[k] % n
                cnt[k] += 1
                return v

            prev_done = {}

            def phaseA(c, s_, gate=None):
                par = s_ % 2
                if s_ == 0:
                    for g, col0 in ((0, c * 128), (2, 512 + c * 128), (4, 1024 + c * 128)):
                        dmab(tr, "pool", "a_w%d" % g, [], [wq], wq.t[:, :, g * 128:(g + 1) * 128],
                             self.w_mix_in[l, :, col0:col0 + 128].rearrange("(k p) n -> p k n", p=128))
                    for g in (0, 2):
                        opb(tr, "pool", [], [wq], lambda e: e.memset(wq.t[:, :, (g + 1) * 128:(g + 2) * 128], 0.0))
                        for hh in range(2):
                            so = g * 128 + hh * 64
                            do = (g + 1) * 128 + hh * 64
                            opb(tr, "dve", [], [wq], lambda e: e.tensor_scalar(
                                out=wq.t[:, :, do:do + 8], in0=wq.t[:, :, so + 8:so + 16], scalar1=-1.0, scalar2=None, op0=ALU.mult))
                            opb(tr, "dve", [], [wq], lambda e: e.tensor_copy(
                                out=wq.t[:, :, do + 8:do + 16], in_=wq.t[:, :, so:so + 8]))
                    yield
                for jl in range(4):
                    t0 = s_ * 2048 + jl * 512
                    hb = nxt("tile", 2)
                    dmab(tr, "sp", "a_lh%d" % hb, [], [ht[hb]], ht[hb].t[:],
                         self.hT[:, t0:t0 + 512].rearrange("(k p) t -> p k t", p=128))
                    dmab(tr, "sp", "a_lc%d" % hb, [], [cs[hb]], cs[hb].t[:], self.cosT[:, t0:t0 + 512])
                    dmab(tr, "sp", "a_ls%d" % hb, [], [sn[hb]], sn[hb].t[:], self.sinT[:, t0:t0 + 512])
                    nat = slice(jl * 512, (jl + 1) * 512)
                    for gi, (g, dest) in enumerate(((0, qs[par]), (2, ks[par]))):
                        while gi == 1 and gate is not None and not prev_done.get(gate, False):
                            yield
                        pa = psp[nxt("pp", 2)]
                        for kc in range(NCH):
                            opb(tr, "pe", [wq, ht[hb]], [pa], lambda e: e.matmul(
                                pa.t[:], lhsT=wq.t[:, kc, g * 128:(g + 1) * 128], rhs=ht[hb].t[:, kc, :],
                                start=(kc == 0), stop=(kc == NCH - 1)))
                        opb(tr, "dve", [pa, cs[hb]], [t1b[gi]], lambda e: e.tensor_tensor(
                            out=t1b[gi].t[:], in0=pa.t[:], in1=cs[hb].t[:], op=ALU.mult))
                        pb = psp[nxt("pp", 2)]
                        for kc in range(NCH):
                            opb(tr, "pe", [wq, ht[hb]], [pb], lambda e: e.matmul(
                                pb.t[:], lhsT=wq.t[:, kc, (g + 1) * 128:(g + 2) * 128], rhs=ht[hb].t[:, kc, :],
                                start=(kc == 0), stop=(kc == NCH - 1)))
                        opb(tr, "dve", [pb, sn[hb]], [t2b[gi]], lambda e: e.tensor_tensor(
                            out=t2b[gi].t[:], in0=pb.t[:], in1=sn[hb].t[:], op=ALU.mult))
                        opb(tr, "pool", [t1b[gi], t2b[gi]], [dest[0]], lambda e: e.tensor_tensor(
                            out=dest[0].t[:, nat], in0=t1b[gi].t[:], in1=t2b[gi].t[:], op=ALU.add))
                        opb(tr, "pool", [dest[0]], [dest[1]], lambda e: e.tensor_copy(
                            out=dest[1].t[:].rearrange("p (r n i) -> p r n i", r=4, n=4)[:, :, jl, :],
                            in_=dest[0].t[:, nat].rearrange("p (i r) -> p r i", r=4)))
                        opb(tr, "act", [dest[0]], [dest[2]], lambda e: e.copy(
                            out=dest[2].t[:].rearrange("p (r j i) -> p r j i", r=16, j=4)[:, :, jl, :],
                            in_=dest[0].t[:, nat].rearrange("p (i r) -> p r i", r=16)))
                        yield
                    pv = psp[nxt("pp", 2)]
                    for kc in range(NCH):
                        opb(tr, "pe", [wq, ht[hb]], [pv], lambda e: e.matmul(
                            pv.t[:], lhsT=wq.t[:, kc, 512:640], rhs=ht[hb].t[:, kc, :],
                            start=(kc == 0), stop=(kc == NCH - 1)))
                    opb(tr, "act", [pv], [vt[0]], lambda e: e.copy(out=vt[0].t[:, nat], in_=pv.t[:]))
                    opb(tr, "pool", [vt[0]], [vt[1]], lambda e: e.tensor_copy(
                        out=vt[1].t[:].rearrange("p (r n i) -> p r n i", r=4, n=4)[:, :, jl, :],
                        in_=vt[0].t[:, nat].rearrange("p (i r) -> p r i", r=4)))
                    opb(tr, "dve", [vt[0]], [vt[2]], lambda e: e.tensor_copy(
                        out=vt[2].t[:].rearrange("p (r j i) -> p r j i", r=16, j=4)[:, :, jl, :],
                        in_=vt[0].t[:, nat].rearrange("p (i r) -> p r i", r=16)))
                    yield
                for o in range(3):
                    for half in range(2):
                        for k_ in range(8):
                            blk = half * 8 + k_
                            opb(tr, "pe", [vt[o], cK], [pst], lambda e: e.transpose(
                                pstT[:, k_ * 128:(k_ + 1) * 128], vt[o].t[:, blk * 128:(blk + 1) * 128], self.ident_b[:]))
                        opb(tr, "act" if half == 0 else "dve", [pst], [V[par][o]],
                            (lambda e: e.copy(out=V[par][o].t[:, half * 8:(half + 1) * 8, :],
                                              in_=pstT[:].rearrange("p (k n) -> p k n", k=8))) if half == 0 else
                            (lambda e: e.tensor_copy(out=V[par][o].t[:, half * 8:(half + 1) * 8, :],
                                                     in_=pstT[:].rearrange("p (k n) -> p k n", k=8))))
                        yield

            def phaseB(c, s_):
                par = s_ % 2
                order = ([(2, p_) for p_ in range(8)] + [(1, p_) for p_ in (0, 2, 4, 6)] + [(0, 0)]
                         + [(1, p_) for p_ in (1, 3, 5, 7)] + [(0, p_) for p_ in range(1, 8)])

                def scores(o, lbp):
                    prevs = []
                    for bb in range(2):
                        lb = lbp * 2 + bb
                        if o == 0:
                            pv_ = (par, lb - 1) if lb > 0 else ((1 - par, 15) if s_ > 0 else None)
                        elif o == 1:
                            pv_ = (par, lb - 1) if lb % 4 > 0 else ((1 - par, lb + 3) if s_ > 0 else None)
                        else:
                            pv_ = (1 - par, lb) if s_ > 0 else None
                        prevs.append(pv_)
                    mk = masks[0] if prevs[0] is not None else (masks[1] if prevs[1] is not None else masks[2])
                    sps = [pss[nxt("ps", 4)] for _ in range(2)]
                    for hh in range(2):
                        opb(tr, "pe", [mk, cK], [sps[hh]], lambda e: e.matmul(
                            sps[hh].t[:], lhsT=self.ident_b[:], rhs=mk.t[:], start=True, stop=False))
                    for bb in range(2):
                        lb = lbp * 2 + bb
                        last = (bb == 1)
                        if prevs[bb] is not None:
                            pp, pl = prevs[bb]
                            for hh in range(2):
                                rows = slice(hh * 64, (hh + 1) * 64)
                                opb(tr, "pe", [ks[pp][o], qs[par][o]], [sps[hh]], lambda e: e.matmul(
                                    sps[hh].t[:, bb * 256:bb * 256 + 128], lhsT=ks[pp][o].t[rows, pl * 128:(pl + 1) * 128],
                                    rhs=qs[par][o].t[rows, lb * 128:(lb + 1) * 128], start=False, stop=False))
                        for hh in range(2):
                            rows = slice(hh * 64, (hh + 1) * 64)
                            opb(tr, "pe", [ks[par][o], qs[par][o]], [sps[hh]], lambda e: e.matmul(
                                sps[hh].t[:, bb * 256 + 128:bb * 256 + 256], lhsT=ks[par][o].t[rows, lb * 128:(lb + 1) * 128],
                                rhs=qs[par][o].t[rows, lb * 128:(lb + 1) * 128], start=False, stop=last))
                    pts = []
                    for hh in range(2):
                        pt = pT[nxt("pT", 4)]
                        opb(tr, "act", [sps[hh]], [pt], lambda e: e.activation(
                            out=pt.t[:], in_=sps[hh].t[:], func=AF.Exp, scale=0.125))
                        pts.append(pt)
                    return prevs, pts

                def pv_acc(o, lbp, prevs, pts):
                    nd = psn[nxt("pn", 2)]
                    for bb in range(2):
                        lb = lbp * 2 + bb
                        srcs = []
                        if prevs[bb] is not None:
                            pp, pl = prevs[bb]
                            srcs.append((V[pp][o], pl, bb * 256))
                        srcs.append((V[par][o], lb, bb * 256 + 128))
                        for kind in range(2):
                            for si, (vb, blk, pc) in enumerate(srcs):
                                for hh in range(2):
                                    rows = slice(hh * 64, (hh + 1) * 64)
                                    pt = pts[hh]
                                    lhs = vb.t[:, blk, rows] if kind == 0 else self.ones_b[:, 0:64]
                                    opb(tr, "pe", [vb, pt, cK], [nd], lambda e: e.matmul(
                                        nd.t[rows, kind * 256 + bb * 128:kind * 256 + (bb + 1) * 128],
                                        lhsT=lhs, rhs=pt.t[:, pc:pc + 128],
                                        start=(si == 0), stop=(si == len(srcs) - 1)))
                    for kind, acc in ((0, accN), (1, accD)):
                        if o == 0:
                            av = acc.t[:, lbp * 256:(lbp + 1) * 256].rearrange("p (b i) -> p b i", b=2)
                        elif o == 1:
                            r_, n0 = lbp // 2, (lbp % 2) * 2
                            av = acc.t[:].rearrange("p (n i r) -> p r n i", n=4, r=4)[:, r_, n0:n0 + 2, :]
                        else:
                            av = acc.t[:].rearrange("p (i r) -> p r i", r=16)[:, lbp * 2:lbp * 2 + 2, :]
                        sv = nd.t[:, kind * 256:(kind + 1) * 256].rearrange("p (b i) -> p b i", b=2)
                        if o == 2:
                            opb(tr, "dve", [nd], [acc], lambda e: e.tensor_copy(out=av, in_=sv))
                        else:
                            opb(tr, "dve", [nd], [acc], lambda e: e.tensor_tensor(out=av, in0=av, in1=sv, op=ALU.add))

                ctx = scores(*order[0])
                for oi_, (o, lbp) in enumerate(order):
                    nctx = scores(*order[oi_ + 1]) if oi_ + 1 < len(order) else None
                    pv_acc(o, lbp, *ctx)
                    ctx = nctx
                    if oi_ == 12:
                        prev_done[(c, s_)] = True
                    yield
                ab = ao[nxt("ao", 2)]
                opb(tr, "act", [], [accD], lambda e: e.activation(out=accD.t[:], in_=accD.t[:], func=AF.Ln))
                opb(tr, "act", [], [accD], lambda e: e.activation(out=accD.t[:], in_=accD.t[:], func=AF.Exp, scale=-1.0))
                opb(tr, "pool", [accN, accD], [ab], lambda e: e.tensor_tensor(
                    out=ab.t[:], in0=accN.t[:], in1=accD.t[:], op=ALU.mult))
                dmab(tr, "sp", "a_so%d" % (cnt["ao"] % 2), [ab], [],
                     self.mixT[c * 128:(c + 1) * 128, s_ * 2048:(s_ + 1) * 2048], ab.t[:])
                yield

            items = [(c, s_) for c in range(4) for s_ in range(NSEG)]
            interleave([phaseA(*items[0])])
            for i_, it_ in enumerate(items):
                gens = [phaseB(*it_)]
                if i_ + 1 < len(items):
                    gens.append(phaseA(*items[i_ + 1], gate=it_))
                interleave(gens)
            tr.barrier()

    def hgrn(self, l):
        nc, tr, S = self.nc, self.tr, self.S
        NT = S // 512
        with ExitStack() as es:
            B = lambda name, shape, dt: Buf(self.sb(es, name, shape, dt))
            P = lambda name, shape, dt=F32: Buf(self.ps(es, name, shape, dt))
            cK = Buf(None)
            cK.w = {t[0]: t for t in self.const_tok + self.lb_tok + self.setup_tok}
            wh = B("h_w", [128, NCH, 2048], BF16)
            ht = [B("h_ht%d" % i, [128, NCH, 512], BF16) for i in range(2)]
            qsl = [B("h_qs%d" % i, [128, 512], F32) for i in range(2)]
            fb = [B("h_f%d" % i, [128, 512], F32) for i in range(2)]
            ab_ = [B("h_a%d" % i, [128, 512], F32) for i in range(2)]
            kk = [B("h_kk%d" % i, [128, 512], F32) for i in range(2)]
            bc = [B("h_b%d" % i, [128, 512], F32) for i in range(2)]
            e1 = [B("h_e1%d" % i, [128, 512], F32) for i in range(2)]
            e2 = [B("h_e2%d" % i, [128, 512], F32) for i in range(2)]
            kh = [B("h_kh%d" % i, [128, 512], BF16) for i in range(2)]
            QT = [[B("h_QT%d%d" % (h, i), [128, 512], BF16) for i in range(2)] for h in range(4)]
            KT = [[B("h_KT%d%d" % (h, i), [128, 512], BF16) for i in range(2)] for h in range(4)]
            KK = [[B("h_KK%d%d" % (h, i), [64, 8, 128], BF16) for i in range(2)] for h in range(4)]
            SG = [[B("h_SG%d%d" % (h, i), [128, 512], BF16) for i in range(2)] for h in range(4)]
            DEC = [[B("h_DC%d%d" % (h, i), [128, 8], F32) for i in range(2)] for h in range(4)]
            VV = [B("h_VV%d" % i, [64, 8, 512], BF16) for i in range(2)]
            ST = [B("h_ST%d" % h, [128, 128], F32) for h in range(4)]
            STb = [B("h_STb%d" % h, [128, 128], BF16) for h in range(4)]
            AM = [B("h_AM%d" % i, [64, 256], BF16) for i in range(2)]
            O32 = [B("h_O%d" % i, [128, 4, 512], F32) for i in range(2)]
            osq = B("h_osq", [128, 512], BF16)
            rt = B("h_rt", [128, 512], F32)
            g1 = B("h_g1", [128, 512], F32)
            GO = [B("h_GO%d" % i, [128, 512], BF16) for i in range(2)]
            tri4 = B("h_tri", [64, 256], BF16)
            rmask = B("h_rm", [128, 512], F32)
            psp = [P("h_pp%d" % i, [128, 512]) for i in range(2)]
            psA = P("h_pA", [128, 512])
            pso = [P("h_po%d" % i, [128, 512]) for i in range(2)]
            psS = [P("h_pS%d" % i, [128, 512]) for i in range(2)]
            pst = psp[0]
            pstT = psp[0].t.bitcast(BF16)
            psm = P("h_pm", [128, 512])
            cnt = {"pp": 0, "po": 0, "am": 0, "t": 0, "go": 0}

            def nxt(k, n):
                v = cnt[k] % n
                cnt[k] += 1
                return v

            for h in range(4):
                opb(tr, "dve", [cK], [tri4], lambda e: e.tensor_copy(
                    out=tri4.t[:, h * 64:(h + 1) * 64], in_=self.cst[0:64, 640:704]))
            opb(tr, "pool", [], [rmask], lambda e: e.memset(rmask.t[:], 1.0))
            opb(tr, "pool", [], [rmask], lambda e: e.memset(
                rmask.t[:].rearrange("p (n t) -> p n t", t=64)[:, :, 0:1], 0.0))
            for h in range(4):
                opb(tr, "pool", [], [ST[h]], lambda e: e.memset(ST[h].t[:], 0.0))
                opb(tr, "pool", [], [STb[h]], lambda e: e.memset(STb[h].t[:], 0.0))
            for kc in range(NCH):
                dmab(tr, "pool", "h_w%d" % kc, [], [wh], wh.t[:, kc, :],
                     self.w_mix_in[l, kc * 128:(kc + 1) * 128, 1536:3584])
            lbc = lambda h: self.lbv[:, l * 4 + h:l * 4 + h + 1]
            omc = lambda h: self.oml[:, l * 4 + h:l * 4 + h + 1]
            nwc = lambda h: self.vecs[:, V_NW + l * 4 + h:V_NW + l * 4 + h + 1]

            def proj(hb, col0, dst):
                for kc in range(NCH):
                    opb(tr, "pe", [wh, ht[hb]], [dst], lambda e: e.matmul(
                        dst.t[:], lhsT=wh.t[:, kc, col0:col0 + 128], rhs=ht[hb].t[:, kc, :],
                        start=(kc == 0), stop=(kc == NCH - 1)))

            def hA(j):
                t0 = j * 512
                tb = j % 2
                dmab(tr, "sp", "h_lh%d" % tb, [], [ht[tb]], ht[tb].t[:],
                     self.hT[:, t0:t0 + 512].rearrange("(k p) t -> p k t", p=128))
                for hp in range(2):
                    hs = (2 * hp, 2 * hp + 1)
                    for h in hs:
                        i2 = h % 2
                        pq = psp[nxt("pp", 2)]
                        proj(tb, h * 128, pq)
                        opb(tr, "act", [pq], [qsl[i2]], lambda e: e.activation(out=qsl[i2].t[:], in_=pq.t[:], func=AF.Silu))
                        pg = psp[nxt("pp", 2)]
                        proj(tb, 1536 + h * 128, pg)
                        opb(tr, "act", [pg], [SG[h][tb]], lambda e: e.activation(out=SG[h][tb].t[:], in_=pg.t[:], func=AF.Silu))
                    for h in hs:
                        i2 = h % 2
                        pf = psp[nxt("pp", 2)]
                        proj(tb, 512 + h * 128, pf)
                        opb(tr, "act", [pf], [fb[i2]], lambda e: e.activation(out=fb[i2].t[:], in_=pf.t[:], func=AF.Sigmoid))
                    yield
                    for h in hs:
                        i2 = h % 2
                        opb(tr, "dve", [cK], [fb[i2]], lambda e: e.tensor_scalar(
                            out=fb[i2].t[:], in0=fb[i2].t[:], scalar1=omc(h), scalar2=lbc(h), op0=ALU.mult, op1=ALU.add))
                    for h in hs:
                        i2 = h % 2
                        opb(tr, "act", [fb[i2]], [ab_[i2]], lambda e: e.activation(out=ab_[i2].t[:], in_=fb[i2].t[:], func=AF.Ln))
                        opb(tr, "pool", [fb[i2]], [kk[i2]], lambda e: e.tensor_scalar(
                            out=kk[i2].t[:], in0=fb[i2].t[:], scalar1=-1.0, scalar2=1.0, op0=ALU.mult, op1=ALU.add))
                    for h in hs:
                        i2 = h % 2
                        opb(tr, "dve", [rmask, ab_[i2]], [bc[i2]], lambda e: e.tensor_tensor_scan(
                            out=bc[i2].t[:], data0=rmask.t[:], data1=ab_[i2].t[:], initial=0.0, op0=ALU.mult, op1=ALU.add))
                    yield
                    for h in hs:
                        i2 = h % 2
                        opb(tr, "act", [bc[i2]], [e1[i2]], lambda e: e.activation(out=e1[i2].t[:], in_=bc[i2].t[:], func=AF.Exp))
                        opb(tr, "act", [bc[i2]], [e2[i2]], lambda e: e.activation(out=e2[i2].t[:], in_=bc[i2].t[:], func=AF.Exp, scale=-1.0))
                    for h in hs:
                        i2 = h % 2
                        opb(tr, "pool", [qsl[i2], e1[i2]], [QT[h][tb]], lambda e: e.tensor_tensor(
                            out=QT[h][tb].t[:], in0=qsl[i2].t[:], in1=e1[i2].t[:], op=ALU.mult))
                        opb(tr, "dve", [kk[i2], e2[i2]], [KT[h][tb]], lambda e: e.tensor_tensor(
                            out=KT[h][tb].t[:], in0=kk[i2].t[:], in1=e2[i2].t[:], op=ALU.mult))
                        e1v = e1[i2].t[:].rearrange("p (n t) -> p n t", t=64)
                        opb(tr, "dve", [e1[i2]], [DEC[h][tb]], lambda e: e.tensor_copy(
                            out=DEC[h][tb].t[:].rearrange("p (n o) -> p n o", o=1), in_=e1v[:, :, 63:64]))
                    yield
                    for h in hs:
                        i2 = h % 2
                        e1v = e1[i2].t[:].rearrange("p (n t) -> p n t", t=64)
                        opb(tr, "pool", [KT[h][tb], e1[i2]], [kh[i2]], lambda e: e.tensor_tensor(
                            out=kh[i2].t[:].rearrange("p (n t) -> p n t", t=64),
                            in0=KT[h][tb].t[:].rearrange("p (n t) -> p n t", t=64),
                            in1=e1v[:, :, 63:64].to_broadcast([128, 8, 64]), op=ALU.mult))
                    for h in hs:
                        i2 = h % 2
                        for ch in range(8):
                            opb(tr, "pe", [kh[i2], cK], [pst], lambda e: e.transpose(
                                pstT[0:64, ch * 128:(ch + 1) * 128], kh[i2].t[:, ch * 64:(ch + 1) * 64], self.ident_b[:]))
                        opb(tr, "act", [pst], [KK[h][tb]], lambda e: e.copy(
                            out=KK[h][tb].t[:], in_=pstT[0:64, :].rearrange("p (k n) -> p k n", k=8)))
                    yield
                for ch in range(8):
                    pv = psp[nxt("pp", 2)]
                    for kc in range(NCH):
                        opb(tr, "pe", [wh, ht[tb]], [pv], lambda e: e.matmul(
                            pv.t[0:64, :], lhsT=ht[tb].t[:, kc, ch * 64:(ch + 1) * 64], rhs=wh.t[:, kc, 1024:1536],
                            start=(kc == 0), stop=(kc == NCH - 1)))
                    if ch % 2 == 0:
                        opb(tr, "act", [pv], [VV[tb]], lambda e: e.copy(out=VV[tb].t[:, ch, :], in_=pv.t[0:64, :]))
                    else:
                        opb(tr, "dve", [pv], [VV[tb]], lambda e: e.tensor_copy(out=VV[tb].t[:, ch, :], in_=pv.t[0:64, :]))
                        yield

            def hB(j):
                t0 = j * 512
                tb = j % 2
                ob = O32[tb]

                def mmA(ch):
                    cols = slice(ch * 64, (ch + 1) * 64)
                    for h in range(4):
                        opb(tr, "pe", [KT[h][tb], QT[h][tb]], [psA], lambda e: e.matmul(
                            psA.t[0:64, h * 64:(h + 1) * 64], lhsT=KT[h][tb].t[:, cols], rhs=QT[h][tb].t[:, cols],
                            start=True, stop=True))
                    am = AM[nxt("am", 2)]
                    opb(tr, "dve", [psA, tri4], [am], lambda e: e.tensor_tensor(
                        out=am.t[:], in0=psA.t[0:64, 0:256], in1=tri4.t[:], op=ALU.mult))
                    return am

                def mmS(ch):
                    pS = psS[ch % 2]
                    for h in range(4):
                        opb(tr, "pe", [KK[h][tb], VV[tb]], [pS], lambda e: e.matmul(
                            pS.t[:, h * 128:(h + 1) * 128], lhsT=KK[h][tb].t[:, ch, :], rhs=VV[tb].t[:, ch, h * 128:(h + 1) * 128],
                            start=True, stop=True))

                def mmO(ch, am):
                    cols = slice(ch * 64, (ch + 1) * 64)
                    po = pso[nxt("po", 2)]
                    for h in range(4):
                        opb(tr, "pe", [VV[tb], am], [po], lambda e: e.matmul(
                            po.t[:, h * 64:(h + 1) * 64], lhsT=VV[tb].t[:, ch, h * 128:(h + 1) * 128], rhs=am.t[:, h * 64:(h + 1) * 64],
                            start=True, stop=False))
                        opb(tr, "pe", [STb[h], QT[h][tb]], [po], lambda e: e.matmul(
                            po.t[:, h * 64:(h + 1) * 64], lhsT=STb[h].t[:], rhs=QT[h][tb].t[:, cols],
                            start=False, stop=True))
                    opb(tr, "act", [po], [ob], lambda e: e.copy(
                        out=ob.t[:, :, cols], in_=po.t[:, 0:256].rearrange("p (h t) -> p h t", h=4)))

                def upd(ch):
                    pS = psS[ch % 2]
                    for h in range(4):
                        opb(tr, "dve", [pS, DEC[h][tb]], [ST[h]], lambda e: e.scalar_tensor_tensor(
                            out=ST[h].t[:], in0=ST[h].t[:], scalar=DEC[h][tb].t[:, ch:ch + 1], in1=pS.t[:, h * 128:(h + 1) * 128],
                            op0=ALU.mult, op1=ALU.add))
                        opb(tr, "act", [ST[h]], [STb[h]], lambda e: e.copy(out=STb[h].t[:], in_=ST[h].t[:]))

                am = mmA(0)
                mmS(0)
                for ch in range(8):
                    am_n = mmA(ch + 1) if ch + 1 < 8 else None
                    mmO(ch, am)
                    upd(ch)
                    if ch + 1 < 8:
                        mmS(ch + 1)
                    am = am_n
                    yield
                for h in range(4):
                    opb(tr, "act", [ob], [osq], lambda e: e.activation(out=osq.t[:], in_=ob.t[:, h, :], func=AF.Square))
                    opb(tr, "pe", [osq, cK], [psm], lambda e: e.matmul(
                        psm.t[:], lhsT=self.ones_b[:], rhs=osq.t[:], start=True, stop=True))
                    opb(tr, "act", [psm, cK], [rt], lambda e: e.activation(
                        out=rt.t[:], in_=psm.t[:], func=AF.Ln, bias=self.epsc(RMS_EPS), scale=1.0 / 128))
                    opb(tr, "act", [], [rt], lambda e: e.activation(out=rt.t[:], in_=rt.t[:], func=AF.Exp, scale=-0.5))
                    opb(tr, "pool", [ob, rt], [g1], lambda e: e.tensor_tensor(
                        out=g1.t[:], in0=ob.t[:, h, :], in1=rt.t[:], op=ALU.mult))
                    go = GO[nxt("go", 2)]
                    opb(tr, "dve", [g1, SG[h][tb], cK], [go], lambda e: e.scalar_tensor_tensor(
                        out=go.t[:], in0=g1.t[:], scalar=nwc(h), in1=SG[h][tb].t[:], op0=ALU.mult, op1=ALU.mult))
                    dmab(tr, "sp", "h_so%d" % (cnt["go"] % 2), [go], [],
                         self.mixT[512 + h * 128:512 + (h + 1) * 128, t0:t0 + 512], go.t[:])
                    yield

            interleave([hA(0)])
            for j in range(NT):
                gens = [hB(j)]
                if j + 1 < NT:
                    gens.append(hA(j + 1))
                interleave(gens)
            tr.barrier()

    def ffn(self, l, sub, src, dst, hdst=None):
        nc, tr, S = self.nc, self.tr, self.S
        is_ffn = sub != 1
        T = 256 if is_ffn else 512
        which = 0 if sub == 0 else 1
        base = l * 72 + sub * 24
        sh = lambda c: self.msc[:, base + c:base + c + 1]
        sc = lambda c: self.msc[:, base + 8 + c:base + 9 + c]
        gf = lambda c: self.msc[:, base + 16 + c:base + 17 + c]
        nb_ = l * 72 + (sub + 1) * 24
        sh2 = lambda c: self.msc[:, nb_ + c:nb_ + c + 1]
        sc2 = lambda c: self.msc[:, nb_ + 8 + c:nb_ + 9 + c]
        gcol = lambda c: self.vecs[:, V_LNG + (l * 3 + sub) * 8 + c:V_LNG + (l * 3 + sub) * 8 + c + 1]
        bcol = lambda c: self.vecs[:, V_LNB + (l * 3 + sub) * 8 + c:V_LNB + (l * 3 + sub) * 8 + c + 1]
        eps = LN_EPS / (ALPHA * ALPHA)
        nh = NHC if is_ffn else NCH
        with ExitStack() as es:
            if is_ffn:
                w_in = self.sb(es, "w_in", [128, NCH, 2 * DFF], BF16)
                ht = [self.sb(es, "f_h%d" % i, [128, NCH, T], BF16) for i in range(1)]
                sg = [self.sb(es, "f_sg%d" % i, [128, T], F32) for i in range(2)]
                psg = [self.ps(es, "f_pg%d" % i, [128, 512], F32) for i in range(4)]
            w_out = self.sb(es, "w_out", [128, nh, D], BF16)
            xt = [self.sb(es, "f_x%d" % i, [128, NCH, T], F32) for i in range(2)]
            hid = [self.sb(es, "f_hid%d" % i, [128, nh, T], BF16) for i in range(1 if is_ffn else 2)]
            zb = self.sb(es, "f_zb", [128, NCH, T], BF16)
            zq = self.sb(es, "f_zq", [128, NCH, T], BF16)
            st = self.sb(es, "f_st", [128, 4, T], F32)
            if hdst is not None:
                hb = [self.sb(es, "f_hb%d" % i, [128, NCH, T], BF16) for i in range(1)]
            psy = [self.ps(es, "f_py%d" % i, [128, 512], F32) for i in range(3)]
            pss = self.ps(es, "f_pss", [128, 2 * T], F32)
            wtok = []
            wtok2 = []
            if is_ffn:
                for m0 in range(0, NHC, 2):
                    tg = []
                    for half in range(2):
                        c0 = half * DFF + m0 * 128
                        tg.append(tr.dma("pool", None, "w_a%d_%d" % (m0 // 2, half), w_in[:, :, c0:c0 + 256],
                                         self.w_ffn_in[which][l, :, c0:c0 + 256].rearrange("(c p) n -> p c n", p=128)))
                    wtok.append(tg)
                for m0 in range(0, NHC, 2):
                    wtok2.append(tr.dma("pool", None, "w_b%d" % (m0 // 2), w_out[:, m0:m0 + 2, :],
                                        self.w_ffn_out[which][l, m0 * 128:(m0 + 2) * 128, :].rearrange("(m p) d -> p m d", p=128)))
            else:
                for m0 in range(0, NCH, 2):
                    wtok2.append(tr.dma("pool", None, "w_b%d" % (m0 // 2), w_out[:, m0:m0 + 2, :],
                                        self.w_mix_out[l, m0 * 128:(m0 + 2) * 128, :].rearrange("(m p) d -> p m d", p=128)))
            x_free = [None, None]
            h_free = [None]
            hid_free = [None, None]
            hb_free = [None, None]
            sg_free = [None, None]
            psg_free = [None] * 4
            psy_free = [None] * 3
            pss_free = None
            zb_free = None
            st_free = None
            ng = 0
            ny = 0
            nsg = 0
            nhb = 1 if is_ffn else 2

            def load_h(it):
                b = it % 2
                t0 = it * T
                tl = tr.dma("sp", x_free[b], "f_l%d" % b, xt[b][:],
                            src[:, t0:t0 + T].rearrange("(c p) t -> p c t", p=128))
                th = []
                if is_ffn:
                    for c in range(NCH):
                        th.append(tr.op("act", [tl, h_free[0], self.setup_tok], lambda e: e.activation(
                            out=ht[0][:, c, :], in_=xt[b][:, c, :], func=AF.Identity, bias=sh(c), scale=sc(c))))
                else:
                    hb_ = it % nhb
                    th = tr.dma("sp", hid_free[hb_], "f_lm%d" % hb_, hid[hb_][:],
                                self.mixT[:, t0:t0 + T].rearrange("(c p) t -> p c t", p=128))
                return tl, th

            Fv = {"ng": 0, "ny": 0, "nsg": 0, "pss": None, "zb": None, "st": None}
            NT_ = S // T

            def mm1_gen(th, hbi, thid):
                tm = None
                for m in range(NHC):
                    pb = Fv["ng"] % 4
                    Fv["ng"] += 1
                    for half in range(2):
                        col0 = half * DFF + m * 128
                        for c in range(NCH):
                            tm = tr.op("pe", [th[c], wtok[m // 2], psg_free[pb]], lambda e: e.matmul(
                                psg[pb][:, half * T:(half + 1) * T], lhsT=w_in[:, c, col0:col0 + 128], rhs=ht[0][:, c, :],
                                start=(c == 0), stop=(c == NCH - 1)))
                    sb_ = Fv["nsg"] % 2
                    Fv["nsg"] += 1
                    ts = tr.op("act", [tm, sg_free[sb_]], lambda e: e.activation(
                        out=sg[sb_][:], in_=psg[pb][:, 0:T], func=AF.Silu))
                    tu = tr.op("dve", [ts, tm, hid_free[hbi]], lambda e: e.tensor_tensor(
                        out=hid[hbi][:, m, :], in0=sg[sb_][:], in1=psg[pb][:, T:2 * T], op=ALU.mult))
                    psg_free[pb] = tu
                    sg_free[sb_] = tu
                    thid.append(tu)
                    h_free[0] = tm
                    yield

            def tail_gen(it, tz, tzb):
                b = it % 2
                t0 = it * T
                yield
                yield
                tm = None
                for half in range(2):
                    srcb = zb if half == 0 else zq
                    for dc in range(NCH):
                        tm = tr.op("pe", [tzb[dc], Fv["pss"], self.const_tok], lambda e: e.matmul(
                            pss[:, half * T:(half + 1) * T], lhsT=self.ones_b[:], rhs=srcb[:, dc, :],
                            start=(dc == 0), stop=(dc == NCH - 1)))
                Fv["zb"] = tm
                ta = tr.op("dve", [tm, Fv["st"]], lambda e: e.tensor_scalar(
                    out=st[:, 0:2, :], in0=pss[:].rearrange("p (a t) -> p a t", a=2), scalar1=1.0 / D, scalar2=None, op0=ALU.mult))
                Fv["pss"] = ta
                tb = tr.op("dve", ta, lambda e: e.tensor_tensor(out=st[:, 2, :], in0=st[:, 0, :], in1=st[:, 0, :], op=ALU.mult))
                tc = tr.op("dve", tb, lambda e: e.tensor_tensor(out=st[:, 3, :], in0=st[:, 1, :], in1=st[:, 2, :], op=ALU.subtract))
                tc2 = tr.op("act", tc, lambda e: e.activation(
                    out=st[:, 3, :], in_=st[:, 3, :], func=AF.Sqrt, bias=self.epsc(eps), scale=1.0))
                td = tr.op("dve", tc2, lambda e: e.reciprocal(out=st[:, 2, :], in_=st[:, 3, :]))
                yield
                touts = []
                thb = []
                for dc in range(NCH):
                    eng = "dve" if dc % 2 == 0 else "pool"
                    t1 = tr.op(eng, [td, tz[dc], tzb[dc]], lambda e: e.tensor_tensor(
                        out=xt[b][:, dc, :], in0=xt[b][:, dc, :], in1=st[:, 0, :], op=ALU.subtract))
                    t2 = tr.op(eng, t1, lambda e: e.tensor_tensor(
                        out=xt[b][:, dc, :], in0=xt[b][:, dc, :], in1=st[:, 2, :], op=ALU.mult))
                    t3 = tr.op("act", t2, lambda e: e.activation(
                        out=xt[b][:, dc, :], in_=xt[b][:, dc, :], func=AF.Identity, bias=bcol(dc), scale=gcol(dc)))
                    touts.append(t3)
                    if hdst is not None:
                        thb.append(tr.op("act", [t3, hb_free[0]], lambda e: e.activation(
                            out=hb[0][:, dc, :], in_=xt[b][:, dc, :], func=AF.Identity, bias=sh2(dc), scale=sc2(dc))))
                    yield
                Fv["st"] = touts
                tst = tr.dma("sp", touts, "f_s%d" % b,
                             dst[:, t0:t0 + T].rearrange("(c p) t -> p c t", p=128), xt[b][:])
                x_free[b] = [tst] + thb
                if hdst is not None:
                    hb_free[0] = tr.dma("sp", thb, "f_sh",
                                        hdst[:, t0:t0 + T].rearrange("(c p) t -> p c t", p=128), hb[0][:])
                yield

            pre = load_h(0)
            pending = None
            for it in range(NT_):
                b = it % 2
                hbi = it % nhb
                tl, th = pre
                gens = []
                if is_ffn:
                    thid = []
                    gens.append(mm1_gen(th, hbi, thid))
                else:
                    thid = [th] * nh
                if pending is not None:
                    gens.append(pending)
                interleave(gens)
                if it + 1 < NT_:
                    pre = load_h(it + 1)
                tz = []
                tzb = []
                tm = None
                for dc in range(NCH):
                    pb = Fv["ny"] % 3
                    Fv["ny"] += 1
                    for m in range(nh):
                        tm = tr.op("pe", [thid[m], wtok2[m // 2], psy_free[pb]], lambda e: e.matmul(
                            psy[pb][:, 0:T], lhsT=w_out[:, m, dc * 128:(dc + 1) * 128], rhs=hid[hbi][:, m, :],
                            start=(m == 0), stop=(m == nh - 1)))
                    t1 = tr.op("dve", [tm, tl, self.setup_tok], lambda e: e.scalar_tensor_tensor(
                        out=xt[b][:, dc, :], in0=psy[pb][:, 0:T], scalar=gf(dc), in1=xt[b][:, dc, :],
                        op0=ALU.mult, op1=ALU.add))
                    psy_free[pb] = t1
                    tz.append(t1)
                    t2 = tr.op("act", [t1, Fv["zb"]], lambda e: e.copy(out=zb[:, dc, :], in_=xt[b][:, dc, :]))
                    t3 = tr.op("act", [t1, Fv["zb"]], lambda e: e.activation(
                        out=zq[:, dc, :], in_=xt[b][:, dc, :], func=AF.Square))
                    tzb.append([t2, t3])
                hid_free[hbi] = tm
                pending = tail_gen(it, tz, tzb)
            interleave([pending])
            tr.barrier()


NEG = -30000.0


def host_consts():
    c = np.zeros((128, 1024), np.float32)
    c[:, 0:128] = np.eye(128, dtype=np.float32)
    p = np.arange(128)
    ch = p % 64
    invf = np.where(ch < 16, np.float32(ROPE_THETA) ** (-(ch % 8).astype(np.float32) * np.float32(2.0 / 16)), 0.0)
    c[:, 128] = invf.astype(np.float32)
    c[:, 129] = math.pi / 2
    j = p[:, None]
    i = p[None, :]
    c[:, 256:384] = np.where(j >= i, 0.0, NEG)
    c[:, 384:512] = np.where(j <= i, 0.0, NEG)
    c[:, 512:640] = NEG
    s_ = (p % 64)[:, None]
    t_ = np.arange(64)[None, :]
    c[:, 640:704] = (s_ <= t_).astype(np.float32)
    return c


def pack_vecs(b, c, ln_g, ln_b, ada_b, hgrn_norm_w, hgrn_lb_logits):
    v = np.zeros((NVEC, 128), np.float32)
    v[V_ADAB:V_ADAB + 144] = ada_b.reshape(144, 128)
    v[V_LNG:V_LNG + 48] = ln_g.reshape(48, 128)
    v[V_LNB:V_LNB + 48] = ln_b.reshape(48, 128)
    v[V_C:V_C + 8] = c[b].reshape(8, 128)
    v[V_NW:V_NW + 8] = hgrn_norm_w.reshape(8, 128)
    v[V_LB:V_LB + 8] = hgrn_lb_logits.reshape(8, 128)
    return v


def make_in_maps(inputs, ncores, S):
    g = lambda k: np.ascontiguousarray(np.asarray(inputs[k]))
    consts = host_consts()
    maps = []
    for b in range(ncores):
        m = {
            "x": np.ascontiguousarray(g("x")[b, :S]),
            "pos": np.ascontiguousarray(g("positions")[b, :S].reshape(1, S).astype(np.int32)),
            "vecs": pack_vecs(b, g("c"), g("ln_g"), g("ln_b"), g("ada_b"), g("hgrn_norm_w"), g("hgrn_lb_logits")),
            "consts": consts,
            "ada_w": g("ada_w"),
            "ffn1_w_in": g("ffn1_w_in"), "ffn2_w_in": g("ffn2_w_in"),
            "ffn1_w_out": g("ffn1_w_out"), "ffn2_w_out": g("ffn2_w_out"),
            "mix_w_in": g("mix_w_in"), "mix_w_out": g("mix_w_out"),
        }
        maps.append(m)
    return maps


def kernel(**inputs):
    S = 8192
    nc = Builder(S).build()
    maps = make_in_maps(inputs, 8, S)
    res = run_bass_kernel_spmd(nc, maps, core_ids=list(range(8)))
    return np.stack([r["out"] for r in res.results], axis=0)
```

```python
import math
from contextlib import ExitStack
import numpy as np
import concourse.bass as bass
import concourse.mybir as mybir
from concourse.bass_utils import run_bass_kernel_spmd

F32 = mybir.dt.float32
BF16 = mybir.dt.bfloat16
I32 = mybir.dt.int32
AF = mybir.ActivationFunctionType
ALU = mybir.AluOpType

D = 1024
NCH = 8
DEPTH = 2
DFF = 2816
NHC = DFF // 128
IN_COLS = 3584
ALPHA = (2 * DEPTH) ** 0.25
LN_EPS = 1e-5
RMS_EPS = 1e-6
ROPE_THETA = 500000.0

V_ADAB = 0
V_LNG = 144
V_LNB = 192
V_C = 240
V_NW = 248
V_LB = 256
NVEC = 264


class Tracker:
    def __init__(self, nc, es):
        self.nc = nc
        self.es = es
        self.sems = {}
        self.cnt = {}
        self.seen = {}
        self.engs = {"pe": nc.tensor, "act": nc.scalar, "dve": nc.vector, "pool": nc.gpsimd, "sp": nc.sync}
        for n in self.engs:
            self.sems[n] = es.enter_context(nc.semaphore("s_" + n))
            self.cnt[n] = 0
            self.seen[n] = {}
        self.dsems = {}

    def dsem(self, name):
        if name not in self.dsems:
            s = self.es.enter_context(self.nc.semaphore("d_" + name))
            self.dsems[name] = [s, 0]
        return self.dsems[name]

    def wait(self, en, tok):
        if tok is None:
            return
        if isinstance(tok, list):
            for t in tok:
                self.wait(en, t)
            return
        key, sem, val = tok
        if self.seen[en].get(key, 0) >= val:
            return
        self.engs[en].wait_ge(sem, val)
        self.seen[en][key] = val

    def op(self, en, deps, inst_fn):
        self.wait(en, deps)
        inst = inst_fn(self.engs[en])
        self.cnt[en] += 1
        inst.then_inc(self.sems[en], 1)
        return ("e_" + en, self.sems[en], self.cnt[en])

    def last(self, en):
        if self.cnt[en] == 0:
            return None
        return ("e_" + en, self.sems[en], self.cnt[en])

    def dma(self, en, deps, dname, out, in_):
        self.wait(en, deps)
        d = self.dsem(dname)
        self.engs[en].dma_start(out=out, in_=in_).then_inc(d[0], 16)
        d[1] += 16
        return ("d_" + dname, d[0], d[1])

    def all_tokens(self):
        toks = [self.last(n) for n in self.engs if self.cnt[n]]
        toks += [("d_" + k, v[0], v[1]) for k, v in self.dsems.items() if v[1]]
        return toks

    def barrier(self):
        toks = self.all_tokens()
        for en in self.engs:
            self.wait(en, toks)


class Buf:
    def __init__(self, t):
        self.t = t
        self.w = {}
        self.r = {}


def _deps(reads, writes, deps, en=None):
    d = []
    for b in reads:
        d += list(b.w.values())
    for b in writes:
        d += list(b.w.values()) + list(b.r.values())
    if deps:
        d += deps if isinstance(deps, list) else [deps]
    if en == "pe":
        d = [t for t in d if t[0] != "e_pe"]
    return d


def _mark(tok, reads, writes):
    for b in reads:
        if b not in writes:
            b.r[tok[0]] = tok
    for b in writes:
        b.w[tok[0]] = tok
        b.r = {}


def opb(tr, en, reads, writes, fn, deps=None):
    tok = tr.op(en, _deps(reads, writes, deps, en), fn)
    _mark(tok, reads, writes)
    return tok


def dmab(tr, en, dname, reads, writes, out, in_, deps=None):
    tok = tr.dma(en, _deps(reads, writes, deps), dname, out, in_)
    _mark(tok, reads, writes)
    return tok


def interleave(gens):
    gens = list(gens)
    while gens:
        for g in list(gens):
            try:
                next(g)
            except StopIteration:
                gens.remove(g)


class Builder:
    def __init__(self, S, layers=DEPTH, debug_outs=(), stop_after=None):
        self.S = S
        self.layers = layers
        self.debug_outs = debug_outs
        self.stop_after = stop_after

    def sb(self, es, name, shape, dt):
        self.uid = getattr(self, "uid", 0) + 1
        return es.enter_context(self.nc.sbuf_tensor("sb%d_%s" % (self.uid, name), shape, dt))

    def ps(self, es, name, shape, dt=F32):
        self.uid = getattr(self, "uid", 0) + 1
        return es.enter_context(self.nc.psum_tensor("ps%d_%s" % (self.uid, name), shape, dt))

    def build(self):
        S = self.S
        nc = bass.Bass("TRN2", target_bir_lowering=False)
        self.nc = nc
        dt_ = nc.dram_tensor
        self.x_in = dt_("x", [S, D], F32, kind="ExternalInput").ap()
        self.pos_in = dt_("pos", [1, S], I32, kind="ExternalInput").ap()
        self.vecs_in = dt_("vecs", [NVEC, 128], F32, kind="ExternalInput").ap()
        self.consts_in = dt_("consts", [128, 1024], F32, kind="ExternalInput").ap()
        self.ada_w = dt_("ada_w", [DEPTH, D, 9 * D], F32, kind="ExternalInput").ap()
        self.w_ffn_in = [dt_("ffn1_w_in", [DEPTH, D, 2 * DFF], F32, kind="ExternalInput").ap(),
                         dt_("ffn2_w_in", [DEPTH, D, 2 * DFF], F32, kind="ExternalInput").ap()]
        self.w_ffn_out = [dt_("ffn1_w_out", [DEPTH, DFF, D], F32, kind="ExternalInput").ap(),
                          dt_("ffn2_w_out", [DEPTH, DFF, D], F32, kind="ExternalInput").ap()]
        self.w_mix_in = dt_("mix_w_in", [DEPTH, D, IN_COLS], F32, kind="ExternalInput").ap()
        self.w_mix_out = dt_("mix_w_out", [DEPTH, D, D], F32, kind="ExternalInput").ap()
        self.out = dt_("out", [S, D], F32, kind="ExternalOutput").ap()

        def scratch(name, shape, dt):
            kind = "ExternalOutput" if name in self.debug_outs else "Internal"
            return dt_(name, shape, dt, kind=kind).ap()
        self.xT = [scratch("xT0", [D, S], F32), scratch("xT1", [D, S], F32)]
        self.mixT = scratch("mixT", [D, S], BF16)
        self.hT = scratch("hT", [D, S], BF16)
        self.cosT = scratch("cosT", [128, S], F32)
        self.sinT = scratch("sinT", [128, S], F32)

        with ExitStack() as es:
            self.tr = Tracker(nc, es)
            self.setup(es)
            self.run_phases()
            self.tr.barrier()
        return nc

    def setup(self, es):
        nc, tr = self.nc, self.tr
        self.ident_f = self.sb(es, "ident_f", [128, 128], F32)
        self.ident_b = self.sb(es, "ident_b", [128, 128], BF16)
        self.ones_b = self.sb(es, "ones_b", [128, 128], BF16)
        self.cst = self.sb(es, "cst", [128, 1024], F32)
        self.vecs = self.sb(es, "vecs", [128, NVEC], F32)
        self.ada = self.sb(es, "ada", [128, DEPTH * 72], F32)
        self.msc = self.sb(es, "msc", [128, DEPTH * 72], F32)
        t0 = tr.dma("sp", None, "setup_c", self.cst[:], self.consts_in)
        tok = tr.op("dve", t0, lambda e: e.tensor_copy(out=self.ident_f[:], in_=self.cst[:, 0:128]))
        tokb = tr.op("dve", tok, lambda e: e.tensor_copy(out=self.ident_b[:], in_=self.cst[:, 0:128]))
        tok1 = tr.op("pool", None, lambda e: e.memset(self.ones_b[:], 1.0))
        self.lbv = self.sb(es, "lbv", [128, 8], F32)
        self.oml = self.sb(es, "oml", [128, 8], F32)
        self.epst = self.sb(es, "epst", [128, 4], F32)
        tok2 = tr.op("pool", None, lambda e: e.memset(self.epst[:, 0:1], LN_EPS / (ALPHA * ALPHA)))
        tok3 = tr.op("pool", None, lambda e: e.memset(self.epst[:, 1:2], RMS_EPS))
        self.const_tok = [tok, tokb, tok1, tok2, tok3, t0]
        with ExitStack() as es2:
            vraw = self.sb(es2, "vraw", [128, 3, 128], F32)
            pst = self.ps(es2, "pst", [128, 512], F32)
            psa = self.ps(es2, "psa", [128, 512], F32)
            cond = self.sb(es2, "cond", [128, 8], F32)
            wbuf = [self.sb(es2, "adaw%d" % i, [128, 8, 1152], F32) for i in range(2)]
            nrow = [128, 128, NVEC - 256]
            toks = []
            for i in range(3):
                toks.append(tr.dma("sp", None, "setup_v%d" % i, vraw[0:nrow[i], i, :],
                                   self.vecs_in[i * 128:i * 128 + nrow[i], :]))
            tp = None
            for i in range(3):
                tp = tr.op("pe", [toks[i], tok], lambda e: e.transpose(
                    pst[:, i * 128:i * 128 + nrow[i]], vraw[0:nrow[i], i, :], self.ident_f[0:nrow[i], 0:nrow[i]]))
            tv = tr.op("dve", tp, lambda e: e.tensor_copy(out=self.vecs[:], in_=pst[:, 0:NVEC]))
            tc = tr.op("act", tv, lambda e: e.activation(out=cond[:], in_=self.vecs[:, V_C:V_C + 8], func=AF.Silu))
            free = [None, None]
            blk = 0
            for l in range(self.layers):
                for cb in range(8):
                    buf = wbuf[blk % 2]
                    tl = tr.dma("sp", free[blk % 2], "adaw%d" % (blk % 2), buf[:],
                                self.ada_w[l, :, cb * 1152:(cb + 1) * 1152].rearrange("(c p) n -> p c n", p=128))
                    tm = None
                    for j in range(9):
                        col = l * 72 + cb * 9 + j
                        for kc in range(8):
                            tm = tr.op("pe", [tl, tc], lambda e: e.matmul(
                                psa[:, col:col + 1], lhsT=buf[:, kc, j * 128:(j + 1) * 128], rhs=cond[:, kc:kc + 1],
                                start=(kc == 0), stop=(kc == 7)))
                    free[blk % 2] = tm
                    blk += 1
            n = self.layers * 72
            ta = tr.op("dve", [tm, tv], lambda e: e.tensor_tensor(
                out=self.ada[:, 0:n], in0=psa[:, 0:n], in1=self.vecs[:, V_ADAB:V_ADAB + n], op=ALU.add))
            tlast = ta
            for l in range(self.layers):
                for sub in range(3):
                    base = l * 72 + sub * 24
                    rw = 1.0 if sub == 1 else 0.5
                    t1 = tr.op("dve", ta, lambda e: e.tensor_copy(out=self.msc[:, base:base + 8], in_=self.ada[:, base:base + 8]))
                    t2 = tr.op("dve", ta, lambda e: e.tensor_scalar(
                        out=self.msc[:, base + 8:base + 16], in0=self.ada[:, base + 8:base + 16],
                        scalar1=1.0, scalar2=None, op0=ALU.add))
                    tlast = tr.op("dve", ta, lambda e: e.tensor_scalar(
                        out=self.msc[:, base + 16:base + 24], in0=self.ada[:, base + 16:base + 24],
                        scalar1=1.0, scalar2=rw / ALPHA, op0=ALU.add, op1=ALU.mult))
            self.setup_tok = [tlast, t1, t2, tv]
            self.dump("ada", self.ada[:], [128, DEPTH * 72])
            self.dump("msc", self.msc[:], [128, DEPTH * 72])
            self.dump("vecs", self.vecs[:], [128, NVEC])
            tr.barrier()

    def dump(self, name, ap, shape, dt=F32):
        if not self.debug_outs:
            return
        d = self.nc.dram_tensor("dbg_" + name, list(shape), dt, kind="ExternalOutput").ap()
        self.tr.dma("sp", self.tr.all_tokens(), "dbg", d, ap)

    def epsc(self, eps):
        return self.epst[:, 0:1] if eps > 2e-6 else self.epst[:, 1:2]

    def run_phases(self):
        self.rope_setup()
        self.transpose_in()
        cur = 0
        for l in range(self.layers):
            self.ffn(l, 0, self.xT[cur], self.xT[1 - cur], hdst=self.hT)
            cur = 1 - cur
            if self.stop_after == ("ffn", l, 0):
                break
            self.attn(l)
            if self.stop_after == ("attn", l):
                break
            self.hgrn(l)
            if self.stop_after == ("hgrn", l):
                break
            self.ffn(l, 1, self.xT[cur], self.xT[1 - cur])
            cur = 1 - cur
            if self.stop_after == ("mix", l):
                break
            self.ffn(l, 2, self.xT[cur], self.xT[1 - cur])
            cur = 1 - cur
        self.transpose_out(self.xT[cur])

    def transpose_in(self):
        nc, tr, S = self.nc, self.tr, self.S
        with ExitStack() as es:
            xin = [self.sb(es, "ti_x%d" % i, [128, 4, D], F32) for i in range(2)]
            xo = [self.sb(es, "ti_o%d" % i, [128, NCH, 512], F32) for i in range(2)]
            pss = [self.ps(es, "ti_p%d" % i, [128, 512], F32) for i in range(4)]
            ps_free = [None] * 4
            xin_free = [None] * 2
            xo_free = [None] * 2
            nb = 0
            for it in range(S // 512):
                b = it % 2
                tl = tr.dma("sp", xin_free[b], "ti_l%d" % b, xin[b][:],
                            self.x_in[it * 512:(it + 1) * 512, :].rearrange("(j p) d -> p j d", p=128))
                evs = []
                for c in range(NCH):
                    pb = nb % 4
                    nb += 1
                    tp = None
                    for j in range(4):
                        tp = tr.op("pe", [tl, ps_free[pb], self.const_tok], lambda e: e.transpose(
                            pss[pb][:, j * 128:(j + 1) * 128], xin[b][:, j, c * 128:(c + 1) * 128], self.ident_f[:]))
                    eng = "act" if c % 2 == 0 else "dve"
                    if eng == "act":
                        te = tr.op("act", [tp, xo_free[b]], lambda e: e.copy(out=xo[b][:, c, :], in_=pss[pb][:]))
                    else:
                        te = tr.op("dve", [tp, xo_free[b]], lambda e: e.tensor_copy(out=xo[b][:, c, :], in_=pss[pb][:]))
                    ps_free[pb] = te
                    evs.append(te)
                xin_free[b] = tp
                ts = tr.dma("pool", evs, "ti_s%d" % b,
                            self.xT[0][:, it * 512:(it + 1) * 512].rearrange("(c p) t -> p c t", p=128), xo[b][:])
                xo_free[b] = ts
            tr.barrier()

    def transpose_out(self, src):
        nc, tr, S = self.nc, self.tr, self.S
        with ExitStack() as es:
            xin = [self.sb(es, "to_x%d" % i, [128, NCH, 512], F32) for i in range(2)]
            xo = [self.sb(es, "to_o%d" % i, [128, 4, D], F32) for i in range(2)]
            pss = [self.ps(es, "to_p%d" % i, [128, 512], F32) for i in range(4)]
            ps_free = [None] * 4
            xin_free = [None] * 2
            xo_free = [None] * 2
            nb = 0
            for it in range(S // 512):
                b = it % 2
                tl = tr.dma("sp", xin_free[b], "to_l%d" % b, xin[b][:],
                            src[:, it * 512:(it + 1) * 512].rearrange("(c p) t -> p c t", p=128))
                evs = []
                for j in range(4):
                    for h in range(2):
                        pb = nb % 4
                        nb += 1
                        tp = None
                        for cc in range(4):
                            c = h * 4 + cc
                            tp = tr.op("pe", [tl, ps_free[pb], self.const_tok], lambda e: e.transpose(
                                pss[pb][:, cc * 128:(cc + 1) * 128], xin[b][:, c, j * 128:(j + 1) * 128], self.ident_f[:]))
                        if (j + h) % 2 == 0:
                            te = tr.op("act", [tp, xo_free[b]], lambda e: e.copy(
                                out=xo[b][:, j, h * 512:(h + 1) * 512], in_=pss[pb][:]))
                        else:
                            te = tr.op("dve", [tp, xo_free[b]], lambda e: e.tensor_copy(
                                out=xo[b][:, j, h * 512:(h + 1) * 512], in_=pss[pb][:]))
                        ps_free[pb] = te
                        evs.append(te)
                xin_free[b] = tp
                ts = tr.dma("pool", evs, "to_s%d" % b,
                            self.out[it * 512:(it + 1) * 512, :].rearrange("(j p) d -> p j d", p=128), xo[b][:])
                xo_free[b] = ts
            tr.barrier()

    def rope_setup(self):
        nc, tr, S = self.nc, self.tr, self.S
        CW = min(2048, S)
        PI = math.pi
        with ExitStack() as es:
            posi = Buf(self.sb(es, "r_pi", [128, CW], I32))
            ang = Buf(self.sb(es, "r_ang", [128, CW], F32))
            m1 = Buf(self.sb(es, "r_m1", [128, CW], F32))
            m2 = Buf(self.sb(es, "r_m2", [128, CW], F32))
            cK = Buf(None)
            cK.w = {t[0]: t for t in self.const_tok}
            invf = self.cst[:, 128:129]
            halfpi = self.cst[:, 129:130]
            for i in range(S // CW):
                a, b = i * CW, (i + 1) * CW
                dmab(tr, "sp", "r_l", [], [posi], posi.t[:], self.pos_in[0:1, a:b].partition_broadcast(128))
                opb(tr, "dve", [posi], [ang], lambda e: e.tensor_copy(out=ang.t[:], in_=posi.t[:]))
                opb(tr, "dve", [cK], [ang], lambda e: e.tensor_scalar(
                    out=ang.t[:], in0=ang.t[:], scalar1=invf, scalar2=None, op0=ALU.mult))
                C1 = 6.28125
                C2 = 2 * PI - C1
                opb(tr, "dve", [ang], [m1], lambda e: e.tensor_scalar(
                    out=m1.t[:], in0=ang.t[:], scalar1=1.0 / (2 * PI), scalar2=None, op0=ALU.mult))
                opb(tr, "dve", [m1], [posi], lambda e: e.tensor_copy(out=posi.t[:], in_=m1.t[:]))
                opb(tr, "dve", [posi], [m1], lambda e: e.tensor_copy(out=m1.t[:], in_=posi.t[:]))
                opb(tr, "dve", [m1], [ang], lambda e: e.scalar_tensor_tensor(
                    out=ang.t[:], in0=m1.t[:], scalar=-C1, in1=ang.t[:], op0=ALU.mult, op1=ALU.add))
                opb(tr, "dve", [m1], [ang], lambda e: e.scalar_tensor_tensor(
                    out=ang.t[:], in0=m1.t[:], scalar=-C2, in1=ang.t[:], op0=ALU.mult, op1=ALU.add))
                opb(tr, "dve", [], [ang], lambda e: e.tensor_scalar(
                    out=ang.t[:], in0=ang.t[:], scalar1=-PI, scalar2=PI, op0=ALU.max, op1=ALU.min))
                opb(tr, "dve", [ang], [m2], lambda e: e.scalar_tensor_tensor(
                    out=m2.t[:], in0=ang.t[:], scalar=-1.0, in1=ang.t[:], op0=ALU.mult, op1=ALU.max))
                opb(tr, "act", [ang], [m1], lambda e: e.activation(out=m1.t[:], in_=ang.t[:], func=AF.Sin))
                opb(tr, "act", [cK], [m2], lambda e: e.activation(out=m2.t[:], in_=m2.t[:], func=AF.Sin, bias=halfpi, scale=-1.0))
                dmab(tr, "sp", "r_s1", [m1], [], self.sinT[:, a:b], m1.t[:])
                dmab(tr, "sp", "r_s2", [m2], [], self.cosT[:, a:b], m2.t[:])
            t1 = tr.op("pool", self.setup_tok, lambda e: e.memset(self.lbv[:], 0.0))
            t2 = tr.op("dve", [t1] + self.setup_tok, lambda e: e.tensor_tensor(
                out=self.lbv[:, 4:8], in0=self.vecs[:, V_LB + 4:V_LB + 8], in1=self.vecs[:, V_LB:V_LB + 4], op=ALU.subtract))
            t3 = tr.op("act", t2, lambda e: e.activation(out=self.lbv[:, 4:8], in_=self.lbv[:, 4:8], func=AF.Sigmoid))
            t4 = tr.op("dve", t3, lambda e: e.tensor_scalar(
                out=self.oml[:], in0=self.lbv[:], scalar1=-1.0, scalar2=1.0, op0=ALU.mult, op1=ALU.add))
            self.lb_tok = [t3, t4]
            tr.barrier()

    def attn(self, l):
        nc, tr, S = self.nc, self.tr, self.S
        NSEG = S // 2048
        with ExitStack() as es:
            B = lambda name, shape, dt: Buf(self.sb(es, name, shape, dt))
            P = lambda name, shape, dt=F32: Buf(self.ps(es, name, shape, dt))
            cK = Buf(None)
            cK.w = {t[0]: t for t in self.const_tok}
            wq = B("a_wq", [128, NCH, 640], BF16)
            ht = [B("a_ht%d" % i, [128, NCH, 512], BF16) for i in range(2)]
            cs = [B("a_cs%d" % i, [128, 512], F32) for i in range(2)]
            sn = [B("a_sn%d" % i, [128, 512], F32) for i in range(2)]
            t1b = [B("a_t1%d" % i, [128, 512], F32) for i in range(2)]
            t2b = [B("a_t2%d" % i, [128, 512], F32) for i in range(2)]
            qs = [[B("a_q%d%d" % (p, o), [128, 2048], BF16) for o in range(3)] for p in range(2)]
            ks = [[B("a_k%d%d" % (p, o), [128, 2048], BF16) for o in range(3)] for p in range(2)]
            vt = [B("a_vt%d" % o, [128, 2048], BF16) for o in range(3)]
            V = [[B("a_V%d%d" % (p, o), [128, 16, 128], BF16) for o in range(3)] for p in range(2)]
            pT = [B("a_pT%d" % i, [128, 512], BF16) for i in range(4)]
            accN = B("a_accN", [128, 2048], F32)
            accD = B("a_accD", [128, 2048], F32)
            ao = [B("a_ao%d" % i, [128, 2048], BF16) for i in range(2)]
            masks = [B("a_mk%d" % i, [128, 512], BF16) for i in range(3)]
            psp = [P("a_pp%d" % i, [128, 512]) for i in range(2)]
            pss = [P("a_ps%d" % i, [128, 512]) for i in range(4)]
            psn = [P("a_pn%d" % i, [128, 512]) for i in range(2)]
            pst = psp[0]
            pstT = psp[0].t.bitcast(BF16)
            mP, mC, mN = self.cst[:, 256:384], self.cst[:, 384:512], self.cst[:, 512:640]
            for mi, pat in enumerate([(mP, mC, mP, mC), (mN, mC, mP, mC), (mN, mC, mN, mC)]):
                for q_, src_ in enumerate(pat):
                    opb(tr, "dve", [cK], [masks[mi]], lambda e: e.tensor_copy(
                        out=masks[mi].t[:, q_ * 128:(q_ + 1) * 128], in_=src_))
            cnt = {"pp": 0, "ps": 0, "pn": 0, "pT": 0, "tile": 0, "ao": 0}

            def nxt(k, n):
                v = cnt[k] % n
                cnt[k] += 1
                return v

            prev_done = {}

            def phaseA(c, s_, gate=None):
                par = s_ % 2
                if s_ == 0:
                    for g, col0 in ((0, c * 128), (2, 512 + c * 128), (4, 1024 + c * 128)):
                        dmab(tr, "pool", "a_w%d" % g, [], [wq], wq.t[:, :, g * 128:(g + 1) * 128],
                             self.w_mix_in[l, :, col0:col0 + 128].rearrange("(k p) n -> p k n", p=128))
                    for g in (0, 2):
                        opb(tr, "pool", [], [wq], lambda e: e.memset(wq.t[:, :, (g + 1) * 128:(g + 2) * 128], 0.0))
                        for hh in range(2):
                            so = g * 128 + hh * 64
                            do = (g + 1) * 128 + hh * 64
                            opb(tr, "dve", [], [wq], lambda e: e.tensor_scalar(
                                out=wq.t[:, :, do:do + 8], in0=wq.t[:, :, so + 8:so + 16], scalar1=-1.0, scalar2=None, op0=ALU.mult))
                            opb(tr, "dve", [], [wq], lambda e: e.tensor_copy(
                                out=wq.t[:, :, do + 8:do + 16], in_=wq.t[:, :, so:so + 8]))
                    yield
                for jl in range(4):
                    t0 = s_ * 2048 + jl * 512
                    hb = nxt("tile", 2)
                    dmab(tr, "sp", "a_lh%d" % hb, [], [ht[hb]], ht[hb].t[:],
                         self.hT[:, t0:t0 + 512].rearrange("(k p) t -> p k t", p=128))
                    dmab(tr, "sp", "a_lc%d" % hb, [], [cs[hb]], cs[hb].t[:], self.cosT[:, t0:t0 + 512])
                    dmab(tr, "sp", "a_ls%d" % hb, [], [sn[hb]], sn[hb].t[:], self.sinT[:, t0:t0 + 512])
                    nat = slice(jl * 512, (jl + 1) * 512)
                    for gi, (g, dest) in enumerate(((0, qs[par]), (2, ks[par]))):
                        while gi == 1 and gate is not None and not prev_done.get(gate, False):
                            yield
                        pa = psp[nxt("pp", 2)]
                        for kc in range(NCH):
                            opb(tr, "pe", [wq, ht[hb]], [pa], lambda e: e.matmul(
                                pa.t[:], lhsT=wq.t[:, kc, g * 128:(g + 1) * 128], rhs=ht[hb].t[:, kc, :],
                                start=(kc == 0), stop=(kc == NCH - 1)))
                        opb(tr, "dve", [pa, cs[hb]], [t1b[gi]], lambda e: e.tensor_tensor(
                            out=t1b[gi].t[:], in0=pa.t[:], in1=cs[hb].t[:], op=ALU.mult))
                        pb = psp[nxt("pp", 2)]
                        for kc in range(NCH):
                            opb(tr, "pe", [wq, ht[hb]], [pb], lambda e: e.matmul(
                                pb.t[:], lhsT=wq.t[:, kc, (g + 1) * 128:(g + 2) * 128], rhs=ht[hb].t[:, kc, :],
                                start=(kc == 0), stop=(kc == NCH - 1)))
                        opb(tr, "dve", [pb, sn[hb]], [t2b[gi]], lambda e: e.tensor_tensor(
                            out=t2b[gi].t[:], in0=pb.t[:], in1=sn[hb].t[:], op=ALU.mult))
                        opb(tr, "pool", [t1b[gi], t2b[gi]], [dest[0]], lambda e: e.tensor_tensor(
                            out=dest[0].t[:, nat], in0=t1b[gi].t[:], in1=t2b[gi].t[:], op=ALU.add))
                        opb(tr, "pool", [dest[0]], [dest[1]], lambda e: e.tensor_copy(
                            out=dest[1].t[:].rearrange("p (r n i) -> p r n i", r=4, n=4)[:, :, jl, :],
                            in_=dest[0].t[:, nat].rearrange("p (i r) -> p r i", r=4)))
                        opb(tr, "act", [dest[0]], [dest[2]], lambda e: e.copy(
                            out=dest[2].t[:].rearrange("p (r j i) -> p r j i", r=16, j=4)[:, :, jl, :],
                            in_=dest[0].t[:, nat].rearrange("p (i r) -> p r i", r=16)))
                        yield
                    pv = psp[nxt("pp", 2)]
                    for kc in range(NCH):
                        opb(tr, "pe", [wq, ht[hb]], [pv], lambda e: e.matmul(
                            pv.t[:], lhsT=wq.t[:, kc, 512:640], rhs=ht[hb].t[:, kc, :],
                            start=(kc == 0), stop=(kc == NCH - 1)))
                    opb(tr, "act", [pv], [vt[0]], lambda e: e.copy(out=vt[0].t[:, nat], in_=pv.t[:]))
                    opb(tr, "pool", [vt[0]], [vt[1]], lambda e: e.tensor_copy(
                        out=vt[1].t[:].rearrange("p (r n i) -> p r n i", r=4, n=4)[:, :, jl, :],
                        in_=vt[0].t[:, nat].rearrange("p (i r) -> p r i", r=4)))
                    opb(tr, "dve", [vt[0]], [vt[2]], lambda e: e.tensor_copy(
                        out=vt[2].t[:].rearrange("p (r j i) -> p r j i", r=16, j=4)[:, :, jl, :],
                        in_=vt[0].t[:, nat].rearrange("p (i r) -> p r i", r=16)))
                    yield
                for o in range(3):
                    for half in range(2):
                        for k_ in range(8):
                            blk = half * 8 + k_
                            opb(tr, "pe", [vt[o], cK], [pst], lambda e: e.transpose(
                                pstT[:, k_ * 128:(k_ + 1) * 128], vt[o].t[:, blk * 128:(blk + 1) * 128], self.ident_b[:]))
                        opb(tr, "act" if half == 0 else "dve", [pst], [V[par][o]],
                            (lambda e: e.copy(out=V[par][o].t[:, half * 8:(half + 1) * 8, :],
                                              in_=pstT[:].rearrange("p (k n) -> p k n", k=8))) if half == 0 else
                            (lambda e: e.tensor_copy(out=V[par][o].t[:, half * 8:(half + 1) * 8, :],
                                                     in_=pstT[:].rearrange("p (k n) -> p k n", k=8))))
                        yield

            def phaseB(c, s_):
                par = s_ % 2
                order = ([(2, p_) for p_ in range(8)] + [(1, p_) for p_ in (0, 2, 4, 6)] + [(0, 0)]
                         + [(1, p_) for p_ in (1, 3, 5, 7)] + [(0, p_) for p_ in range(1, 8)])

                def scores(o, lbp):
                    prevs = []
                    for bb in range(2):
                        lb = lbp * 2 + bb
                        if o == 0:
                            pv_ = (par, lb - 1) if lb > 0 else ((1 - par, 15) if s_ > 0 else None)
                        elif o == 1:
                            pv_ = (par, lb - 1) if lb % 4 > 0 else ((1 - par, lb + 3) if s_ > 0 else None)
                        else:
                            pv_ = (1 - par, lb) if s_ > 0 else None
                        prevs.append(pv_)
                    mk = masks[0] if prevs[0] is not None else (masks[1] if prevs[1] is not None else masks[2])
                    sps = [pss[nxt("ps", 4)] for _ in range(2)]
                    for hh in range(2):
                        opb(tr, "pe", [mk, cK], [sps[hh]], lambda e: e.matmul(
                            sps[hh].t[:], lhsT=self.ident_b[:], rhs=mk.t[:], start=True, stop=False))
                    for bb in range(2):
                        lb = lbp * 2 + bb
                        last = (bb == 1)
                        if prevs[bb] is not None:
                            pp, pl = prevs[bb]
                            for hh in range(2):
                                rows = slice(hh * 64, (hh + 1) * 64)
                                opb(tr, "pe", [ks[pp][o], qs[par][o]], [sps[hh]], lambda e: e.matmul(
                                    sps[hh].t[:, bb * 256:bb * 256 + 128], lhsT=ks[pp][o].t[rows, pl * 128:(pl + 1) * 128],
                                    rhs=qs[par][o].t[rows, lb * 128:(lb + 1) * 128], start=False, stop=False))
                        for hh in range(2):
                            rows = slice(hh * 64, (hh + 1) * 64)
                            opb(tr, "pe", [ks[par][o], qs[par][o]], [sps[hh]], lambda e: e.matmul(
                                sps[hh].t[:, bb * 256 + 128:bb * 256 + 256], lhsT=ks[par][o].t[rows, lb * 128:(lb + 1) * 128],
                                rhs=qs[par][o].t[rows, lb * 128:(lb + 1) * 128], start=False, stop=last))
                    pts = []
                    for hh in range(2):
                        pt = pT[nxt("pT", 4)]
                        opb(tr, "act", [sps[hh]], [pt], lambda e: e.activation(
                            out=pt.t[:], in_=sps[hh].t[:], func=AF.Exp, scale=0.125))
                        pts.append(pt)
                    return prevs, pts

                def pv_acc(o, lbp, prevs, pts):
                    nd = psn[nxt("pn", 2)]
                    for bb in range(2):
                        lb = lbp * 2 + bb
                        srcs = []
                        if prevs[bb] is not None:
                            pp, pl = prevs[bb]
                            srcs.append((V[pp][o], pl, bb * 256))
                        srcs.append((V[par][o], lb, bb * 256 + 128))
                        for kind in range(2):
                            for si, (vb, blk, pc) in enumerate(srcs):
                                for hh in range(2):
                                    rows = slice(hh * 64, (hh + 1) * 64)
                                    pt = pts[hh]
                                    lhs = vb.t[:, blk, rows] if kind == 0 else self.ones_b[:, 0:64]
                                    opb(tr, "pe", [vb, pt, cK], [nd], lambda e: e.matmul(
                                        nd.t[rows, kind * 256 + bb * 128:kind * 256 + (bb + 1) * 128],
                                        lhsT=lhs, rhs=pt.t[:, pc:pc + 128],
                                        start=(si == 0), stop=(si == len(srcs) - 1)))
                    for kind, acc in ((0, accN), (1, accD)):
                        if o == 0:
                            av = acc.t[:, lbp * 256:(lbp + 1) * 256].rearrange("p (b i) -> p b i", b=2)
                        elif o == 1:
                            r_, n0 = lbp // 2, (lbp % 2) * 2
                            av = acc.t[:].rearrange("p (n i r) -> p r n i", n=4, r=4)[:, r_, n0:n0 + 2, :]
                        else:
                            av = acc.t[:].rearrange("p (i r) -> p r i", r=16)[:, lbp * 2:lbp * 2 + 2, :]
                        sv = nd.t[:, kind * 256:(kind + 1) * 256].rearrange("p (b i) -> p b i", b=2)
                        if o == 2:
                            opb(tr, "dve", [nd], [acc], lambda e: e.tensor_copy(out=av, in_=sv))
                        else:
                            opb(tr, "dve", [nd], [acc], lambda e: e.tensor_tensor(out=av, in0=av, in1=sv, op=ALU.add))

                ctx = scores(*order[0])
                for oi_, (o, lbp) in enumerate(order):
                    nctx = scores(*order[oi_ + 1]) if oi_ + 1 < len(order) else None
                    pv_acc(o, lbp, *ctx)
                    ctx = nctx
                    if oi_ == 12:
                        prev_done[(c, s_)] = True
                    yield
                ab = ao[nxt("ao", 2)]
                opb(tr, "act", [], [accD], lambda e: e.activation(out=accD.t[:], in_=accD.t[:], func=AF.Ln))
                opb(tr, "act", [], [accD], lambda e: e.activation(out=accD.t[:], in_=accD.t[:], func=AF.Exp, scale=-1.0))
                opb(tr, "pool", [accN, accD], [ab], lambda e: e.tensor_tensor(
                    out=ab.t[:], in0=accN.t[:], in1=accD.t[:], op=ALU.mult))
                dmab(tr, "sp", "a_so%d" % (cnt["ao"] % 2), [ab], [],
                     self.mixT[c * 128:(c + 1) * 128, s_ * 2048:(s_ + 1) * 2048], ab.t[:])
                yield

            items = [(c, s_) for c in range(4) for s_ in range(NSEG)]
            interleave([phaseA(*items[0])])
            for i_, it_ in enumerate(items):
                gens = [phaseB(*it_)]
                if i_ + 1 < len(items):
                    gens.append(phaseA(*items[i_ + 1], gate=it_))
                interleave(gens)
            tr.barrier()

    def hgrn(self, l):
        nc, tr, S = self.nc, self.tr, self.S
        NT = S // 512
        with ExitStack() as es:
            B = lambda name, shape, dt: Buf(self.sb(es, name, shape, dt))
            P = lambda name, shape, dt=F32: Buf(self.ps(es, name, shape, dt))
            cK = Buf(None)
            cK.w = {t[0]: t for t in self.const_tok + self.lb_tok + self.setup_tok}
            wh = B("h_w", [128, NCH, 2048], BF16)
            ht = [B("h_ht%d" % i, [128, NCH, 512], BF16) for i in range(2)]
            qsl = [B("h_qs%d" % i, [128, 512], F32) for i in range(4)]
            fb = [B("h_f%d" % i, [128, 512], F32) for i in range(4)]
            ab_ = [B("h_a%d" % i, [128, 512], F32) for i in range(4)]
            bc = [B("h_b%d" % i, [128, 512], F32) for i in range(4)]
            e1 = [B("h_e1%d" % i, [128, 512], F32) for i in range(4)]
            kh = [B("h_kh%d" % i, [128, 512], BF16) for i in range(2)]
            QT = [[B("h_QT%d%d" % (h, i), [128, 512], BF16) for i in range(2)] for h in range(4)]
            KT = [[B("h_KT%d%d" % (h, i), [128, 512], BF16) for i in range(2)] for h in range(4)]
            KK = [[B("h_KK%d%d" % (h, i), [64, 8, 128], BF16) for i in range(2)] for h in range(4)]
            SG = [[B("h_SG%d%d" % (h, i), [128, 512], BF16) for i in range(2)] for h in range(4)]
            DEC = [[B("h_DC%d%d" % (h, i), [128, 8], F32) for i in range(2)] for h in range(4)]
            VV = [B("h_VV%d" % i, [64, 8, 512], BF16) for i in range(2)]
            ST = [B("h_ST%d" % h, [128, 128], F32) for h in range(4)]
            STb = [B("h_STb%d" % h, [128, 128], BF16) for h in range(4)]
            AM = [B("h_AM%d" % i, [64, 256], BF16) for i in range(2)]
            O32 = [B("h_O%d" % i, [128, 4, 512], F32) for i in range(2)]
            osq = B("h_osq", [128, 512], BF16)
            rt = B("h_rt", [128, 512], F32)
            g1 = B("h_g1", [128, 512], F32)
            GO = [B("h_GO%d" % i, [128, 512], BF16) for i in range(2)]
            tri4 = B("h_tri", [64, 256], BF16)
            rmask = B("h_rm", [128, 512], F32)
            psp = [P("h_pp%d" % i, [128, 512]) for i in range(4)]
            psA = P("h_pA", [128, 512])
            pso = [P("h_po%d" % i, [128, 512]) for i in range(1)]
            psS = [P("h_pS%d" % i, [128, 512]) for i in range(2)]
            pst = psp[0]
            pstT = psp[0].t.bitcast(BF16)
            psm = psA
            cnt = {"pp": 0, "po": 0, "am": 0, "t": 0, "go": 0}

            def nxt(k, n):
                v = cnt[k] % n
                cnt[k] += 1
                return v

            for h in range(4):
                opb(tr, "dve", [cK], [tri4], lambda e: e.tensor_copy(
                    out=tri4.t[:, h * 64:(h + 1) * 64], in_=self.cst[0:64, 640:704]))
            opb(tr, "pool", [], [rmask], lambda e: e.memset(rmask.t[:], 1.0))
            opb(tr, "pool", [], [rmask], lambda e: e.memset(
                rmask.t[:].rearrange("p (n t) -> p n t", t=64)[:, :, 0:1], 0.0))
            for h in range(4):
                opb(tr, "pool", [], [ST[h]], lambda e: e.memset(ST[h].t[:], 0.0))
                opb(tr, "pool", [], [STb[h]], lambda e: e.memset(STb[h].t[:], 0.0))
            for kc in range(NCH):
                dmab(tr, "pool", "h_w%d" % kc, [], [wh], wh.t[:, kc, :],
                     self.w_mix_in[l, kc * 128:(kc + 1) * 128, 1536:3584])
            lbc = lambda h: self.lbv[:, l * 4 + h:l * 4 + h + 1]
            omc = lambda h: self.oml[:, l * 4 + h:l * 4 + h + 1]
            nwc = lambda h: self.vecs[:, V_NW + l * 4 + h:V_NW + l * 4 + h + 1]

            def proj(hb, col0, dst):
                for kc in range(NCH):
                    opb(tr, "pe", [wh, ht[hb]], [dst], lambda e: e.matmul(
                        dst.t[:], lhsT=wh.t[:, kc, col0:col0 + 128], rhs=ht[hb].t[:, kc, :],
                        start=(kc == 0), stop=(kc == NCH - 1)))

            def hA(j):
                t0 = j * 512
                tb = j % 2
                dmab(tr, "sp", "h_lh%d" % tb, [], [ht[tb]], ht[tb].t[:],
                     self.hT[:, t0:t0 + 512].rearrange("(k p) t -> p k t", p=128))
                for h in range(4):
                    pq = psp[nxt("pp", 4)]
                    proj(tb, h * 128, pq)
                    opb(tr, "act", [pq], [qsl[h]], lambda e: e.activation(out=qsl[h].t[:], in_=pq.t[:], func=AF.Silu))
                    pg = psp[nxt("pp", 4)]
                    proj(tb, 1536 + h * 128, pg)
                    opb(tr, "act", [pg], [SG[h][tb]], lambda e: e.activation(out=SG[h][tb].t[:], in_=pg.t[:], func=AF.Silu))
                    yield
                for h in range(4):
                    pf = psp[nxt("pp", 4)]
                    proj(tb, 512 + h * 128, pf)
                    opb(tr, "act", [pf], [fb[h]], lambda e: e.activation(out=fb[h].t[:], in_=pf.t[:], func=AF.Sigmoid))
                    if h % 2 == 1:
                        yield
                for h in range(4):
                    opb(tr, "act", [cK], [fb[h]], lambda e: e.activation(
                        out=fb[h].t[:], in_=fb[h].t[:], func=AF.Identity, bias=lbc(h), scale=omc(h)))
                for h in range(4):
                    opb(tr, "act", [fb[h]], [ab_[h]], lambda e: e.activation(out=ab_[h].t[:], in_=fb[h].t[:], func=AF.Ln))
                    opb(tr, "pool", [], [fb[h]], lambda e: e.tensor_scalar(
                        out=fb[h].t[:], in0=fb[h].t[:], scalar1=-1.0, scalar2=1.0, op0=ALU.mult, op1=ALU.add))
                yield
                for h in range(4):
                    opb(tr, "dve", [rmask, ab_[h]], [bc[h]], lambda e: e.tensor_tensor_scan(
                        out=bc[h].t[:], data0=rmask.t[:], data1=ab_[h].t[:], initial=0.0, op0=ALU.mult, op1=ALU.add))
                    if h % 2 == 1:
                        yield
                for h in range(4):
                    opb(tr, "act", [bc[h]], [e1[h]], lambda e: e.activation(out=e1[h].t[:], in_=bc[h].t[:], func=AF.Exp))
                    opb(tr, "act", [], [bc[h]], lambda e: e.activation(out=bc[h].t[:], in_=bc[h].t[:], func=AF.Exp, scale=-1.0))
                yield
                for h in range(4):
                    opb(tr, "pool", [qsl[h], e1[h]], [QT[h][tb]], lambda e: e.tensor_tensor(
                        out=QT[h][tb].t[:], in0=qsl[h].t[:], in1=e1[h].t[:], op=ALU.mult))
                    opb(tr, "pool", [fb[h], bc[h]], [KT[h][tb]], lambda e: e.tensor_tensor(
                        out=KT[h][tb].t[:], in0=fb[h].t[:], in1=bc[h].t[:], op=ALU.mult))
                    e1v = e1[h].t[:].rearrange("p (n t) -> p n t", t=64)
                    opb(tr, "dve", [e1[h]], [DEC[h][tb]], lambda e: e.tensor_copy(
                        out=DEC[h][tb].t[:].rearrange("p (n o) -> p n o", o=1), in_=e1v[:, :, 63:64]))
                    if h % 2 == 1:
                        yield
                for h in range(4):
                    i2 = h % 2
                    e1v = e1[h].t[:].rearrange("p (n t) -> p n t", t=64)
                    opb(tr, "pool", [KT[h][tb], e1[h]], [kh[i2]], lambda e: e.tensor_tensor(
                        out=kh[i2].t[:].rearrange("p (n t) -> p n t", t=64),
                        in0=KT[h][tb].t[:].rearrange("p (n t) -> p n t", t=64),
                        in1=e1v[:, :, 63:64].to_broadcast([128, 8, 64]), op=ALU.mult))
                    for ch in range(8):
                        opb(tr, "pe", [kh[i2], cK], [pst], lambda e: e.transpose(
                            pstT[0:64, ch * 128:(ch + 1) * 128], kh[i2].t[:, ch * 64:(ch + 1) * 64], self.ident_b[:]))
                    opb(tr, "act", [pst], [KK[h][tb]], lambda e: e.copy(
                        out=KK[h][tb].t[:], in_=pstT[0:64, :].rearrange("p (k n) -> p k n", k=8)))
                    yield
                for ch in range(8):
                    pv = psp[nxt("pp", 4)]
                    for kc in range(NCH):
                        opb(tr, "pe", [wh, ht[tb]], [pv], lambda e: e.matmul(
                            pv.t[0:64, :], lhsT=ht[tb].t[:, kc, ch * 64:(ch + 1) * 64], rhs=wh.t[:, kc, 1024:1536],
                            start=(kc == 0), stop=(kc == NCH - 1)))
                    if ch % 2 == 0:
                        opb(tr, "act", [pv], [VV[tb]], lambda e: e.copy(out=VV[tb].t[:, ch, :], in_=pv.t[0:64, :]))
                    else:
                        opb(tr, "act", [pv], [VV[tb]], lambda e: e.copy(out=VV[tb].t[:, ch, :], in_=pv.t[0:64, :]))
                        yield

            def hB(j):
                t0 = j * 512
                tb = j % 2
                ob = O32[tb]

                def mmA(ch):
                    cols = slice(ch * 64, (ch + 1) * 64)
                    for h in range(4):
                        opb(tr, "pe", [KT[h][tb], QT[h][tb]], [psA], lambda e: e.matmul(
                            psA.t[0:64, h * 64:(h + 1) * 64], lhsT=KT[h][tb].t[:, cols], rhs=QT[h][tb].t[:, cols],
                            start=True, stop=True))
                    am = AM[nxt("am", 2)]
                    opb(tr, "dve", [psA, tri4], [am], lambda e: e.tensor_tensor(
                        out=am.t[:], in0=psA.t[0:64, 0:256], in1=tri4.t[:], op=ALU.mult))
                    return am

                def mmS(ch):
                    pS = psS[ch % 2]
                    for h in range(4):
                        opb(tr, "pe", [KK[h][tb], VV[tb]], [pS], lambda e: e.matmul(
                            pS.t[:, h * 128:(h + 1) * 128], lhsT=KK[h][tb].t[:, ch, :], rhs=VV[tb].t[:, ch, h * 128:(h + 1) * 128],
                            start=True, stop=True))

                def mmO(ch, am):
                    cols = slice(ch * 64, (ch + 1) * 64)
                    po = pso[nxt("po", 1)]
                    for h in range(4):
                        opb(tr, "pe", [VV[tb], am], [po], lambda e: e.matmul(
                            po.t[:, h * 64:(h + 1) * 64], lhsT=VV[tb].t[:, ch, h * 128:(h + 1) * 128], rhs=am.t[:, h * 64:(h + 1) * 64],
                            start=True, stop=False))
                        opb(tr, "pe", [STb[h], QT[h][tb]], [po], lambda e: e.matmul(
                            po.t[:, h * 64:(h + 1) * 64], lhsT=STb[h].t[:], rhs=QT[h][tb].t[:, cols],
                            start=False, stop=True))
                    opb(tr, "dve", [po], [ob], lambda e: e.tensor_copy(
                        out=ob.t[:, :, cols], in_=po.t[:, 0:256].rearrange("p (h t) -> p h t", h=4)))

                def upd(ch):
                    pS = psS[ch % 2]
                    for h in range(4):
                        opb(tr, "dve", [pS, DEC[h][tb]], [ST[h]], lambda e: e.scalar_tensor_tensor(
                            out=ST[h].t[:], in0=ST[h].t[:], scalar=DEC[h][tb].t[:, ch:ch + 1], in1=pS.t[:, h * 128:(h + 1) * 128],
                            op0=ALU.mult, op1=ALU.add))
                        opb(tr, "dve", [ST[h]], [STb[h]], lambda e: e.tensor_copy(out=STb[h].t[:], in_=ST[h].t[:]))

                am = mmA(0)
                mmS(0)
                for ch in range(8):
                    am_n = mmA(ch + 1) if ch + 1 < 8 else None
                    mmO(ch, am)
                    upd(ch)
                    if ch + 1 < 8:
                        mmS(ch + 1)
                    am = am_n
                    yield
                for h in range(4):
                    opb(tr, "act", [ob], [osq], lambda e: e.activation(out=osq.t[:], in_=ob.t[:, h, :], func=AF.Square))
                    opb(tr, "pe", [osq, cK], [psm], lambda e: e.matmul(
                        psm.t[:], lhsT=self.ones_b[:], rhs=osq.t[:], start=True, stop=True))
                    opb(tr, "act", [psm, cK], [rt], lambda e: e.activation(
                        out=rt.t[:], in_=psm.t[:], func=AF.Ln, bias=self.epsc(RMS_EPS), scale=1.0 / 128))
                    opb(tr, "act", [], [rt], lambda e: e.activation(out=rt.t[:], in_=rt.t[:], func=AF.Exp, scale=-0.5))
                    opb(tr, "pool", [ob, rt], [g1], lambda e: e.tensor_tensor(
                        out=g1.t[:], in0=ob.t[:, h, :], in1=rt.t[:], op=ALU.mult))
                    go = GO[nxt("go", 2)]
                    opb(tr, "dve", [g1, SG[h][tb], cK], [go], lambda e: e.scalar_tensor_tensor(
                        out=go.t[:], in0=g1.t[:], scalar=nwc(h), in1=SG[h][tb].t[:], op0=ALU.mult, op1=ALU.mult))
                    dmab(tr, "sp", "h_so%d" % (cnt["go"] % 2), [go], [],
                         self.mixT[512 + h * 128:512 + (h + 1) * 128, t0:t0 + 512], go.t[:])
                    yield

            interleave([hA(0)])
            for j in range(NT):
                gens = [hB(j)]
                if j + 1 < NT:
                    gens.append(hA(j + 1))
                interleave(gens)
            tr.barrier()

    def ffn(self, l, sub, src, dst, hdst=None):
        nc, tr, S = self.nc, self.tr, self.S
        is_ffn = sub != 1
        T = 256 if is_ffn else 512
        which = 0 if sub == 0 else 1
        base = l * 72 + sub * 24
        sh = lambda c: self.msc[:, base + c:base + c + 1]
        sc = lambda c: self.msc[:, base + 8 + c:base + 9 + c]
        gf = lambda c: self.msc[:, base + 16 + c:base + 17 + c]
        nb_ = l * 72 + (sub + 1) * 24
        sh2 = lambda c: self.msc[:, nb_ + c:nb_ + c + 1]
        sc2 = lambda c: self.msc[:, nb_ + 8 + c:nb_ + 9 + c]
        gcol = lambda c: self.vecs[:, V_LNG + (l * 3 + sub) * 8 + c:V_LNG + (l * 3 + sub) * 8 + c + 1]
        bcol = lambda c: self.vecs[:, V_LNB + (l * 3 + sub) * 8 + c:V_LNB + (l * 3 + sub) * 8 + c + 1]
        eps = LN_EPS / (ALPHA * ALPHA)
        nh = NHC if is_ffn else NCH
        with ExitStack() as es:
            if is_ffn:
                w_in = self.sb(es, "w_in", [128, NCH, 2 * DFF], BF16)
                ht = [self.sb(es, "f_h%d" % i, [128, NCH, T], BF16) for i in range(1)]
                sg = [self.sb(es, "f_sg%d" % i, [128, T], F32) for i in range(2)]
                psg = [self.ps(es, "f_pg%d" % i, [128, 512], F32) for i in range(4)]
            w_out = self.sb(es, "w_out", [128, nh, D], BF16)
            nxb = 2 if is_ffn else 3
            xt = [self.sb(es, "f_x%d" % i, [128, NCH, T], F32) for i in range(nxb)]
            hid = [self.sb(es, "f_hid%d" % i, [128, nh, T], BF16) for i in range(1 if is_ffn else 2)]
            nzb = 1 if is_ffn else 2
            zbs = [self.sb(es, "f_zb%d" % i, [128, NCH, T], BF16) for i in range(nzb)]
            zqs = [self.sb(es, "f_zq%d" % i, [128, NCH, T], BF16) for i in range(nzb)]
            st = self.sb(es, "f_st", [128, 4, T], F32)
            if hdst is not None:
                hb = [self.sb(es, "f_hb%d" % i, [128, NCH, T], BF16) for i in range(1)]
            psy = [self.ps(es, "f_py%d" % i, [128, 512], F32) for i in range(3)]
            pss = self.ps(es, "f_pss", [128, 2 * T], F32)
            wtok = []
            wtok2 = []
            if is_ffn:
                for m0 in range(0, NHC, 2):
                    tg = []
                    for half in range(2):
                        c0 = half * DFF + m0 * 128
                        tg.append(tr.dma("pool", None, "w_a%d_%d" % (m0 // 2, half), w_in[:, :, c0:c0 + 256],
                                         self.w_ffn_in[which][l, :, c0:c0 + 256].rearrange("(c p) n -> p c n", p=128)))
                    wtok.append(tg)
                for m0 in range(0, NHC, 2):
                    wtok2.append(tr.dma("pool", None, "w_b%d" % (m0 // 2), w_out[:, m0:m0 + 2, :],
                                        self.w_ffn_out[which][l, m0 * 128:(m0 + 2) * 128, :].rearrange("(m p) d -> p m d", p=128)))
            else:
                for m0 in range(0, NCH, 2):
                    wtok2.append(tr.dma("pool", None, "w_b%d" % (m0 // 2), w_out[:, m0:m0 + 2, :],
                                        self.w_mix_out[l, m0 * 128:(m0 + 2) * 128, :].rearrange("(m p) d -> p m d", p=128)))
            x_free = [None] * 3
            h_free = [None]
            hid_free = [None, None]
            hb_free = [None, None]
            sg_free = [None, None]
            psg_free = [None] * 4
            psy_free = [None] * 3
            pss_free = None
            zb_free = None
            st_free = None
            ng = 0
            ny = 0
            nsg = 0
            nhb = 1 if is_ffn else 2

            def load_h(it):
                b = it % nxb
                t0 = it * T
                tl = tr.dma("sp", x_free[b], "f_l%d" % b, xt[b][:],
                            src[:, t0:t0 + T].rearrange("(c p) t -> p c t", p=128))
                th = []
                if is_ffn:
                    for c in range(NCH):
                        th.append(tr.op("act", [tl, h_free[0], self.setup_tok], lambda e: e.activation(
                            out=ht[0][:, c, :], in_=xt[b][:, c, :], func=AF.Identity, bias=sh(c), scale=sc(c))))
                else:
                    hb_ = it % nhb
                    th = tr.dma("sp", hid_free[hb_], "f_lm%d" % hb_, hid[hb_][:],
                                self.mixT[:, t0:t0 + T].rearrange("(c p) t -> p c t", p=128))
                return tl, th

            Fv = {"ng": 0, "ny": 0, "nsg": 0, "pss": None, "zb": [None, None], "st": None}
            NT_ = S // T

            def mm1_gen(th, hbi, thid):
                tm = None
                for m in range(NHC):
                    pb = Fv["ng"] % 4
                    Fv["ng"] += 1
                    for half in range(2):
                        col0 = half * DFF + m * 128
                        for c in range(NCH):
                            tm = tr.op("pe", [th[c], wtok[m // 2], psg_free[pb]], lambda e: e.matmul(
                                psg[pb][:, half * T:(half + 1) * T], lhsT=w_in[:, c, col0:col0 + 128], rhs=ht[0][:, c, :],
                                start=(c == 0), stop=(c == NCH - 1)))
                    sb_ = Fv["nsg"] % 2
                    Fv["nsg"] += 1
                    ts = tr.op("act", [tm, sg_free[sb_]], lambda e: e.activation(
                        out=sg[sb_][:], in_=psg[pb][:, 0:T], func=AF.Silu))
                    tu = tr.op("dve", [ts, tm, hid_free[hbi]], lambda e: e.tensor_tensor(
                        out=hid[hbi][:, m, :], in0=sg[sb_][:], in1=psg[pb][:, T:2 * T], op=ALU.mult))
                    psg_free[pb] = tu
                    sg_free[sb_] = tu
                    thid.append(tu)
                    h_free[0] = tm
                    yield

            def tail_gen(it, tz, tzb):
                b = it % nxb
                t0 = it * T
                zb, zq = zbs[it % nzb], zqs[it % nzb]
                yield
                yield
                tm = None
                for half in range(2):
                    srcb = zb if half == 0 else zq
                    for dc in range(NCH):
                        tm = tr.op("pe", [tzb[dc], Fv["pss"], self.const_tok], lambda e: e.matmul(
                            pss[:, half * T:(half + 1) * T], lhsT=self.ones_b[:], rhs=srcb[:, dc, :],
                            start=(dc == 0), stop=(dc == NCH - 1)))
                Fv["zb"][it % nzb] = tm
                ta = tr.op("dve", [tm, Fv["st"]], lambda e: e.tensor_scalar(
                    out=st[:, 0:2, :], in0=pss[:].rearrange("p (a t) -> p a t", a=2), scalar1=1.0 / D, scalar2=None, op0=ALU.mult))
                Fv["pss"] = ta
                tb = tr.op("dve", ta, lambda e: e.tensor_tensor(out=st[:, 2, :], in0=st[:, 0, :], in1=st[:, 0, :], op=ALU.mult))
                tc = tr.op("dve", tb, lambda e: e.tensor_tensor(out=st[:, 3, :], in0=st[:, 1, :], in1=st[:, 2, :], op=ALU.subtract))
                tc2 = tr.op("act", tc, lambda e: e.activation(
                    out=st[:, 3, :], in_=st[:, 3, :], func=AF.Sqrt, bias=self.epsc(eps), scale=1.0))
                td = tr.op("dve", tc2, lambda e: e.reciprocal(out=st[:, 2, :], in_=st[:, 3, :]))
                yield
                touts = []
                thb = []
                for dc in range(NCH):
                    eng = "dve" if dc % 2 == 0 else "pool"
                    t1 = tr.op(eng, [td, tz[dc], tzb[dc]], lambda e: e.tensor_tensor(
                        out=xt[b][:, dc, :], in0=xt[b][:, dc, :], in1=st[:, 0, :], op=ALU.subtract))
                    t2 = tr.op(eng, t1, lambda e: e.tensor_tensor(
                        out=xt[b][:, dc, :], in0=xt[b][:, dc, :], in1=st[:, 2, :], op=ALU.mult))
                    t3 = tr.op("act", t2, lambda e: e.activation(
                        out=xt[b][:, dc, :], in_=xt[b][:, dc, :], func=AF.Identity, bias=bcol(dc), scale=gcol(dc)))
                    touts.append(t3)
                    if hdst is not None:
                        thb.append(tr.op("act", [t3, hb_free[0]], lambda e: e.activation(
                            out=hb[0][:, dc, :], in_=xt[b][:, dc, :], func=AF.Identity, bias=sh2(dc), scale=sc2(dc))))
                    yield
                Fv["st"] = touts
                tst = tr.dma("sp", touts, "f_s%d" % b,
                             dst[:, t0:t0 + T].rearrange("(c p) t -> p c t", p=128), xt[b][:])
                x_free[b] = [tst] + thb
                if hdst is not None:
                    hb_free[0] = tr.dma("sp", thb, "f_sh",
                                        hdst[:, t0:t0 + T].rearrange("(c p) t -> p c t", p=128), hb[0][:])
                yield

            pre = load_h(0)
            pending = None
            for it in range(NT_):
                b = it % nxb
                hbi = it % nhb
                tl, th = pre
                gens = []
                if is_ffn:
                    thid = []
                    gens.append(mm1_gen(th, hbi, thid))
                else:
                    thid = [th] * nh
                if pending is not None:
                    gens.append(pending)
                if is_ffn:
                    interleave(gens)
                    gens = []
                if it + 1 < NT_:
                    pre = load_h(it + 1)
                tz = []
                tzb = []

                def mm2_gen(b, hbi, tl, thid, tz, tzb, it=it):
                    tm = None
                    zb, zq = zbs[it % nzb], zqs[it % nzb]
                    zfree = Fv["zb"][it % nzb]
                    for dc in range(NCH):
                        pb = Fv["ny"] % 3
                        Fv["ny"] += 1
                        for m in range(nh):
                            tm = tr.op("pe", [thid[m], wtok2[m // 2], psy_free[pb]], lambda e: e.matmul(
                                psy[pb][:, 0:T], lhsT=w_out[:, m, dc * 128:(dc + 1) * 128], rhs=hid[hbi][:, m, :],
                                start=(m == 0), stop=(m == nh - 1)))
                        t1 = tr.op("dve", [tm, tl, self.setup_tok], lambda e: e.scalar_tensor_tensor(
                            out=xt[b][:, dc, :], in0=psy[pb][:, 0:T], scalar=gf(dc), in1=xt[b][:, dc, :],
                            op0=ALU.mult, op1=ALU.add))
                        psy_free[pb] = t1
                        tz.append(t1)
                        t2 = tr.op("act", [t1, zfree], lambda e: e.copy(out=zb[:, dc, :], in_=xt[b][:, dc, :]))
                        t3 = tr.op("act", [t1, zfree], lambda e: e.activation(
                            out=zq[:, dc, :], in_=xt[b][:, dc, :], func=AF.Square))
                        tzb.append([t2, t3])
                        hid_free[hbi] = tm
                        yield

                interleave([mm2_gen(b, hbi, tl, thid, tz, tzb)] + gens)
                pending = tail_gen(it, tz, tzb)
            interleave([pending])
            tr.barrier()


NEG = -30000.0


def host_consts():
    c = np.zeros((128, 1024), np.float32)
    c[:, 0:128] = np.eye(128, dtype=np.float32)
    p = np.arange(128)
    ch = p % 64
    invf = np.where(ch < 16, np.float32(ROPE_THETA) ** (-(ch % 8).astype(np.float32) * np.float32(2.0 / 16)), 0.0)
    c[:, 128] = invf.astype(np.float32)
    c[:, 129] = math.pi / 2
    j = p[:, None]
    i = p[None, :]
    c[:, 256:384] = np.where(j >= i, 0.0, NEG)
    c[:, 384:512] = np.where(j <= i, 0.0, NEG)
    c[:, 512:640] = NEG
    s_ = (p % 64)[:, None]
    t_ = np.arange(64)[None, :]
    c[:, 640:704] = (s_ <= t_).astype(np.float32)
    return c


def pack_vecs(b, c, ln_g, ln_b, ada_b, hgrn_norm_w, hgrn_lb_logits):
    v = np.zeros((NVEC, 128), np.float32)
    v[V_ADAB:V_ADAB + 144] = ada_b.reshape(144, 128)
    v[V_LNG:V_LNG + 48] = ln_g.reshape(48, 128)
    v[V_LNB:V_LNB + 48] = ln_b.reshape(48, 128)
    v[V_C:V_C + 8] = c[b].reshape(8, 128)
    v[V_NW:V_NW + 8] = hgrn_norm_w.reshape(8, 128)
    v[V_LB:V_LB + 8] = hgrn_lb_logits.reshape(8, 128)
    return v


def make_in_maps(inputs, ncores, S):
    g = lambda k: np.ascontiguousarray(np.asarray(inputs[k]))
    consts = host_consts()
    maps = []
    for b in range(ncores):
        m = {
            "x": np.ascontiguousarray(g("x")[b, :S]),
            "pos": np.ascontiguousarray(g("positions")[b, :S].reshape(1, S).astype(np.int32)),
            "vecs": pack_vecs(b, g("c"), g("ln_g"), g("ln_b"), g("ada_b"), g("hgrn_norm_w"), g("hgrn_lb_logits")),
            "consts": consts,
            "ada_w": g("ada_w"),
            "ffn1_w_in": g("ffn1_w_in"), "ffn2_w_in": g("ffn2_w_in"),
            "ffn1_w_out": g("ffn1_w_out"), "ffn2_w_out": g("ffn2_w_out"),
            "mix_w_in": g("mix_w_in"), "mix_w_out": g("mix_w_out"),
        }
        maps.append(m)
    return maps


def kernel(**inputs):
    S = 8192
    nc = Builder(S).build()
    maps = make_in_maps(inputs, 8, S)
    res = run_bass_kernel_spmd(nc, maps, core_ids=list(range(8)))
    return np.stack([r["out"] for r in res.results], axis=0)
```
